# Optimizing a Trainium2 kernel written in Bass

```python
import jax, jax.numpy as jnp
from jax import lax
import numpy as np

D_MODEL = 1024
BATCH = 8
SEQ = 2048
DEPTH = 4
DEC_BATCH = 128
DEC_SEQ = 8
PAST_LEN = 16384
PAGE_SIZE = 128

D_MIX = D_MODEL
A_HEADS = 4
A_WIDTH = D_MIX // 4
A_HEAD_DIM = A_WIDTH // A_HEADS
CHUNK = 128
B_WIDTH = D_MIX // 4
CONV_B = 3
C_WIDTH = D_MIX // 2
SSM_HEAD_DIM = 64
SSM_HEADS = C_WIDTH // SSM_HEAD_DIM
SSM_GROUPS = 2
SSM_HPG = SSM_HEADS // SSM_GROUPS
D_STATE = 128
CONV_C = 4
SSD_CHUNK = 128
SSM_CONV_DIM = C_WIDTH + 2 * SSM_GROUPS * D_STATE
D_FF = 4 * D_MODEL
EPS = 1e-5
D_IN = 2 * A_WIDTH + 3 * B_WIDTH + C_WIDTH + SSM_CONV_DIM + SSM_HEADS

kernel_name = "hybrid_chunkmlp_shortconv_ssd_decoder_step"


def rmsnorm(x, g):
    xf = x.astype(jnp.float32)
    y = xf * lax.rsqrt(jnp.mean(xf * xf, axis=-1, keepdims=True) + EPS)
    return (y * g.astype(jnp.float32)).astype(x.dtype)


def causal_dwconv(inp, prev, w, b=None):
    K = w.shape[0]
    T = inp.shape[1]
    full = jnp.concatenate([prev.astype(inp.dtype), inp], axis=1)
    out = full[:, 0:T] * w[0]
    for k in range(1, K):
        out = out + full[:, k:k + T] * w[k]
    if b is not None:
        out = out + b
    return out, full[:, T:]


def chunk_mlp(u, v, w_s, b_s):
    bsz, T, _ = u.shape
    n_c = -(-T // CHUNK)
    pad = n_c * CHUNK - T
    vp = jnp.pad(v, ((0, 0), (0, pad), (0, 0))).reshape(bsz, n_c, CHUNK, A_HEADS, A_HEAD_DIM)
    mask = jnp.tril(jnp.ones((CHUNK, CHUNK), dtype=bool))
    wm = jnp.where(mask[None], w_s, jnp.zeros((), w_s.dtype))
    s = jnp.einsum('hts,bcshe->bcthe', wm, vp) + jnp.transpose(b_s)[None, None, :, :, None]
    s = s.reshape(bsz, n_c * CHUNK, A_WIDTH)[:, :T]
    return u * s


def ssd(x, dt, A, Bm, Cm, h0):
    bsz, T = x.shape[0], x.shape[1]
    L = min(SSD_CHUNK, T)
    n_c = -(-T // L)
    pad = n_c * L - T
    padt = lambda a: jnp.pad(a, [(0, 0), (0, pad)] + [(0, 0)] * (a.ndim - 2))
    xs = padt(x).reshape(bsz, n_c, L, SSM_GROUPS, SSM_HPG, SSM_HEAD_DIM)
    dts = padt(dt).reshape(bsz, n_c, L, SSM_GROUPS, SSM_HPG)
    Bs = padt(Bm).reshape(bsz, n_c, L, SSM_GROUPS, D_STATE)
    Cs = padt(Cm).reshape(bsz, n_c, L, SSM_GROUPS, D_STATE)
    a = dts * A.reshape(SSM_GROUPS, SSM_HPG)
    cum = jnp.cumsum(a, axis=2)
    tril = jnp.tril(jnp.ones((L, L), dtype=bool))[None, None, :, :, None, None]
    seg = cum[:, :, :, None] - cum[:, :, None, :]
    decay = jnp.exp(jnp.where(tril, seg, -jnp.inf))
    cb = jnp.einsum('bctgn,bcsgn->bctsg', Cs, Bs)
    wts = cb[..., None] * decay * dts[:, :, None]
    y_diag = jnp.einsum('bctsgk,bcsgkp->bctgkp', wts, xs)
    decay_end = jnp.exp(cum[:, :, -1:] - cum)
    states = jnp.einsum('bcsgn,bcsgk,bcsgkp->bcgkpn', Bs, decay_end * dts, xs)
    chunk_decay = jnp.exp(cum[:, :, -1])

    def step(h, inp):
        st, dec = inp
        return dec[..., None, None] * h + st, h

    h_init = h0.reshape(bsz, SSM_GROUPS, SSM_HPG, SSM_HEAD_DIM, D_STATE)
    h_final, h_starts = lax.scan(step, h_init,
                                 (jnp.moveaxis(states, 1, 0), jnp.moveaxis(chunk_decay, 1, 0)))
    h_starts = jnp.moveaxis(h_starts, 0, 1)
    y_off = jnp.einsum('bctgn,bcgkpn->bctgkp', Cs, h_starts) * jnp.exp(cum)[..., None]
    y = (y_diag + y_off).reshape(bsz, n_c * L, SSM_HEADS, SSM_HEAD_DIM)[:, :T]
    return y, h_final.reshape(bsz, SSM_HEADS, SSM_HEAD_DIM, D_STATE)


def layer(x, conv_prev, sconv_prev, ssm_prev, g1, w_in, w_s, b_s, conv_w, sconv_w, sconv_b,
          dt_bias, a_log, d_skip, ssm_norm, w_out, g2, w_ff1, w_ff2):
    bsz, T, _ = x.shape
    h = rmsnorm(x, g1)
    proj = h @ w_in
    cuts = np.cumsum([A_WIDTH, A_WIDTH, B_WIDTH, B_WIDTH, B_WIDTH, C_WIDTH, SSM_CONV_DIM])
    u, v, bgate, cgate, hb, z, xbc, dt_raw = jnp.split(proj, cuts, axis=-1)

    u = jax.nn.gelu(u, approximate=False)
    v = jax.nn.gelu(v, approximate=False)
    ya = chunk_mlp(u, v, w_s, b_s)
    v_rows = v[:, ((T - 1) // CHUNK) * CHUNK:]

    conv_out, conv_new = causal_dwconv(cgate * hb, conv_prev, conv_w)
    yb = bgate * conv_out

    xbc_c, sconv_new = causal_dwconv(xbc, sconv_prev, sconv_w, sconv_b)
    xbc_c = jax.nn.silu(xbc_c)
    xs, Bm, Cm = jnp.split(xbc_c, [C_WIDTH, C_WIDTH + SSM_GROUPS * D_STATE], axis=-1)
    xs = xs.reshape(bsz, T, SSM_HEADS, SSM_HEAD_DIM).astype(jnp.float32)
    Bm = Bm.reshape(bsz, T, SSM_GROUPS, D_STATE).astype(jnp.float32)
    Cm = Cm.reshape(bsz, T, SSM_GROUPS, D_STATE).astype(jnp.float32)
    dt = jax.nn.softplus(dt_raw.astype(jnp.float32) + dt_bias.astype(jnp.float32))
    A = -jnp.exp(a_log.astype(jnp.float32))
    y, ssm_new = ssd(xs, dt, A, Bm, Cm, ssm_prev.astype(jnp.float32))
    y = y + d_skip.astype(jnp.float32)[:, None] * xs
    y = y.reshape(bsz, T, C_WIDTH).astype(x.dtype)
    yc = rmsnorm(y * jax.nn.silu(z), ssm_norm)

    x = x + jnp.concatenate([ya, yb, yc], axis=-1) @ w_out
    f = jnp.square(jax.nn.relu(rmsnorm(x, g2) @ w_ff1))
    x = x + f @ w_ff2
    return x, v_rows, conv_new, sconv_new, ssm_new.astype(x.dtype)


def trunk(x, conv0, sconv0, ssm0, norm1, w_in, w_s, b_s, conv_w, ssm_conv_w, ssm_conv_b,
          dt_bias, a_log, d_skip, ssm_norm, w_out, norm2, w_ff1, w_ff2, final_norm):
    vs, cs, scs, ss = [], [], [], []
    for l in range(DEPTH):
        x, v_rows, c_new, sc_new, s_new = layer(
            x, conv0[l], sconv0[l], ssm0[l], norm1[l], w_in[l], w_s[l], b_s[l], conv_w[l],
            ssm_conv_w[l], ssm_conv_b[l], dt_bias[l], a_log[l], d_skip[l], ssm_norm[l],
            w_out[l], norm2[l], w_ff1[l], w_ff2[l])
        vs.append(v_rows); cs.append(c_new); scs.append(sc_new); ss.append(s_new)
    y = rmsnorm(x, final_norm)
    return y, jnp.stack(vs), jnp.stack(cs), jnp.stack(scs), jnp.stack(ss)


def setup_inputs(seed: int = 0) -> dict:
    key = jax.random.key(seed)
    ks = jax.random.split(key, 24)
    nrm = lambda k, shape, s: jax.random.normal(k, shape, jnp.float32) * s
    dt0 = jnp.exp(jax.random.uniform(ks[10], (DEPTH, SSM_HEADS), jnp.float32)
                  * (np.log(0.1) - np.log(0.001)) + np.log(0.001))
    return {
        "x_prompt": nrm(ks[0], (BATCH, SEQ, D_MODEL), 1.0),
        "x_sample": nrm(ks[1], (DEC_BATCH, DEC_SEQ, D_MODEL), 1.0),
        "state_conv": nrm(ks[2], (DEPTH, DEC_BATCH, CONV_B - 1, B_WIDTH), 0.5),
        "state_ssm_conv": nrm(ks[3], (DEPTH, DEC_BATCH, CONV_C - 1, SSM_CONV_DIM), 1.0),
        "state_ssm": nrm(ks[4], (DEPTH, DEC_BATCH, SSM_HEADS, SSM_HEAD_DIM, D_STATE), 0.1),
        "norm1": 1.0 + nrm(ks[5], (DEPTH, D_MODEL), 0.02),
        "w_in": nrm(ks[6], (DEPTH, D_MODEL, D_IN), D_MODEL ** -0.5),
        "w_s": nrm(ks[7], (DEPTH, A_HEADS, CHUNK, CHUNK), CHUNK ** -0.5),
        "b_s": nrm(ks[8], (DEPTH, A_HEADS, CHUNK), 0.1),
        "conv_w": nrm(ks[9], (DEPTH, CONV_B, B_WIDTH), CONV_B ** -0.5),
        "ssm_conv_w": nrm(ks[11], (DEPTH, CONV_C, SSM_CONV_DIM), CONV_C ** -0.5),
        "ssm_conv_b": nrm(ks[12], (DEPTH, SSM_CONV_DIM), 0.02),
        "dt_bias": dt0 + jnp.log(-jnp.expm1(-dt0)),
        "a_log": jnp.log(jax.random.uniform(ks[13], (DEPTH, SSM_HEADS), jnp.float32, 1.0, 16.0)),
        "d_skip": 1.0 + nrm(ks[14], (DEPTH, SSM_HEADS), 0.02),
        "ssm_norm": 1.0 + nrm(ks[15], (DEPTH, C_WIDTH), 0.02),
        "w_out": nrm(ks[16], (DEPTH, D_MIX, D_MODEL), D_MIX ** -0.5),
        "norm2": 1.0 + nrm(ks[17], (DEPTH, D_MODEL), 0.02),
        "w_ff1": nrm(ks[18], (DEPTH, D_MODEL, D_FF), D_MODEL ** -0.5),
        "w_ff2": nrm(ks[19], (DEPTH, D_FF, D_MODEL), D_FF ** -0.5),
        "final_norm": 1.0 + nrm(ks[20], (D_MODEL,), 0.02),
    }


def reference(x_prompt, x_sample, state_conv, state_ssm_conv, state_ssm, norm1, w_in, w_s, b_s,
              conv_w, ssm_conv_w, ssm_conv_b, dt_bias, a_log, d_skip, ssm_norm, w_out, norm2,
              w_ff1, w_ff2, final_norm):
    bp = x_prompt.shape[0]
    dtp = x_prompt.dtype
    conv0 = jnp.zeros((DEPTH, bp, CONV_B - 1, B_WIDTH), dtp)
    sconv0 = jnp.zeros((DEPTH, bp, CONV_C - 1, SSM_CONV_DIM), dtp)
    ssm0 = jnp.zeros((DEPTH, bp, SSM_HEADS, SSM_HEAD_DIM, D_STATE), dtp)
    y_prompt, chunk_v_prompt, conv_prompt, ssm_conv_prompt, ssm_prompt = trunk(
        x_prompt, conv0, sconv0, ssm0, norm1, w_in, w_s, b_s, conv_w, ssm_conv_w, ssm_conv_b,
        dt_bias, a_log, d_skip, ssm_norm, w_out, norm2, w_ff1, w_ff2, final_norm)
    y_sample, chunk_v_sample, conv_sample, ssm_conv_sample, ssm_sample = trunk(
        x_sample, state_conv, state_ssm_conv, state_ssm, norm1, w_in, w_s, b_s, conv_w,
        ssm_conv_w, ssm_conv_b, dt_bias, a_log, d_skip, ssm_norm, w_out, norm2, w_ff1, w_ff2,
        final_norm)
    return (y_prompt, y_sample, chunk_v_prompt, conv_prompt, ssm_conv_prompt, ssm_prompt,
            chunk_v_sample, conv_sample, ssm_conv_sample, ssm_sample)
```

```python
from contextlib import ExitStack
from collections import deque
import numpy as np
import concourse.bass as bass
import concourse.mybir as mybir
from concourse.bass_utils import run_bass_kernel_spmd

F32 = mybir.dt.float32
BF16 = mybir.dt.bfloat16
AF = mybir.ActivationFunctionType
ALU = mybir.AluOpType

NCORES = 8
DEPTH = 4
EPS = 1e-5
D_IN = 2824
COMPUTE = ("pe", "act", "dve", "pool")
ENGINES = ("pe", "act", "dve", "pool", "sp")


class Op:
    __slots__ = ("eng", "fn", "deps", "signal", "dma_key", "event", "idx", "finish")

    def __init__(self, eng, fn, dma_key):
        self.eng = eng
        self.fn = fn
        self.deps = []
        self.signal = dma_key is not None
        self.dma_key = dma_key
        self.event = None
        self.idx = -1
        self.finish = 0.0


class _FakeIns:
    def then_inc(self, *a, **k):
        return self


class _FakeEng:
    def __init__(self, kind):
        self.kind = kind
        self.cost = 0.0

    @staticmethod
    def _free(ap):
        n = 1
        for d in ap.shape[1:]:
            n *= int(d)
        return n

    def matmul(self, out, lhsT=None, rhs=None, **kw):
        n = max(self._free(out), 64)
        self.cost += n / 2.0 * (4.0 if lhsT.dtype == F32 else 1.0) + 8.0
        return _FakeIns()

    def dma_start(self, out=None, in_=None, **kw):
        self.cost += 2500.0
        return _FakeIns()

    def __getattr__(self, name):
        def f(*a, **k):
            out = a[0] if a else k.get("out", k.get("ap"))
            fr = self._free(out)
            if self.kind == "act":
                self.cost += 230.0 + 0.83 * fr
            elif self.kind == "pool":
                self.cost += 250.0 + 2.1 * fr
            else:
                self.cost += 110.0 + 1.05 * fr
            return _FakeIns()
        return f


HOP_NS = 120.0
PRIO_B = 0.0
PRE_UP = False
ENG_STB = 'act'
ENG_STBS = 'act'
ENG_BTOK = 'act'
ENG_YCT = 'dve'
PRE_A = True
FFN_ALIGN = 0
PRE_BANK = 2
EARLY_REL = True
FENCE_T = []
PRIO_C = 0.0


class Strm:
    def __init__(self, g, prio=0.0):
        self.g = g
        self.done = g is None
        self.t = 0.0
        self.prio = prio

    def step(self):
        if self.done:
            return False
        try:
            next(self.g)
            return True
        except StopIteration:
            self.done = True
            return False

BLAME = None
NORM_ENG = 'dve'


class Prog:
    def __init__(self, nc):
        self.nc = nc
        self.eng_ops = {e: [] for e in ENGINES}
        self.last_writer = {}
        self.readers = {}
        self.dma_keys = []
        self.last_dma = {}
        self.nops = 0
        self.eng_free = {e: 0.0 for e in ENGINES}
        self.scr_tok = "SCR"
        self.step_finish = 0.0
        self.busy = {}
        self.stall = {}

    def op(self, eng, fn, reads=(), writes=(), dma=None, scr=None):
        if scr is None:
            scr = dma is None
        if scr:
            reads = list(reads) + [self.scr_tok]
        o = Op(eng, fn, dma)
        o.idx = self.nops
        self.nops += 1
        if dma is not None and dma not in self.last_dma:
            self.dma_keys.append(dma)
        is_dma = dma is not None
        deps = {}

        def add(d, kind):
            if d is None or d is o:
                return
            if (not is_dma) and d.dma_key is None and d.eng == eng and kind != "raw":
                return
            deps[d.idx] = d

        if is_dma:
            add(self.last_dma.get(dma), "raw")
            self.last_dma[dma] = o
        for t in reads:
            add(self.last_writer.get(t), "raw")
        for t in writes:
            add(self.last_writer.get(t), "waw")
            for r in self.readers.get(t, ()):
                add(r, "war")
        o.deps = list(deps.values())
        for d in o.deps:
            d.signal = True
        fe = _FakeEng(eng)
        try:
            fn(fe)
        except Exception:
            fe.cost = 500.0
        start = self.eng_free[eng]
        blame = None
        for d in o.deps:
            if d.finish + HOP_NS > start:
                start = d.finish + HOP_NS
                blame = d
        if blame is not None and BLAME is not None:
            key = (eng, fn.__code__.co_firstlineno, blame.eng, blame.fn.__code__.co_firstlineno)
            BLAME[key] = BLAME.get(key, 0.0) + (start - self.eng_free[eng])
        if is_dma:
            o.finish = start + fe.cost
            self.eng_free[eng] = start + 60.0
        else:
            o.finish = start + fe.cost
            self.eng_free[eng] = o.finish
        if o.finish > self.step_finish:
            self.step_finish = o.finish
        self.busy[eng] = self.busy.get(eng, 0.0) + fe.cost
        self.stall[eng] = self.stall.get(eng, 0.0) + (start - (self.eng_free[eng] - (fe.cost if not is_dma else 60.0)) if False else 0.0)
        for t in reads:
            self.readers.setdefault(t, []).append(o)
        for t in writes:
            self.last_writer[t] = o
            self.readers[t] = []
        self.eng_ops[eng].append(o)
        return o

    def emit(self):
        nc = self.nc
        with ExitStack() as st:
            esem = {e: st.enter_context(nc.semaphore("s_" + e)) for e in COMPUTE}
            dsem = {k: st.enter_context(nc.semaphore("d%d" % i)) for i, k in enumerate(self.dma_keys)}
            for e in ENGINES:
                cnt = 0
                for o in self.eng_ops[e]:
                    if o.dma_key is None and o.signal:
                        cnt += 1
                        o.event = (esem[e], cnt)
            dcnt = {k: 0 for k in self.dma_keys}
            allops = sorted((o for e in ENGINES for o in self.eng_ops[e]), key=lambda o: o.idx)
            for o in allops:
                if o.dma_key is not None:
                    dcnt[o.dma_key] += 16
                    o.event = (dsem[o.dma_key], dcnt[o.dma_key])
            block = st.enter_context(nc.Block())

            def run(e, handle):
                waited = {}
                for o in self.eng_ops[e]:
                    need = {}
                    for d in o.deps:
                        sem, val = d.event
                        if need.get(id(sem), (None, 0))[1] < val:
                            need[id(sem)] = (sem, val)
                    for k, (sem, val) in need.items():
                        if waited.get(k, 0) < val:
                            handle.wait_ge(sem, val)
                            waited[k] = val
                    ins = o.fn(handle)
                    if o.dma_key is not None:
                        ins.then_inc(o.event[0], 16)
                    elif o.signal:
                        ins.then_inc(o.event[0], 1)
                if e == "sp":
                    for k in self.dma_keys:
                        if dcnt[k] and waited.get(id(dsem[k]), 0) < dcnt[k]:
                            handle.wait_ge(dsem[k], dcnt[k])

            @block.tensor
            def _(h):
                run("pe", h)

            @block.scalar
            def _(h):
                run("act", h)

            @block.vector
            def _(h):
                run("dve", h)

            @block.gpsimd
            def _(h):
                run("pool", h)

            @block.sync
            def _(h):
                run("sp", h)


def sub(ap, off, dims, np_=128, p0=0):
    ps = ap.ap[0][0]
    return bass.AP(ap.tensor, ap.offset + p0 * ps + off, [[ps, np_]] + [list(d) for d in dims])


def _cv_layout():
    lay = {}
    c = 0
    for nm, n in (("g1", 8), ("g2", 8), ("cw", 6), ("sw", 32), ("sb", 8)):
        lay[nm] = (c, n)
        c += n * DEPTH
    lay["gf"] = (c, 8)
    c += 8
    return lay, c


CVL, NCV = _cv_layout()


def cvcol(nm, l, i):
    base, n = CVL[nm]
    return base + l * n + i


NS = 16
NSG = 256
NFF = 512
GROUPS = [(list(range(0, 4)), True), (list(range(4, 10)), False), (list(range(10, 16)), False)]
NCOLMAX = 6 * 128
NL = DEPTH
LAST_TILE = 15
DEBUG_STOP = False
CARVE_DBG = {}


def build():
    nc = bass.Bass("TRN2", target_bir_lowering=False)
    P = Prog(nc)
    din = lambda n, s: nc.dram_tensor(n, s, F32, kind="ExternalInput").ap()
    dout = lambda n, s: nc.dram_tensor(n, s, F32, kind="ExternalOutput").ap()
    xp_d = din("xp", [1024, 128 * (LAST_TILE + 1)])
    xs_d = din("xs", [1024, 128])
    hc_d = din("hc", [DEPTH, 256, 16, 2])
    hx_d = din("hx", [DEPTH, 1024, 16, 3])
    ssm_d = din("ssm", [DEPTH, 16, 512, 128])
    win_d = din("w_in", [DEPTH, 1024, D_IN])
    wout_d = din("w_out", [DEPTH, 1024, 1024])
    wf1_d = din("w_ff1", [DEPTH, 1024, 4096])
    wf2_d = din("w_ff2", [DEPTH, 4096, 1024])
    cv_d = din("cv", [128, NCV])
    rows_d = din("rows", [DEPTH, 536])
    ws_d = din("w_s", [DEPTH, 4, 128, 128])
    brow_d = din("brow", [DEPTH, 2, 512])
    cm_d = din("cmat", [128, 5, 128])
    bind_d = din("bind", [128, 17])
    sel_d = din("sel2", [2, 128])

    yp_d = dout("yp", [1024, 128 * (LAST_TILE + 1)])
    ys_d = dout("ys", [1024, 128])
    cvp_d = dout("cvp", [DEPTH, 128, 256])
    cvs_d = dout("cvs", [DEPTH, 128, 256])
    ocp_d = dout("ocp", [DEPTH, 256, 2])
    oxp_d = dout("oxp", [DEPTH, 1024, 3])
    ocs_d = dout("ocs", [DEPTH, 256, 16, 2])
    oxs_d = dout("oxs", [DEPTH, 1024, 16, 3])
    osp_d = dout("osp", [DEPTH, 512, 128])
    oss_d = dout("oss", [DEPTH, 16, 512, 128])

    with ExitStack() as st:
        SB = lambda n, s, d=F32: st.enter_context(nc.sbuf_tensor(n, s, d))
        PSB = lambda n, s, d=F32: st.enter_context(nc.psum_tensor(n, s, d))
        xT = SB("xT", [128, 8, NCOLMAX])
        ring = SB("ring", [128, NS, 2048], BF16)
        wdt = SB("wdt", [128, DEPTH, 8, 8], BF16)
        cvt = SB("cvt", [128, NCV])
        cmat = SB("cmat_t", [128, 5, 128])
        identb = SB("identb", [128, 128], BF16)
        onesb = SB("onesb", [128, 128], BF16)
        bind = SB("bind_t", [128, 17])
        sel_f = SB("sel_f", [2, 128])
        sel_b = SB("sel_b", [2, 128], BF16)
        S_all = SB("S_all", [128, DEPTH, 4, 128])
        hist_g = SB("hist_g", [128, DEPTH, 2, 2])
        hist_x = SB("hist_x", [128, DEPTH, 8, 3])
        Cmask = SB("Cmask", [128, 2, 16, 128], BF16)
        rowsT = SB("rowsT", [128, 536])
        Arow = SB("Arow", [128, 8])
        wmT = SB("wmT", [128, 2, 4, 128], BF16)
        ws_nat = SB("ws_nat", [128, 4, 128])
        ws_nat1 = SB("ws_nat1", [128, 4, 128])
        brow_f = SB("brow_f", [2, 512])
        diagC = SB("diagC", [128, 4, 8, 128], BF16)
        diagB = SB("diagB", [128, 3, 2, 128], BF16)
        brow_b = SB("brow_b", [2, 512], BF16)

        SCRW = 19 * 1024
        scr = SB("scr", [128, SCRW])
        scr_b = scr.bitcast(BF16)
        dummy = SB("dummy_t", [128, 8])
        cur = [0]

        def carve(shape, dt=F32):
            n = int(np.prod(shape))
            words = n if dt == F32 else (n + 1) // 2
            words = (words + 7) // 8 * 8
            o = cur[0]
            cur[0] += words
            assert cur[0] <= SCRW, ("scratch overflow", cur[0])
            dims = []
            stride = 1
            for s_ in reversed(shape):
                dims.append([stride, s_])
                stride *= s_
            dims.reverse()
            if dt == F32:
                return bass.AP(scr, o, [[SCRW, 128]] + dims)
            return bass.AP(scr_b, 2 * o, [[2 * SCRW, 128]] + dims)

        N = NSG
        sq = carve([8, N], BF16)
        hT = carve([8, N], BF16)
        lnv = carve([N])
        rstd = carve([N])
        cg = carve([N])
        gext = carve([2, N + 2], BF16)
        xe = carve([2, N + 4], BF16)
        bgb = carve([2, N], BF16)
        off_cross = cur[0]
        uT = [carve([2, N], BF16) for _ in range(2)]
        mixT = [carve([8, N], BF16) for _ in range(2)]
        off_xact = cur[0]
        xact = [carve([8, N], BF16) for _ in range(2)]
        vb = [carve([2, 256], BF16) for _ in range(2)]
        zs = [carve([2, 512], BF16) for _ in range(2)]
        dtt = [carve([2, 8]) for _ in range(2)]
        att = [carve([2, 8]) for _ in range(2)]
        vf = carve([256])
        off_bfront = cur[0]
        xdt = carve([512], BF16)
        xdtd = carve([512], BF16)
        Btok = carve([256], BF16)
        xsD = carve([512], BF16)
        rseg = carve([1024])
        decT2 = [carve([1024], BF16) for _ in range(2)]
        cbTm = carve([256], BF16)
        aexp = carve([512])
        off_y1 = cur[0]
        y1 = carve([512])
        yg = carve([512])
        junk = carve([512], BF16)
        yc = carve([512], BF16)
        STb = carve([2, 512], BF16)
        de2 = [carve([8]) for _ in range(2)]
        ec2 = [carve([8]) for _ in range(2)]
        e1 = carve([8])
        dtr = carve([8])
        dechp2 = [carve([4, 16]) for _ in range(2)]
        ss = carve([8])
        lns = carve([8])
        rs = carve([8])
        off_sin = cur[0]
        Sin = carve([4, 3, 128])
        tmpS = carve([2, 128])
        BmaskQ = carve([2, 256], BF16)
        gext_s = carve([2, 16, 10], BF16)
        xe_s = carve([2, 16, 12], BF16)
        ocst = carve([2, 16, 2])
        oxst = carve([8, 16, 3])
        hcs = carve([2, 16, 2])
        hxs = carve([8, 16, 3])
        mixer_words = cur[0]
        for _nm in ("y1", "yg", "STb", "xdt", "xdtd", "cbTm", "xsD", "yc", "Btok", "aexp", "rseg"):
            _a = locals()[_nm]
            CARVE_DBG[_nm] = (int(_a.offset), [list(x) for x in _a.ap], str(_a.dtype))
        cur[0] = 0
        h2 = carve([8, NCOLMAX], BF16)
        assert cur[0] <= off_cross, (cur[0], off_cross)
        rr = carve([2, NFF], BF16)
        assert cur[0] <= off_cross, (cur[0], off_cross)
        yout = carve([8, 256])
        ffn_words = cur[0]
        assert cur[0] <= off_xact, (cur[0], off_xact)
        cur[0] = off_xact
        hidb = [carve([8, NFF], BF16)]
        assert cur[0] <= off_xact + 2 * 8 * N // 2, cur[0]
        cur[0] = off_sin
        hidb.append(carve([8, NFF], BF16))
        assert cur[0] <= mixer_words, (cur[0], mixer_words)
        cur[0] = off_bfront
        sq2 = carve([8, NFF], BF16)
        lnv2 = carve([NFF])
        rstd2 = carve([NFF])
        assert cur[0] <= off_y1, (cur[0], off_y1)

        bk = [PSB("bk%d" % i, [128, 512]) for i in range(4)]
        bk45 = PSB("bk45", [128, 1024])
        bk6 = PSB("bk6", [128, 512])
        bk7 = PSB("bk7", [128, 512])
        BKT = {i: ["bk%d" % i] for i in range(4)}
        B45 = ["bk4", "bk5"]
        PA = [bk[0][:, 0:256], bk[1][:, 0:256]]
        ssq_ps = bk[2][:, 0:256]
        pv_ps = bk[2][:, 256:512]
        pz_ps = bk[2]
        pdt_ps = bk[2][:, 0:8]
        de_ps = bk[3][:, 8:16]
        ec_ps = bk[3][:, 16:24]
        cl_ps = bk[3][:, 32:96]
        sTA_ps = bk[3][:, 256:512]
        bkST = [(bk[2], BKT[2]), (bk[3], BKT[3])]

        cm_U = [cmat[:, 0, :], cmat[:, 2, :]]
        cm_L = [cmat[:, 1, :], cmat[:, 3, :]]
        ident = cmat[:, 4, :]

        P.op("sp", lambda e: e.dma_start(out=cvt[:], in_=cv_d), writes=["cvt"], dma="c0")
        P.op("sp", lambda e: e.dma_start(out=cmat[:], in_=cm_d), writes=["cmat"], dma="c1")
        P.op("sp", lambda e: e.dma_start(out=bind[:], in_=bind_d), writes=["bind"], dma="c2")
        P.op("sp", lambda e: e.dma_start(out=sel_f[:], in_=sel_d), writes=["sel_f"], dma="c3")
        for l in range(DEPTH):
            P.op("pool", lambda e, l=l: e.dma_start(
                out=wdt[:, l, :, :], in_=win_d[l, :, 2816:2824].rearrange("(k p) c -> p k c", p=128)),
                writes=["wdt"], dma="c4")
        P.op("dve", lambda e: e.tensor_copy(identb[:], ident), reads=["cmat"], writes=["identb"])
        P.op("dve", lambda e: e.memset(onesb[:], 1.0), writes=["onesb"])
        P.op("dve", lambda e: e.tensor_copy(sel_b[:], sel_f[:]), reads=["sel_f"], writes=["sel_b"])
        P.op("dve", lambda e: e.memset(S_all[:], 0.0), writes=[("S", l) for l in range(DEPTH)])
        P.op("dve", lambda e: e.memset(hist_g[:], 0.0), writes=[("hist_g", l) for l in range(DEPTH)])
        P.op("dve", lambda e: e.memset(hist_x[:], 0.0), writes=[("hist_x", l) for l in range(DEPTH)])
        P.op("dve", lambda e: e.memset(Cmask[:], 0.0), writes=["Cmask"])
        P.op("dve", lambda e: e.memset(ws_nat1[:], 0.0), writes=["ws_nat1z"])

        free_slots = deque(range(NS))
        pending = deque()
        loc = {}
        for gi in range(len(GROUPS)):
            for l in range(NL):
                for i in range(11):
                    pending.append((gi, l, "win", i))
                for i in range(4):
                    pending.append((gi, l, "wout", i))
                for j in range(32):
                    pending.append((gi, l, "ffn", j))

        def rtok(s_):
            return [("ring", s_), ("ring2", s_)]

        def issue(ld, s_):
            gi, l, kind, i = ld
            if kind in ("win", "wout"):
                src = (win_d if kind == "win" else wout_d)[l, :, 256 * i:256 * i + 256]
                dst = sub(ring[:], s_ * 2048, [[256, 8], [1, 256]])
                P.op("pool", lambda e: e.dma_start(out=dst, in_=src.rearrange("(k p) c -> p k c", p=128)),
                     writes=rtok(s_), dma=("w", s_, 0))
            else:
                src1 = wf1_d[l, :, 128 * i:128 * i + 128].rearrange("(k p) c -> p k c", p=128)
                dst1 = sub(ring[:], s_ * 2048, [[128, 8], [1, 128]])
                P.op("pool", lambda e: e.dma_start(out=dst1, in_=src1), writes=[("ring", s_)], dma=("w", s_, 0))
                src2 = wf2_d[l, 128 * i:128 * i + 128, :]
                dst2 = sub(ring[:], s_ * 2048 + 1024, [[1, 1024]])
                P.op("pool", lambda e: e.dma_start(out=dst2, in_=src2), writes=[("ring2", s_)], dma=("w", s_, 1))

        def pump():
            while free_slots and pending:
                ld = pending.popleft()
                s_ = free_slots.popleft()
                issue(ld, s_)
                loc[ld] = s_

        def wslot(ld):
            if ld not in loc:
                pump()
            assert ld in loc, ("ring too small for", ld)
            return loc[ld]

        def release(ld):
            free_slots.append(loc.pop(ld))
            pump()

        def wview(s_, k, c0, n):
            return sub(ring[:], s_ * 2048 + k * 256 + c0, [[1, n]])

        def rmsnorm_steps(xall, xk, gname, l, sqb, ssqp, pstok, lnvb, rstdb, out_fn, n, xtok, otok, tagp, fine=False):
            if not fine:
                P.op("act", lambda e: e.activation(sqb[:, :, 0:n], xall, AF.Square), reads=xtok, writes=[(tagp, "sq")])
                yield

                def mm(e):
                    for k in range(8):
                        i = e.matmul(ssqp[:, 0:n], lhsT=onesb[:], rhs=sqb[:, k, 0:n], start=(k == 0), stop=(k == 7))
                    return i
                P.op("pe", mm, reads=[(tagp, "sq"), "onesb"], writes=[pstok])
            else:
                for q in range(4):
                    P.op("act", lambda e, q=q: e.activation(sqb[:, 2 * q:2 * q + 2, 0:n], xall[:, 2 * q:2 * q + 2, :], AF.Square),
                         reads=xtok, writes=[(tagp, "sq", q)])
                yield
                for q in range(4):
                    def mmq(e, q=q):
                        for k in (2 * q, 2 * q + 1):
                            i = e.matmul(ssqp[:, 0:n], lhsT=onesb[:], rhs=sqb[:, k, 0:n], start=(k == 0), stop=(k == 7))
                        return i
                    P.op("pe", mmq, reads=[(tagp, "sq", q), "onesb"], writes=[pstok])
            P.op("act", lambda e: e.activation(lnvb[:, 0:n], ssqp[:, 0:n], AF.Ln, bias=EPS, scale=1.0 / 1024),
                 reads=[pstok], writes=[(tagp, "lnv")])
            P.op("act", lambda e: e.activation(rstdb[:, 0:n], lnvb[:, 0:n], AF.Exp, scale=-0.5),
                 reads=[(tagp, "lnv")], writes=[(tagp, "rstd")])
            yield
            for k in range(8):
                gcol = cvt[:, cvcol(gname, l, k):cvcol(gname, l, k) + 1]
                P.op(NORM_ENG if tagp == "nA" else "dve", lambda e, k=k, gcol=gcol: e.scalar_tensor_tensor(
                    out_fn(k), xk(k), gcol, rstdb[:, 0:n], ALU.mult, ALU.mult),
                    reads=xtok + [(tagp, "rstd"), "cvt"], writes=(otok + [("hTk", k)]) if fine else otok)

        def rmsnorm_cols(*a):
            for _ in rmsnorm_steps(*a):
                pass

        def b16(ap, n=16):
            return ap.rearrange("p (b t) -> p b t", b=n)

        def prep_load(l):
            P.op("sp", lambda e: e.dma_start(out=rowsT[:], in_=rows_d[l:l + 1, :].partition_broadcast(128)),
                 writes=["rowsT"], dma="r0")
            P.op("sp", lambda e: e.dma_start(out=brow_f[:], in_=brow_d[l]), writes=["brow_f"], dma="r1")
            P.op("sp", lambda e: e.dma_start(out=ws_nat[:], in_=ws_d[l].rearrange("h t s -> t h s")),
                 writes=["ws_nat"], dma="r2")
            for b in range(16):
                P.op("sp", lambda e, b=b: e.dma_start(
                    out=ws_nat1[8 * b:8 * b + 8, :, 8 * b:8 * b + 8],
                    in_=ws_d[l, :, 0:8, 0:8].rearrange("h t s -> t h s")),
                    reads=["ws_nat1z"], writes=[("ws_blk", b)], dma=("r3", b % 4))

        def prep_compute(l):
            P.op("act", lambda e: e.activation(Arow[:], rowsT[:, 8:16], AF.Exp), reads=["rowsT"], writes=["Arow"])
            P.op("dve", lambda e: e.tensor_scalar(Arow[:], Arow[:], -1.0, None, ALU.mult), reads=["Arow"], writes=["Arow"])
            P.op("dve", lambda e: e.tensor_copy(brow_b[:], brow_f[:]), reads=["brow_f"], writes=["brow_b"])
            for k in range(4):
                c0_ = cvcol("sw", l, k * 8)
                P.op("dve", lambda e, k=k, c0_=c0_: e.tensor_tensor(
                    diagC[:, k, :, :], sub(ident, 0, [[0, 8], [1, 128]]), sub(cvt[:, c0_:c0_ + 1], 0, [[1, 8], [0, 128]]), ALU.mult),
                    reads=["cmat", "cvt"], writes=["diagC"])
            for k in range(3):
                c0_ = cvcol("cw", l, k * 2)
                P.op("dve", lambda e, k=k, c0_=c0_: e.tensor_tensor(
                    diagB[:, k, :, :], sub(ident, 0, [[0, 2], [1, 128]]), sub(cvt[:, c0_:c0_ + 1], 0, [[1, 2], [0, 128]]), ALU.mult),
                    reads=["cmat", "cvt"], writes=["diagB"])
            for v in range(2):
                src_t = ws_nat if v == 0 else ws_nat1
                rd = ["ws_nat", "cmat"] if v == 0 else ["ws_nat1z", "cmat"] + [("ws_blk", b) for b in range(16)]

                bkt_ = bk7 if v == 0 else bk6
                bkn_ = "bk7" if v == 0 else "bk6"

                def tr(e, src_t=src_t, bkt_=bkt_):
                    for h in range(4):
                        i = e.matmul(bkt_[:, h * 128:(h + 1) * 128], lhsT=src_t[:, h, :], rhs=ident, start=True, stop=True)
                    return i
                P.op("pe", tr, reads=rd, writes=[bkn_])
                P.op("dve", lambda e, v=v, bkt_=bkt_: e.tensor_tensor(
                    wmT[:, v, :, :], bkt_[:].rearrange("p (h t) -> p h t", h=4),
                    sub(cm_U[v], 0, [[0, 4], [1, 128]]), ALU.mult),
                    reads=[bkn_, "cmat"], writes=[("wmT", v)])

        def layer_prep(l):
            prep_load(l)
            prep_compute(l)

        def norm1_steps(l, sg, guard=()):
            slots, c0, n = sg["slots"], sg["c0"], sg["n"]
            xtok = [("xT", t, k) for t in slots for k in range(8)] + list(guard)
            for _ in rmsnorm_steps(xT[:, :, c0:c0 + n], lambda k: xT[:, k, c0:c0 + n], "g1", l,
                                   sq, ssq_ps, "bk2", lnv, rstd, lambda k: hT[:, k, 0:n], n, xtok, ["hT"], "nA", fine=True):
                yield

        def stream_A(gi, l, sg, skip_norm=False):
            tiles, slots, c0, n, is_s, par = sg["tiles"], sg["slots"], sg["c0"], sg["n"], sg["sample"], sg["par"]
            xtok = [("xT", t, k) for t in slots for k in range(8)]
            for _ in ([] if skip_norm else [0]):
              for _ in rmsnorm_steps(xT[:, :, c0:c0 + n], lambda k: xT[:, k, c0:c0 + n], "g1", l,
                                     sq, ssq_ps, "bk2", lnv, rstd, lambda k: hT[:, k, 0:n], n, xtok, ["hT"], "nA", fine=True):
                  yield
            yield
            wl = lambda i: wslot((gi, l, "win", i))

            def proj(load, half, pa):
                s_ = wl(load)

                def mm(e):
                    for k in range(8):
                        i = e.matmul(pa[:, 0:n], lhsT=wview(s_, k, half * 128, 128), rhs=hT[:, k, 0:n],
                                     start=(k == 0), stop=(k == 7))
                    return i
                return mm, rtok(s_) + ["hT"]
            pi = [0]

            def nextpa():
                pi[0] ^= 1
                return PA[pi[0]], "bk%d" % pi[0]
            for ti in range(len(tiles)):
                if ti == 0:
                    for k in range(8):
                        P.op("pe", lambda e, k=k: e.matmul(pdt_ps, lhsT=hT[:, k, 0:128], rhs=wdt[:, l, k, :],
                                                          start=(k == 0), stop=(k == 7)),
                             reads=[("hTk", k), "wdt"], writes=["bk2"])
                else:
                    def mmd(e, ti=ti):
                        for k in range(8):
                            i = e.matmul(pdt_ps, lhsT=hT[:, k, ti * 128:(ti + 1) * 128], rhs=wdt[:, l, k, :],
                                         start=(k == 0), stop=(k == 7))
                        return i
                    P.op("pe", mmd, reads=["hT", "wdt"], writes=["bk2"])
                P.op("dve", lambda e: e.tensor_tensor(dtr, pdt_ps, rowsT[:, 0:8], ALU.add),
                     reads=["bk2", "rowsT"], writes=["dtr"])
                P.op("act", lambda e: e.activation(e1, dtr, AF.Exp), reads=["dtr"], writes=["e1"])
                P.op("act", lambda e, ti=ti: e.activation(dtt[par][:, ti, :], e1, AF.Ln, bias=1.0), reads=["e1"],
                     writes=[("dtt", par, ti)])
                P.op("dve", lambda e, ti=ti: e.tensor_tensor(att[par][:, ti, :], dtt[par][:, ti, :], Arow[:], ALU.mult),
                     reads=[("dtt", par, ti), "Arow"], writes=[("att", par, ti)])
            yield
            for j in range(2):
                pa, pt = nextpa()
                mm, rd = proj(0, j, pa)
                P.op("pe", mm, reads=rd, writes=[pt])
                P.op("act", lambda e, j=j, pa=pa: e.activation(uT[par][:, j, 0:n], pa[:, 0:n], AF.Gelu),
                     reads=[pt], writes=[("uT", par)])
            yield
            sv = wl(1)
            for ti, t in enumerate(tiles):
                def mmv(e, ti=ti):
                    for k in range(8):
                        i = e.matmul(pv_ps, lhsT=hT[:, k, ti * 128:(ti + 1) * 128], rhs=wview(sv, k, 0, 256),
                                     start=(k == 0), stop=(k == 7))
                    return i
                P.op("pe", mmv, reads=["hT"] + rtok(sv), writes=["bk2"])
                P.op("act", lambda e, ti=ti: e.activation(vb[par][:, ti, :], pv_ps, AF.Gelu),
                     reads=["bk2"], writes=[("vb", par, ti)])
                if is_s or t == LAST_TILE:
                    P.op("act", lambda e: e.activation(vf, pv_ps, AF.Gelu), reads=["bk2"], writes=["vf"])
                    dst = cvs_d[l] if is_s else cvp_d[l]
                    P.op("sp", lambda e, dst=dst: e.dma_start(out=dst, in_=vf), reads=["vf"], dma="ov", scr=True)
            yield
            cps = bk[2]
            for j in range(2):
                pa, pt = nextpa()
                mm, rd = proj(2, j, pa)
                P.op("pe", mm, reads=rd, writes=[pt])
                P.op("act", lambda e, pa=pa, j=j: e.activation(bgb[:, j, 0:n], pa[:, 0:n], AF.Copy),
                     reads=[pt], writes=[("bgb", j)])
            for j in range(2):
                pa, pt = nextpa()
                mm, rd = proj(3, j, pa)
                P.op("pe", mm, reads=rd, writes=[pt])
                P.op("act", lambda e, pa=pa: e.activation(cg[:, 0:n], pa[:, 0:n], AF.Copy), reads=[pt], writes=["cg"])
                pa, pt = nextpa()
                mm, rd = proj(4, j, pa)
                P.op("pe", mm, reads=rd, writes=[pt])
                if not is_s:
                    gx = gext[:, j, :]
                    gt = ("gext", j)
                    taps = [gx[:, k:k + n] for k in range(3)]
                    cout = cps[:, 0:n]
                    P.op("dve", lambda e, gx=gx, j=j: e.tensor_copy(gx[:, 0:2], hist_g[:, l, j, :]),
                         reads=[("hist_g", l)], writes=[gt])
                    P.op("dve", lambda e, gx=gx, pa=pa: e.tensor_tensor(gx[:, 2:2 + n], pa[:, 0:n], cg[:, 0:n], ALU.mult),
                         reads=[pt, "cg", gt], writes=[gt])
                    P.op("dve", lambda e, pa=pa, j=j: e.tensor_tensor(hist_g[:, l, j, :], pa[:, n - 2:n], cg[:, n - 2:n], ALU.mult),
                         reads=[pt, "cg", gt], writes=[("hist_g", l)])
                else:
                    gx = gext_s[:, j, :, :]
                    gt = ("gext_s", j)
                    taps = [gx[:, :, k:k + 8] for k in range(3)]
                    cout = b16(cps[:, 0:128])
                    P.op("dve", lambda e, gx=gx, j=j: e.tensor_copy(gx[:, :, 0:2], hcs[:, j, :, :]),
                         reads=["hcs"], writes=[gt])
                    P.op("dve", lambda e, gx=gx, pa=pa: e.tensor_tensor(gx[:, :, 2:10], b16(pa[:, 0:128]), b16(cg[:, 0:128]), ALU.mult),
                         reads=[pt, "cg", gt], writes=[gt])
                    P.op("dve", lambda e, pa=pa, j=j: e.tensor_tensor(
                        ocst[:, j, :, :], b16(pa[:, 0:128])[:, :, 6:8], b16(cg[:, 0:128])[:, :, 6:8], ALU.mult),
                        reads=[pt, "cg"], writes=["ocst"])

                def mmc(e, j=j, taps=taps, cout=cout):
                    for k in range(3):
                        i = e.matmul(cout, lhsT=diagB[:, k, j, :], rhs=taps[k], start=(k == 0), stop=(k == 2))
                    return i
                P.op("pe", mmc, reads=[gt, "diagB"], writes=["bk2"])
                P.op("dve", lambda e, j=j: e.tensor_tensor(mixT[par][:, 2 + j, 0:n], cps[:, 0:n], bgb[:, j, 0:n], ALU.mult),
                     reads=["bk2", ("bgb", j)], writes=[("mixT", par, 2 + j)])
                yield
            if is_s:
                P.op("sp", lambda e: e.dma_start(out=ocs_d[l].rearrange("(j p) b k -> p j b k", p=128), in_=ocst),
                     reads=["ocst"], dma="oc", scr=True)
            sz = [wl(5), wl(6)]
            for ti in range(len(tiles)):
                def mmz(e, ti=ti):
                    for hf in range(2):
                        for k in range(8):
                            i = e.matmul(pz_ps[:, hf * 256:(hf + 1) * 256], lhsT=hT[:, k, ti * 128:(ti + 1) * 128],
                                         rhs=wview(sz[hf], k, 0, 256), start=(k == 0), stop=(k == 7))
                    return i
                P.op("pe", mmz, reads=["hT"] + rtok(sz[0]) + rtok(sz[1]), writes=["bk2"])
                P.op("act", lambda e, ti=ti: e.activation(zs[par][:, ti, :], pz_ps[:], AF.Silu),
                     reads=["bk2"], writes=[("zs", par, ti)])
            yield
            def xbc_front(c):
                pa, pt = nextpa()
                mm, rd = proj(7 + c // 2, c % 2, pa)
                P.op("pe", mm, reads=rd, writes=[pt])
                r = c % 2
                if not is_s:
                    xx = xe[:, r, :]
                    xt_ = ("xe", r)
                    taps = [xx[:, k:k + n] for k in range(4)]
                    cout = cps[:, 0:n]
                    P.op("dve", lambda e, xx=xx, c=c: e.tensor_copy(xx[:, 0:3], hist_x[:, l, c, :]),
                         reads=[("hist_x", l)], writes=[xt_])
                    P.op("act", lambda e, xx=xx, pa=pa: e.activation(xx[:, 3:3 + n], pa[:, 0:n], AF.Copy),
                         reads=[pt, xt_], writes=[xt_])
                    P.op("act", lambda e, pa=pa, c=c: e.activation(hist_x[:, l, c, :], pa[:, n - 3:n], AF.Copy),
                         reads=[pt, xt_], writes=[("hist_x", l)])
                else:
                    xx = xe_s[:, r, :, :]
                    xt_ = ("xe_s", r)
                    taps = [xx[:, :, k:k + 8] for k in range(4)]
                    cout = b16(cps[:, 0:128])
                    P.op("dve", lambda e, xx=xx, c=c: e.tensor_copy(xx[:, :, 0:3], hxs[:, c, :, :]),
                         reads=["hxs"], writes=[xt_])
                    P.op("act", lambda e, xx=xx, pa=pa: e.activation(xx[:, :, 3:11], b16(pa[:, 0:128]), AF.Copy),
                         reads=[pt, xt_], writes=[xt_])
                    P.op("act", lambda e, pa=pa, c=c: e.activation(oxst[:, c, :, :], b16(pa[:, 0:128])[:, :, 5:8], AF.Copy),
                         reads=[pt], writes=["oxst"])
                return xt_, taps, cout

            def xbc_back(c, xt_, taps, cout):
                bcol = cvt[:, cvcol("sb", l, c):cvcol("sb", l, c) + 1]

                def mmc(e):
                    for k in range(4):
                        i = e.matmul(cout, lhsT=diagC[:, k, c, :], rhs=taps[k], start=(k == 0), stop=(k == 3))
                    return i
                P.op("pe", mmc, reads=[xt_, "diagC"], writes=["bk2"])
                P.op("act", lambda e: e.activation(xact[par][:, c, 0:n], cps[:, 0:n], AF.Silu, bias=bcol),
                     reads=["bk2", "cvt"], writes=[("xact", par, c)])
            pend = xbc_front(0)
            for c in range(8):
                nxt_ = xbc_front(c + 1) if c + 1 < 8 else None
                xbc_back(c, *pend)
                pend = nxt_
                yield
            if is_s:
                P.op("sp", lambda e: e.dma_start(out=oxs_d[l].rearrange("(c p) b k -> p c b k", p=128), in_=oxst),
                     reads=["oxst"], dma="ox", scr=True)

        def stream_B(gi, l, sg):
            tiles, slots, c0, n, is_s, par = sg["tiles"], sg["slots"], sg["c0"], sg["n"], sg["sample"], sg["par"]
            v = 1 if is_s else 0
            U, L = cm_U[v], cm_L[v]
            NB = 16 if is_s else 1
            bcol0 = 1 if is_s else 0
            XA = xact[par]
            h8 = lambda a: a.rearrange("p (h q) -> p h q", h=8)
            for ti, t in enumerate(tiles):
                a_ = att[par][:, ti, :]
                de, ec, dechp, decT = de2[ti], ec2[ti], dechp2[ti], decT2[ti]
                P.op("dve", lambda e, a_=a_: e.tensor_tensor(
                    rseg.rearrange("p (h t) -> p h t", h=8), sub(U, 0, [[0, 8], [1, 128]]),
                    sub(a_, 0, [[1, 8], [0, 128]]), ALU.mult),
                    reads=["cmat", ("att", par, ti)], writes=["rseg"])
                P.op("dve", lambda e, a_=a_: e.tensor_copy(h8(aexp), sub(a_, 0, [[1, 8], [0, 64]])),
                     reads=[("att", par, ti)], writes=["aexp"])
                yield

                def mmS(e, a_=a_):
                    e.matmul(de_ps, lhsT=L, rhs=a_, start=True, stop=True)
                    e.matmul(ec_ps, lhsT=U, rhs=a_, start=True, stop=True)
                    for j in range(4):
                        i = e.matmul(cl_ps[:, j * 16:j * 16 + NB], lhsT=aexp[:, j * 128:(j + 1) * 128],
                                     rhs=bind[:, bcol0:bcol0 + NB], start=True, stop=True)
                    return i
                P.op("pe", mmS, reads=["cmat", ("att", par, ti), "aexp", "bind"], writes=["bk3"])
                P.op("act", lambda e, de=de: e.activation(de, de_ps, AF.Exp), reads=["bk3"], writes=[("de", ti)])
                P.op("act", lambda e, ec=ec: e.activation(ec, ec_ps, AF.Exp), reads=["bk3"], writes=[("ec", ti)])
                for j in range(4):
                    P.op("act", lambda e, j=j, dechp=dechp: e.activation(dechp[:, j, 0:NB], cl_ps[:, j * 16:j * 16 + NB], AF.Exp),
                         reads=["bk3"], writes=[("dechp", ti)])

                def mmSeg(e):
                    e.matmul(bk45[:, 0:512], lhsT=L, rhs=rseg[:, 0:512], start=True, stop=True)
                    return e.matmul(bk45[:, 512:1024], lhsT=L, rhs=rseg[:, 512:1024], start=True, stop=True)
                P.op("pe", mmSeg, reads=["cmat", "rseg"], writes=B45)
                P.op("act", lambda e, decT=decT: e.activation(decT, bk45[:], AF.Exp), reads=B45, writes=[("decT", ti)])
                yield
            def tile_gen(ti, t):
                tc0 = ti * 128
                de, ec, dechp, decT = de2[ti], ec2[ti], dechp2[ti], decT2[ti]
                def mmA(e, ti=ti):
                    for j in range(2):
                        e.matmul(sTA_ps[:, j * 128:(j + 1) * 128], lhsT=sel_b[:],
                                 rhs=brow_b[:, v * 256 + j * 128: v * 256 + (j + 1) * 128], start=True, stop=False)
                        for hh in range(2):
                            h = 2 * j + hh
                            i = e.matmul(sTA_ps[64 * hh:64 * hh + 64, j * 128:(j + 1) * 128],
                                         lhsT=vb[par][:, ti, h * 64:(h + 1) * 64], rhs=wmT[:, v, h, :],
                                         start=False, stop=True, tile_position=(0, 64 * hh))
                    return i
                P.op("pe", mmA, reads=["sel_b", "brow_b", ("vb", par, ti), ("wmT", v)], writes=["bk3"])
                P.op("dve", lambda e, tc0=tc0: e.tensor_tensor(
                    mixT[par][:, 0:2, tc0:tc0 + 128], sTA_ps.rearrange("p (j t) -> p j t", j=2),
                    uT[par][:, :, tc0:tc0 + 128], ALU.mult),
                    reads=["bk3", ("uT", par)], writes=[("mixT", par, 0), ("mixT", par, 1)])
                yield
                dt_ = dtt[par][:, ti, :]
                a_ = att[par][:, ti, :]
                def mmT(e, tc0=tc0):
                    for c in range(4):
                        e.matmul(bk45[:, c * 128:(c + 1) * 128], lhsT=XA[:, c, tc0:tc0 + 128], rhs=identb[:],
                                 start=True, stop=True)
                    for c in range(2):
                        i = e.matmul(bk[3][:, c * 128:(c + 1) * 128], lhsT=XA[:, 4 + c, tc0:tc0 + 128], rhs=identb[:],
                                     start=True, stop=True)
                    return i
                P.op("pe", mmT, reads=[("xact", par, c) for c in range(6)] + ["identb"], writes=["bk4", "bk3"])
                tp3 = h8(bk45[:, 0:512])
                P.op("dve", lambda e, dt_=dt_: e.tensor_tensor(h8(xdt), tp3, sub(dt_, 0, [[1, 8], [0, 64]]), ALU.mult),
                     reads=["bk4", ("dtt", par, ti)], writes=["xdt"])
                P.op("dve", lambda e: e.tensor_tensor(h8(xsD), tp3, sub(rowsT[:, 16:24], 0, [[1, 8], [0, 64]]), ALU.mult),
                     reads=["bk4", "rowsT"], writes=["xsD"])
                P.op(ENG_BTOK, (lambda e: e.activation(Btok, bk[3][:, 0:256], AF.Copy)) if ENG_BTOK == "act" else
                     (lambda e: e.tensor_copy(Btok, bk[3][:, 0:256])), reads=["bk3"], writes=["Btok"])
                P.op("dve", lambda e, de=de: e.tensor_tensor(h8(xdtd), h8(xdt), sub(de, 0, [[1, 8], [0, 64]]), ALU.mult),
                     reads=["xdt", ("de", ti)], writes=["xdtd"])
                yield
                def mmCB(e, tc0=tc0):
                    for g in range(2):
                        i = e.matmul(bk7[:, g * 128:(g + 1) * 128], lhsT=XA[:, 4 + g, tc0:tc0 + 128],
                                     rhs=XA[:, 6 + g, tc0:tc0 + 128], start=True, stop=True)
                    return i
                P.op("pe", mmCB, reads=[("xact", par, c) for c in range(4, 8)], writes=["bk7"])
                P.op("dve", lambda e: e.tensor_tensor(
                    cbTm.rearrange("p (g t) -> p g t", g=2), bk7[:, 0:256].rearrange("p (g t) -> p g t", g=2),
                    sub(U, 0, [[0, 2], [1, 128]]), ALU.mult), reads=["bk7", "cmat"], writes=["cbTm"])
                d4 = decT.rearrange("p (g k t) -> p g k t", g=2, k=4)
                P.op("dve", lambda e, d4=d4: e.tensor_tensor(d4, d4, sub(cbTm, 0, [[128, 2], [0, 4], [1, 128]]), ALU.mult),
                     reads=[("decT", ti), "cbTm"], writes=[("decT", ti)])
                yield

                def mmYD(e, decT=decT):
                    e.matmul(bk6[:, 0:512], lhsT=identb[:], rhs=xsD, start=True, stop=False)
                    for h in range(8):
                        i = e.matmul(bk6[:, h * 64:(h + 1) * 64], lhsT=decT[:, h * 128:(h + 1) * 128],
                                     rhs=xdt[:, h * 64:(h + 1) * 64], start=False, stop=(h == 7))
                    return i
                P.op("pe", mmYD, reads=[("decT", ti), "xdt", "xsD", "identb"], writes=["bk6"])
                if not is_s:
                    def mmST(e):
                        for j in range(4):
                            i = e.matmul(bk7[:, j * 128:(j + 1) * 128], lhsT=S_all[:, l, j, :], rhs=ident,
                                         start=True, stop=True)
                        return i
                    P.op("pe", mmST, reads=[("S", l), "cmat"], writes=["bk7"])
                    P.op(ENG_STB, (lambda e: e.activation(STb[:, 0, :], bk7[:], AF.Copy)) if ENG_STB == "act" else
                         (lambda e: e.tensor_copy(STb[:, 0, :], bk7[:])), reads=["bk7"], writes=[("STb", 0)])
                    yield

                    def mmYO(e, tc0=tc0):
                        for g in range(2):
                            i = e.matmul(bk7[:, g * 256:(g + 1) * 256], lhsT=XA[:, 6 + g, tc0:tc0 + 128],
                                         rhs=STb[:, 0, g * 256:(g + 1) * 256], start=True, stop=True)
                        return i
                    P.op("pe", mmYO, reads=[("xact", par, 6), ("xact", par, 7), ("STb", 0)], writes=["bk7"])
                    yield

                    def mmSt(e):
                        for j in range(4):
                            g = j // 2
                            i = e.matmul(bk45[:, j * 128:(j + 1) * 128], lhsT=xdtd[:, j * 128:(j + 1) * 128],
                                         rhs=Btok[:, g * 128:(g + 1) * 128], start=True, stop=True)
                        return i
                    P.op("pe", mmSt, reads=["xdtd", "Btok"], writes=["bk4"])
                    for j in range(4):
                        P.op("dve", lambda e, j=j, dechp=dechp: e.scalar_tensor_tensor(
                            S_all[:, l, j, :], S_all[:, l, j, :], dechp[:, j, 0:1], bk45[:, j * 128:(j + 1) * 128],
                            ALU.mult, ALU.add), reads=[("S", l), ("dechp", ti), "bk4"], writes=[("S", l)])
                    if t == LAST_TILE:
                        P.op("sp", lambda e: e.dma_start(out=osp_d[l].rearrange("(j p) n -> p j n", p=128),
                                                         in_=S_all[:, l, :, :]), reads=[("S", l)], dma="os")
                else:
                    P.op("dve", lambda e: e.tensor_copy(
                        sub(Cmask[:], 0, [[2048, 2], [136, 16], [1, 8]]),
                        sub(XA[:, 6, 0:1], 0, [[NSG, 2], [8, 16], [1, 8]])),
                        reads=[("xact", par, 6), ("xact", par, 7)], writes=["Cmask"])
                    def load_state(b):
                        P.op("sp", lambda e, b=b: e.dma_start(
                            out=Sin[:, :, b % 3, :], in_=ssm_d[l, b].rearrange("(j p) n -> p j n", p=128)),
                            writes=[("Sin", b % 3)], dma=("si", b % 3), scr=True)
                    load_state(0)
                    load_state(1)
                    for b in range(16):
                        r = b % 2
                        bkr, bkt = bkST[r]
                        r3 = b % 3
                        if b + 2 < 16:
                            load_state(b + 2)

                        def mmST(e, r3=r3, bkr=bkr):
                            for j in range(4):
                                i = e.matmul(bkr[:, j * 128:(j + 1) * 128], lhsT=Sin[:, j, r3, :], rhs=ident,
                                             start=True, stop=True)
                            return i
                        P.op("pe", mmST, reads=[("Sin", r3), "cmat"], writes=bkt)
                        P.op(ENG_STBS, (lambda e, r=r, bkr=bkr: e.activation(STb[:, r, :], bkr[:], AF.Copy)) if ENG_STBS == "act" else
                             (lambda e, r=r, bkr=bkr: e.tensor_copy(STb[:, r, :], bkr[:])),
                             reads=bkt, writes=[("STb", r)])

                        def mmYO(e, b=b, r=r):
                            for g in range(2):
                                i = e.matmul(bk7[:, g * 256:(g + 1) * 256], lhsT=Cmask[:, g, b, :],
                                             rhs=STb[:, r, g * 256:(g + 1) * 256], start=(b == 0 and g == 0), stop=(b == 15),
                                             skip_group_check=True)
                            return i
                        P.op("pe", mmYO, reads=["Cmask", ("STb", r)], writes=["bk7"])
                        P.op("dve", lambda e, b=b, r=r: e.tensor_scalar(BmaskQ[:, r, :], Btok, bind[:, 1 + b:2 + b], None, ALU.mult),
                             reads=["Btok", "bind"], writes=[("BmaskQ", r)])
                        st_ps = bk45[:, r * 512:(r + 1) * 512]

                        def mmSt(e, r=r, st_ps=st_ps):
                            for j in range(4):
                                g = j // 2
                                i = e.matmul(st_ps[:, j * 128:(j + 1) * 128], lhsT=xdtd[:, j * 128:(j + 1) * 128],
                                             rhs=BmaskQ[:, r, g * 128:(g + 1) * 128], start=True, stop=True)
                            return i
                        P.op("pe", mmSt, reads=["xdtd", ("BmaskQ", r)], writes=[B45[r]])
                        for j in range(4):
                            P.op("dve", lambda e, j=j, b=b, r3=r3, st_ps=st_ps, dechp=dechp: e.scalar_tensor_tensor(
                                Sin[:, j, r3, :], Sin[:, j, r3, :], dechp[:, j, b:b + 1], st_ps[:, j * 128:(j + 1) * 128],
                                ALU.mult, ALU.add), reads=[("Sin", r3), ("dechp", ti), B45[r]], writes=[("Sin", r3)])
                        P.op("sp", lambda e, b=b, r3=r3: e.dma_start(
                            out=oss_d[l, b].rearrange("(j p) n -> p j n", p=128), in_=Sin[:, :, r3, :]),
                            reads=[("Sin", r3)], dma=("so", r3), scr=True)
                        yield
                yield "SPLIT"
                P.op("dve", lambda e, ec=ec: e.tensor_tensor(h8(y1), h8(bk7[:]), sub(ec, 0, [[1, 8], [0, 64]]), ALU.mult),
                     reads=["bk7", ("ec", ti)], writes=["y1"])
                P.op("dve", lambda e: e.tensor_tensor(y1, y1, bk6[:], ALU.add), reads=["y1", "bk6"], writes=["y1"])
                P.op("dve", lambda e, ti=ti: e.tensor_tensor(yg, y1, zs[par][:, ti, :], ALU.mult),
                     reads=["y1", ("zs", par, ti)], writes=["yg"])
                P.op("dve", lambda e: e.memset(ss[:, 0:1], 0.0), writes=["ss"])
                P.op("act", lambda e: e.activation(junk, yg, AF.Square, accum_out=ss[:, 0:1]),
                     reads=["yg", "ss"], writes=["junk", "ss"])
                P.op("act", lambda e: e.activation(lns[:, 0:1], ss[:, 0:1], AF.Ln, bias=EPS, scale=1.0 / 512),
                     reads=["ss"], writes=["lns"])
                P.op("act", lambda e: e.activation(rs[:, 0:1], lns[:, 0:1], AF.Exp, scale=-0.5),
                     reads=["lns"], writes=["rs"])
                yield
                P.op("dve", lambda e: e.scalar_tensor_tensor(yc, yg, rs[:, 0:1], rowsT[:, 24:536], ALU.mult, ALU.mult),
                     reads=["yg", "rs", "rowsT"], writes=["yc"])
                yield

                def mmYT(e):
                    for j in range(4):
                        i = e.matmul(bk45[:, 512 + j * 128:512 + (j + 1) * 128], lhsT=yc[:, j * 128:(j + 1) * 128], rhs=identb[:],
                                     start=True, stop=True)
                    return i
                P.op("pe", mmYT, reads=["yc", "identb"], writes=["bk5"])
                P.op(ENG_YCT, (lambda e, tc0=tc0: e.activation(
                    mixT[par][:, 4:8, tc0:tc0 + 128], bk45[:, 512:1024].rearrange("p (j t) -> p j t", j=4), AF.Copy)) if ENG_YCT == "act" else
                    (lambda e, tc0=tc0: e.tensor_copy(
                    mixT[par][:, 4:8, tc0:tc0 + 128], bk45[:, 512:1024].rearrange("p (j t) -> p j t", j=4))),
                    reads=["bk5"], writes=[("mixT", par, 4 + j) for j in range(4)])
                yield
            prev_tail = None
            pc = sg.get("prev_carry")
            for ti, t in enumerate(tiles):
                g = tile_gen(ti, t)
                while True:
                    r_ = next(g)
                    if r_ == "SPLIT":
                        while pc is not None and not pc.done:
                            if pc.step():
                                yield
                        break
                    yield
                    if prev_tail is not None:
                        try:
                            P.scr_tok = "SCRT"
                            next(prev_tail)
                            P.scr_tok = "SCR"
                            yield
                        except StopIteration:
                            P.scr_tok = "SCR"
                            prev_tail = None
                while prev_tail is not None:
                    try:
                        P.scr_tok = "SCRT"
                        next(prev_tail)
                        P.scr_tok = "SCR"
                        yield
                    except StopIteration:
                        P.scr_tok = "SCR"
                        prev_tail = None
                prev_tail = g
            sg["carry"] = prev_tail

        def stream_C(gi, l, sg):
            tiles, slots, c0, n, is_s, par = sg["tiles"], sg["slots"], sg["c0"], sg["n"], sg["sample"], sg["par"]
            g_ = sg["carry"]
            while True:
                try:
                    P.scr_tok = "SCRT"
                    next(g_)
                    P.scr_tok = "SCR"
                    yield
                except StopIteration:
                    P.scr_tok = "SCR"
                    break
            xtok = lambda oc: [("xT", t, oc) for t in slots]
            for oc in range(8):
                pa, pt = PA[oc % 2], "bk%d" % (oc % 2)
                s_ = wslot((gi, l, "wout", oc // 2))

                def mm(e, oc=oc, s_=s_, pa=pa):
                    for k in range(8):
                        i = e.matmul(pa[:, 0:n], lhsT=wview(s_, k, (oc % 2) * 128, 128), rhs=mixT[par][:, k, 0:n],
                                     start=(k == 0), stop=(k == 7))
                    return i
                P.scr_tok = "SCRT"
                P.op("pe", mm, reads=rtok(s_) + [("mixT", par, k) for k in range(8)], writes=[pt])
                P.op("dve", lambda e, oc=oc, pa=pa: e.tensor_tensor(
                    xT[:, oc, c0:c0 + n], xT[:, oc, c0:c0 + n], pa[:, 0:n], ALU.add),
                    reads=[pt] + xtok(oc), writes=xtok(oc))
                P.scr_tok = "SCR"
                yield

        def ffn_norm_steps(l, fs):
            c0, n, slots = fs["c0"], fs["n"], fs["slots"]
            xtok = [("xT", t, k) for t in slots for k in range(8)]
            for _ in rmsnorm_steps(xT[:, :, c0:c0 + n], lambda k: xT[:, k, c0:c0 + n], "g2", l,
                                   sq2, bk45[:, 0:512], "bk4", lnv2, rstd2, lambda k: h2[:, k, c0:c0 + n], n,
                                   xtok, [("h2", c0)], "nF", True):
                yield

        def ffn_make(gi, l, fsgs):
            its = [(blk, fs) for blk in range(4) for fs in fsgs]
            slots_of = {}

            def ffn_up(i, bo=0):
                blk, fs = its[i]
                if blk not in slots_of:
                    slots_of[blk] = [wslot((gi, l, "ffn", blk * 8 + hc)) for hc in range(8)]
                sl = slots_of[blk]
                c0, n = fs["c0"], fs["n"]
                hb_ = i % 2
                for hc in range(8):
                    s_ = sl[hc]
                    hp = bk[bo + hc % 2]
                    hpt = BKT[bo + hc % 2]

                    def mm1(e, s_=s_, hp=hp):
                        for k in range(8):
                            i_ = e.matmul(hp[:, 0:n], lhsT=sub(ring[:], s_ * 2048 + k * 128, [[1, 128]]),
                                          rhs=h2[:, k, c0:c0 + n], start=(k == 0), stop=(k == 7))
                        return i_
                    P.op("pe", mm1, reads=[("ring", s_), ("h2", c0)], writes=hpt)
                    P.op("act", lambda e, hp=hp, hc=hc: e.activation(rr[:, hc % 2, 0:n], hp[:, 0:n], AF.Relu),
                         reads=hpt, writes=[("rr", hc % 2)])
                    P.op("dve", lambda e, hc=hc: e.tensor_tensor(
                        hidb[hb_][:, hc, 0:n], rr[:, hc % 2, 0:n], rr[:, hc % 2, 0:n], ALU.mult),
                        reads=[("rr", hc % 2)], writes=[("hid", hb_, hc)])
                    yield

            def ffn_down(i):
                blk, fs = its[i]
                sl = slots_of[blk]
                c0, n, slots = fs["c0"], fs["n"], fs["slots"]
                hb_ = i % 2
                for oc in range(8):
                    op_ = bk[2 + oc % 2]

                    def mm2(e, oc=oc, op_=op_):
                        for hc in range(8):
                            i_ = e.matmul(op_[:, 0:n], lhsT=sub(ring[:], sl[hc] * 2048 + 1024 + oc * 128, [[1, 128]]),
                                          rhs=hidb[hb_][:, hc, 0:n], start=(hc == 0), stop=(hc == 7))
                        return i_
                    P.op("pe", mm2, reads=[("ring2", s_) for s_ in sl] + [("hid", hb_, hc) for hc in range(8)],
                         writes=BKT[2 + oc % 2])
                    xt = [("xT", t, oc) for t in slots]
                    P.op("dve", lambda e, oc=oc, op_=op_: e.tensor_tensor(
                        xT[:, oc, c0:c0 + n], xT[:, oc, c0:c0 + n], op_[:, 0:n], ALU.add),
                        reads=BKT[2 + oc % 2] + xt, writes=xt)
                if fs is fsgs[-1]:
                    for hc in range(8):
                        release((gi, l, "ffn", blk * 8 + hc))
            return dict(its=its, up=ffn_up, down=ffn_down)

        def ffn_phase(gi, l, fsgs, ctx, mid_hook=None, skip=(), upped0=False, tail_hook=None):
            for fs in fsgs:
                if fs in skip:
                    continue
                for _ in ffn_norm_steps(l, fs):
                    pass
            its = ctx["its"]
            if not upped0:
                for _ in ctx["up"](0):
                    pass
            for i in range(len(its)):
                if i + 1 < len(its):
                    for _ in ctx["up"](i + 1):
                        pass
                if tail_hook is not None and i == len(its) - 1:
                    tail_hook()
                ctx["down"](i)
                if mid_hook is not None and i == len(its) // 2:
                    mid_hook()

        def fence():
            o_ = P.op("dve", lambda e: e.memset(dummy[:], 0.0), reads=[], writes=["SCR", "SCRT"], scr=False)
            FENCE_T.append((o_.finish / 1e3, P.busy.get("pe", 0.0) / 1e3))

        def fenceA():
            P.op("dve", lambda e: e.memset(dummy[:], 0.0), reads=[], writes=["SCR"], scr=False)

        pump()
        for gi, (ptiles, has_s) in enumerate(GROUPS):
            npc = 128 * len(ptiles)
            ncol = npc + (128 if has_s else 0)
            nsl = ncol // 128
            tids = list(ptiles) + ([16] if has_s else [])
            P.op("sp", lambda e, ptiles=ptiles, npc=npc: e.dma_start(
                out=xT[:, :, 0:npc],
                in_=xp_d[:, ptiles[0] * 128:ptiles[0] * 128 + npc].rearrange("(k p) n -> p k n", p=128)),
                writes=[("xT", t, k) for t in range(len(ptiles)) for k in range(8)], dma="xi0")
            if has_s:
                P.op("sp", lambda e, npc=npc: e.dma_start(
                    out=xT[:, :, npc:npc + 128], in_=xs_d.rearrange("(k p) n -> p k n", p=128)),
                    writes=[("xT", nsl - 1, k) for k in range(8)], dma="xi1")
            sgs = []
            for i in range(0, len(ptiles), 2):
                tl = ptiles[i:i + 2]
                sgs.append(dict(tiles=tl, slots=list(range(i, i + len(tl))), c0=i * 128, n=128 * len(tl), sample=False))
            if has_s:
                sgs.append(dict(tiles=[16], slots=[nsl - 1], c0=npc, n=128, sample=True))
            fsgs = []
            if FFN_ALIGN and ncol - sgs[-1]["n"] <= NFF and sgs[-1]["n"] >= FFN_ALIGN:
                cuts = [0, ncol - sgs[-1]["n"], ncol]
            else:
                nf = -(-ncol // NFF)
                wf = -(-ncol // nf)
                cuts = list(range(0, ncol, wf)) + [ncol]
            for c, c1 in zip(cuts[:-1], cuts[1:]):
                n = c1 - c
                fsgs.append(dict(c0=c, n=n, slots=list(range(c // 128, (c + n - 1) // 128 + 1))))
            pre_normed = False
            for l in range(NL):
                if l == 0 and gi == 0:
                    layer_prep(0)
                fence()
                if has_s:
                    P.op("sp", lambda e, l=l: e.dma_start(out=hcs, in_=hc_d[l].rearrange("(j p) b k -> p j b k", p=128)),
                         writes=["hcs"], dma="h0", scr=True)
                    P.op("sp", lambda e, l=l: e.dma_start(out=hxs, in_=hx_d[l].rearrange("(c p) b k -> p c b k", p=128)),
                         writes=["hxs"], dma="h1", scr=True)
                for si, sg in enumerate(sgs):
                    sg["par"] = si % 2
                def run_streams(streams, gated=None, gate=None):
                    live = [s_ for s_ in streams if s_ is not None and not s_.done]
                    while live or (gated is not None and not gated.done):
                        if gated is not None and (gate is None or gate.done) and gated not in live and not gated.done:
                            live.append(gated)
                        if not live:
                            break
                        live.sort(key=lambda x: x.t - x.prio)
                        st_ = live[0]
                        P.step_finish = 0.0
                        if st_.step():
                            st_.t = max(st_.t, P.step_finish)
                        live = [s_ for s_ in live if not s_.done]
                def rel_win():
                    for i in range(11):
                        release((gi, l, "win", i))
                run_streams([Strm(stream_A(gi, l, sgs[0], skip_norm=pre_normed))])
                pre_normed = False
                if len(sgs) == 1 and EARLY_REL:
                    rel_win()
                carry = None
                for si, sg in enumerate(sgs):
                    nxt = Strm(stream_A(gi, l, sgs[si + 1])) if si + 1 < len(sgs) else None
                    sg["prev_carry"] = carry
                    if carry is not None:
                        P.step_finish = 0.0
                        carry.step()
                        carry.t = P.step_finish
                    run_streams([carry, Strm(stream_B(gi, l, sg), PRIO_B)], gated=nxt, gate=carry)
                    carry = Strm(stream_C(gi, l, sg), PRIO_C)
                    if si == len(sgs) - 2 and EARLY_REL:
                        rel_win()
                fenceA()
                last_slots = set(sgs[-1]["slots"])
                pre = [fs for fs in fsgs if not (set(fs["slots"]) & last_slots)]
                fctx = ffn_make(gi, l, fsgs)
                upped0 = PRE_UP and bool(pre) and (fsgs[0] in pre)

                def pre_ffn(pre=pre, fctx=fctx, upped0=upped0):
                    for fs in pre:
                        for _ in ffn_norm_steps(l, fs):
                            yield
                    if upped0:
                        for _ in fctx["up"](0, PRE_BANK):
                            yield
                run_streams([carry, Strm(pre_ffn())])
                if not EARLY_REL:
                    rel_win()
                for i in range(4):
                    release((gi, l, "wout", i))
                if DEBUG_STOP:
                    break
                fence()
                nl_ = (l + 1) if l + 1 < NL else (0 if gi + 1 < len(GROUPS) else None)
                th_ = None
                if PRE_A and l + 1 < NL and set(sgs[0]["slots"]) <= set(fsgs[0]["slots"]) and len(fsgs) > 1:
                    def th_(l=l):
                        P.op("dve", lambda e: e.memset(dummy[:], 0.0), reads=[],
                             writes=[("h2", fs["c0"]) for fs in fsgs] + ["hTguard"])
                        for _ in norm1_steps(l + 1, sgs[0], guard=["hTguard"]):
                            pass
                    pre_normed = True
                if nl_ is not None:
                    prep_load(nl_)
                    ffn_phase(gi, l, fsgs, fctx, mid_hook=lambda nl_=nl_: prep_compute(nl_), skip=pre, upped0=upped0, tail_hook=th_)
                else:
                    ffn_phase(gi, l, fsgs, fctx, skip=pre, upped0=upped0, tail_hook=th_)
            blocks = [(c, min(256, npc - c), False) for c in range(0, npc, 256)] + ([(npc, 128, True)] if has_s else [])
            for (c, n, smp) in ([] if DEBUG_STOP else blocks):
                slots = list(range(c // 128, (c + n) // 128))
                xtok = [("xT", t, k) for t in slots for k in range(8)]
                rmsnorm_cols(xT[:, :, c:c + n], lambda k, c=c, n=n: xT[:, k, c:c + n], "gf", 0,
                             sq2, bk45[:, 0:512], "bk4", lnv2, rstd2, lambda k, n=n: yout[:, k, 0:n], n, xtok, ["yout"], "nO")
                if smp:
                    dst = ys_d.rearrange("(k p) n -> p k n", p=128)
                else:
                    t0 = ptiles[0] + c // 128
                    dst = yp_d[:, t0 * 128:t0 * 128 + n].rearrange("(k p) n -> p k n", p=128)
                P.op("sp", lambda e, dst=dst, n=n: e.dma_start(out=dst, in_=yout[:, :, 0:n]), reads=["yout"], dma="yo", scr=True)
        for l in range(NL):
            P.op("sp", lambda e, l=l: e.dma_start(out=ocp_d[l].rearrange("(j p) k -> p j k", p=128), in_=hist_g[:, l, :, :]),
                 reads=[("hist_g", l)], dma="op0")
            P.op("sp", lambda e, l=l: e.dma_start(out=oxp_d[l].rearrange("(c p) k -> p c k", p=128), in_=hist_x[:, l, :, :]),
                 reads=[("hist_x", l)], dma="op1")
        print("ops:", P.nops, "scratch words mixer/ffn:", mixer_words, ffn_words, "model_us: %.0f" % (max(P.eng_free.values()) / 1e3), "busy_us:", {k: int(v / 1e3) for k, v in P.busy.items()}, flush=True)
        P.emit()
    return nc


_NC = None


def kernel(x_prompt, x_sample, state_conv, state_ssm_conv, state_ssm, norm1, w_in, w_s, b_s, conv_w,
           ssm_conv_w, ssm_conv_b, dt_bias, a_log, d_skip, ssm_norm, w_out, norm2, w_ff1, w_ff2, final_norm):
    global _NC
    f = lambda a: np.ascontiguousarray(np.asarray(a, dtype=np.float32))
    x_prompt, x_sample = f(x_prompt), f(x_sample)
    state_conv, state_ssm_conv, state_ssm = f(state_conv), f(state_ssm_conv), f(state_ssm)
    cv = np.zeros((128, NCV), np.float32)

    def put(nm, l, arr):
        base, n = CVL[nm]
        cv[:, base + l * n: base + l * n + n] = arr.reshape(n, 128).T
    for l in range(DEPTH):
        put("g1", l, f(norm1)[l])
        put("g2", l, f(norm2)[l])
        put("cw", l, f(conv_w)[l])
        put("sw", l, f(ssm_conv_w)[l])
        put("sb", l, f(ssm_conv_b)[l])
    base, n = CVL["gf"]
    cv[:, base:base + 8] = f(final_norm).reshape(8, 128).T
    rows = np.concatenate([f(dt_bias), f(a_log), f(d_skip), f(ssm_norm)], axis=1)
    bs = f(b_s)
    brow = np.zeros((DEPTH, 2, 512), np.float32)
    for hh in range(2):
        for j in range(2):
            brow[:, hh, j * 128:(j + 1) * 128] = bs[:, 2 * j + hh, :]
            brow[:, hh, 256 + j * 128:256 + (j + 1) * 128] = np.tile(bs[:, 2 * j + hh, 0:8], (1, 16))
    idx = np.arange(128)
    U_p = (idx[:, None] <= idx[None, :]).astype(np.float32)
    L_p = (idx[:, None] > idx[None, :]).astype(np.float32)
    same = (idx[:, None] // 8 == idx[None, :] // 8).astype(np.float32)
    cmat = np.stack([U_p, L_p, U_p * same, L_p * same, np.eye(128, dtype=np.float32)], axis=1)
    bind = np.zeros((128, 17), np.float32)
    bind[:, 0] = 1.0
    bind[idx, 1 + idx // 8] = 1.0
    sel2 = np.zeros((2, 128), np.float32)
    sel2[0, 0:64] = 1.0
    sel2[1, 64:128] = 1.0
    shared = dict(w_in=f(w_in), w_out=f(w_out), w_ff1=f(w_ff1), w_ff2=f(w_ff2), cv=cv, rows=np.ascontiguousarray(rows),
                  w_s=f(w_s), brow=brow, cmat=np.ascontiguousarray(cmat), bind=bind, sel2=sel2)
    in_maps = []
    for c in range(NCORES):
        sl = slice(16 * c, 16 * c + 16)
        m = dict(shared)
        m["xp"] = np.ascontiguousarray(x_prompt[c].T)
        m["xs"] = np.ascontiguousarray(x_sample[sl].reshape(128, 1024).T)
        m["hc"] = np.ascontiguousarray(state_conv[:, sl].transpose(0, 3, 1, 2))
        m["hx"] = np.ascontiguousarray(state_ssm_conv[:, sl].transpose(0, 3, 1, 2))
        m["ssm"] = np.ascontiguousarray(state_ssm[:, sl].reshape(DEPTH, 16, 512, 128))
        in_maps.append(m)
    if _NC is None:
        _NC = build()
    res = run_bass_kernel_spmd(_NC, in_maps, core_ids=list(range(NCORES)))
    R = res.results
    y_prompt = np.stack([R[c]["yp"].T for c in range(NCORES)])
    y_sample = np.concatenate([R[c]["ys"].T.reshape(16, 8, 1024) for c in range(NCORES)])
    chunk_v_prompt = np.stack([R[c]["cvp"] for c in range(NCORES)], axis=1)
    conv_prompt = np.stack([R[c]["ocp"].transpose(0, 2, 1) for c in range(NCORES)], axis=1)
    ssm_conv_prompt = np.stack([R[c]["oxp"].transpose(0, 2, 1) for c in range(NCORES)], axis=1)
    ssm_prompt = np.stack([R[c]["osp"].reshape(DEPTH, 8, 64, 128) for c in range(NCORES)], axis=1)
    chunk_v_sample = np.concatenate([R[c]["cvs"].reshape(DEPTH, 16, 8, 256) for c in range(NCORES)], axis=1)
    conv_sample = np.concatenate([R[c]["ocs"].transpose(0, 2, 3, 1) for c in range(NCORES)], axis=1)
    ssm_conv_sample = np.concatenate([R[c]["oxs"].transpose(0, 2, 3, 1) for c in range(NCORES)], axis=1)
    ssm_sample = np.concatenate([R[c]["oss"].reshape(DEPTH, 16, 8, 64, 128) for c in range(NCORES)], axis=1)
    outs = (y_prompt, y_sample, chunk_v_prompt, conv_prompt, ssm_conv_prompt, ssm_prompt,
            chunk_v_sample, conv_sample, ssm_conv_sample, ssm_sample)
    return tuple(np.ascontiguousarray(o, dtype=np.float32) for o in outs)
```

```python
from contextlib import ExitStack
from collections import deque
import numpy as np
import concourse.bass as bass
import concourse.mybir as mybir
from concourse.bass_utils import run_bass_kernel_spmd

F32 = mybir.dt.float32
BF16 = mybir.dt.bfloat16
AF = mybir.ActivationFunctionType
ALU = mybir.AluOpType

NCORES = 8
DEPTH = 4
EPS = 1e-5
D_IN = 2824
COMPUTE = ("pe", "act", "dve", "pool")
ENGINES = ("pe", "act", "dve", "pool", "sp")


class Op:
    __slots__ = ("eng", "fn", "deps", "signal", "dma_key", "event", "idx", "finish")

    def __init__(self, eng, fn, dma_key):
        self.eng = eng
        self.fn = fn
        self.deps = []
        self.signal = dma_key is not None
        self.dma_key = dma_key
        self.event = None
        self.idx = -1
        self.finish = 0.0


class _FakeIns:
    def then_inc(self, *a, **k):
        return self


class _FakeEng:
    def __init__(self, kind):
        self.kind = kind
        self.cost = 0.0

    @staticmethod
    def _free(ap):
        n = 1
        for d in ap.shape[1:]:
            n *= int(d)
        return n

    def matmul(self, out, lhsT=None, rhs=None, **kw):
        n = max(self._free(out), 64)
        self.cost += n / 2.0 * (4.0 if lhsT.dtype == F32 else 1.0) + 8.0
        return _FakeIns()

    def dma_start(self, out=None, in_=None, **kw):
        self.cost += 2500.0
        return _FakeIns()

    def __getattr__(self, name):
        def f(*a, **k):
            out = a[0] if a else k.get("out", k.get("ap"))
            fr = self._free(out)
            if self.kind == "act":
                self.cost += 230.0 + 0.83 * fr
            elif self.kind == "pool":
                self.cost += 250.0 + 2.1 * fr
            else:
                self.cost += 110.0 + 1.05 * fr
            return _FakeIns()
        return f


HOP_NS = 120.0
PRIO_B = 0.0
PRE_UP = False
SQ_ON_ACT = True
PRE_A = True
FFN_ALIGN = 0
PRE_BANK = 2
EARLY_REL = True
FENCE_T = []
PRIO_C = 0.0


class Strm:
    def __init__(self, g, prio=0.0):
        self.g = g
        self.done = g is None
        self.t = 0.0
        self.prio = prio

    def step(self):
        if self.done:
            return False
        try:
            next(self.g)
            return True
        except StopIteration:
            self.done = True
            return False

BLAME = None
NORM_ENG = 'dve'


class Prog:
    def __init__(self, nc):
        self.nc = nc
        self.eng_ops = {e: [] for e in ENGINES}
        self.last_writer = {}
        self.readers = {}
        self.dma_keys = []
        self.last_dma = {}
        self.nops = 0
        self.eng_free = {e: 0.0 for e in ENGINES}
        self.scr_tok = "SCR"
        self.step_finish = 0.0
        self.busy = {}
        self.stall = {}

    def op(self, eng, fn, reads=(), writes=(), dma=None, scr=None):
        if scr is None:
            scr = dma is None
        if scr:
            reads = list(reads) + [self.scr_tok]
        o = Op(eng, fn, dma)
        o.idx = self.nops
        self.nops += 1
        if dma is not None and dma not in self.last_dma:
            self.dma_keys.append(dma)
        is_dma = dma is not None
        deps = {}

        def add(d, kind):
            if d is None or d is o:
                return
            if (not is_dma) and d.dma_key is None and d.eng == eng and kind != "raw":
                return
            deps[d.idx] = d

        if is_dma:
            add(self.last_dma.get(dma), "raw")
            self.last_dma[dma] = o
        for t in reads:
            add(self.last_writer.get(t), "raw")
        for t in writes:
            add(self.last_writer.get(t), "waw")
            for r in self.readers.get(t, ()):
                add(r, "war")
        o.deps = list(deps.values())
        for d in o.deps:
            d.signal = True
        fe = _FakeEng(eng)
        try:
            fn(fe)
        except Exception:
            fe.cost = 500.0
        start = self.eng_free[eng]
        blame = None
        for d in o.deps:
            if d.finish + HOP_NS > start:
                start = d.finish + HOP_NS
                blame = d
        if blame is not None and BLAME is not None:
            key = (eng, fn.__code__.co_firstlineno, blame.eng, blame.fn.__code__.co_firstlineno)
            BLAME[key] = BLAME.get(key, 0.0) + (start - self.eng_free[eng])
        if is_dma:
            o.finish = start + fe.cost
            self.eng_free[eng] = start + 60.0
        else:
            o.finish = start + fe.cost
            self.eng_free[eng] = o.finish
        if o.finish > self.step_finish:
            self.step_finish = o.finish
        self.busy[eng] = self.busy.get(eng, 0.0) + fe.cost
        self.stall[eng] = self.stall.get(eng, 0.0) + (start - (self.eng_free[eng] - (fe.cost if not is_dma else 60.0)) if False else 0.0)
        for t in reads:
            self.readers.setdefault(t, []).append(o)
        for t in writes:
            self.last_writer[t] = o
            self.readers[t] = []
        self.eng_ops[eng].append(o)
        return o

    def emit(self):
        nc = self.nc
        with ExitStack() as st:
            esem = {e: st.enter_context(nc.semaphore("s_" + e)) for e in COMPUTE}
            dsem = {k: st.enter_context(nc.semaphore("d%d" % i)) for i, k in enumerate(self.dma_keys)}
            for e in ENGINES:
                cnt = 0
                for o in self.eng_ops[e]:
                    if o.dma_key is None and o.signal:
                        cnt += 1
                        o.event = (esem[e], cnt)
            dcnt = {k: 0 for k in self.dma_keys}
            allops = sorted((o for e in ENGINES for o in self.eng_ops[e]), key=lambda o: o.idx)
            for o in allops:
                if o.dma_key is not None:
                    dcnt[o.dma_key] += 16
                    o.event = (dsem[o.dma_key], dcnt[o.dma_key])
            block = st.enter_context(nc.Block())

            def run(e, handle):
                waited = {}
                for o in self.eng_ops[e]:
                    need = {}
                    for d in o.deps:
                        sem, val = d.event
                        if need.get(id(sem), (None, 0))[1] < val:
                            need[id(sem)] = (sem, val)
                    for k, (sem, val) in need.items():
                        if waited.get(k, 0) < val:
                            handle.wait_ge(sem, val)
                            waited[k] = val
                    ins = o.fn(handle)
                    if o.dma_key is not None:
                        ins.then_inc(o.event[0], 16)
                    elif o.signal:
                        ins.then_inc(o.event[0], 1)
                if e == "sp":
                    for k in self.dma_keys:
                        if dcnt[k] and waited.get(id(dsem[k]), 0) < dcnt[k]:
                            handle.wait_ge(dsem[k], dcnt[k])

            @block.tensor
            def _(h):
                run("pe", h)

            @block.scalar
            def _(h):
                run("act", h)

            @block.vector
            def _(h):
                run("dve", h)

            @block.gpsimd
            def _(h):
                run("pool", h)

            @block.sync
            def _(h):
                run("sp", h)


def sub(ap, off, dims, np_=128, p0=0):
    ps = ap.ap[0][0]
    return bass.AP(ap.tensor, ap.offset + p0 * ps + off, [[ps, np_]] + [list(d) for d in dims])


def _cv_layout():
    lay = {}
    c = 0
    for nm, n in (("g1", 8), ("g2", 8), ("cw", 6), ("sw", 32), ("sb", 8)):
        lay[nm] = (c, n)
        c += n * DEPTH
    lay["gf"] = (c, 8)
    c += 8
    return lay, c


CVL, NCV = _cv_layout()


def cvcol(nm, l, i):
    base, n = CVL[nm]
    return base + l * n + i


NS = 16
NSG = 256
NFF = 512
GROUPS = [(list(range(0, 4)), True), (list(range(4, 10)), False), (list(range(10, 16)), False)]
NCOLMAX = 6 * 128
NL = DEPTH
LAST_TILE = 15
DEBUG_STOP = False
CARVE_DBG = {}


def build():
    nc = bass.Bass("TRN2", target_bir_lowering=False)
    P = Prog(nc)
    din = lambda n, s: nc.dram_tensor(n, s, F32, kind="ExternalInput").ap()
    dout = lambda n, s: nc.dram_tensor(n, s, F32, kind="ExternalOutput").ap()
    xp_d = din("xp", [1024, 128 * (LAST_TILE + 1)])
    xs_d = din("xs", [1024, 128])
    hc_d = din("hc", [DEPTH, 256, 16, 2])
    hx_d = din("hx", [DEPTH, 1024, 16, 3])
    ssm_d = din("ssm", [DEPTH, 16, 512, 128])
    win_d = din("w_in", [DEPTH, 1024, D_IN])
    wout_d = din("w_out", [DEPTH, 1024, 1024])
    wf1_d = din("w_ff1", [DEPTH, 1024, 4096])
    wf2_d = din("w_ff2", [DEPTH, 4096, 1024])
    cv_d = din("cv", [128, NCV])
    rows_d = din("rows", [DEPTH, 536])
    ws_d = din("w_s", [DEPTH, 4, 128, 128])
    brow_d = din("brow", [DEPTH, 2, 512])
    cm_d = din("cmat", [128, 5, 128])
    bind_d = din("bind", [128, 17])
    sel_d = din("sel2", [2, 128])

    yp_d = dout("yp", [1024, 128 * (LAST_TILE + 1)])
    ys_d = dout("ys", [1024, 128])
    cvp_d = dout("cvp", [DEPTH, 128, 256])
    cvs_d = dout("cvs", [DEPTH, 128, 256])
    ocp_d = dout("ocp", [DEPTH, 256, 2])
    oxp_d = dout("oxp", [DEPTH, 1024, 3])
    ocs_d = dout("ocs", [DEPTH, 256, 16, 2])
    oxs_d = dout("oxs", [DEPTH, 1024, 16, 3])
    osp_d = dout("osp", [DEPTH, 512, 128])
    oss_d = dout("oss", [DEPTH, 16, 512, 128])

    with ExitStack() as st:
        SB = lambda n, s, d=F32: st.enter_context(nc.sbuf_tensor(n, s, d))
        PSB = lambda n, s, d=F32: st.enter_context(nc.psum_tensor(n, s, d))
        xT = SB("xT", [128, 8, NCOLMAX])
        ring = SB("ring", [128, NS, 2048], BF16)
        wdt = SB("wdt", [128, DEPTH, 8, 8], BF16)
        cvt = SB("cvt", [128, NCV])
        cmat = SB("cmat_t", [128, 5, 128])
        identb = SB("identb", [128, 128], BF16)
        onesb = SB("onesb", [128, 128], BF16)
        bind = SB("bind_t", [128, 17])
        sel_f = SB("sel_f", [2, 128])
        sel_b = SB("sel_b", [2, 128], BF16)
        S_all = SB("S_all", [128, DEPTH, 4, 128])
        hist_g = SB("hist_g", [128, DEPTH, 2, 2])
        hist_x = SB("hist_x", [128, DEPTH, 8, 3])
        Cmask = SB("Cmask", [128, 2, 16, 128], BF16)
        rowsT = SB("rowsT", [128, 536])
        Arow = SB("Arow", [128, 8])
        wmT = SB("wmT", [128, 2, 4, 128], BF16)
        ws_nat = SB("ws_nat", [128, 4, 128])
        ws_nat1 = SB("ws_nat1", [128, 4, 128])
        brow_f = SB("brow_f", [2, 512])
        diagC = SB("diagC", [128, 4, 8, 128], BF16)
        diagB = SB("diagB", [128, 3, 2, 128], BF16)
        brow_b = SB("brow_b", [2, 512], BF16)

        SCRW = 19 * 1024
        scr = SB("scr", [128, SCRW])
        scr_b = scr.bitcast(BF16)
        dummy = SB("dummy_t", [128, 8])
        cur = [0]

        def carve(shape, dt=F32):
            n = int(np.prod(shape))
            words = n if dt == F32 else (n + 1) // 2
            words = (words + 7) // 8 * 8
            o = cur[0]
            cur[0] += words
            assert cur[0] <= SCRW, ("scratch overflow", cur[0])
            dims = []
            stride = 1
            for s_ in reversed(shape):
                dims.append([stride, s_])
                stride *= s_
            dims.reverse()
            if dt == F32:
                return bass.AP(scr, o, [[SCRW, 128]] + dims)
            return bass.AP(scr_b, 2 * o, [[2 * SCRW, 128]] + dims)

        N = NSG
        sq = carve([8, N], BF16)
        hT = carve([8, N], BF16)
        lnv = carve([N])
        rstd = carve([N])
        cg = carve([N])
        gext = carve([2, N + 2], BF16)
        xe = carve([2, N + 4], BF16)
        bgb = carve([2, N], BF16)
        off_cross = cur[0]
        uT = [carve([2, N], BF16) for _ in range(2)]
        mixT = [carve([8, N], BF16) for _ in range(2)]
        off_xact = cur[0]
        xact = [carve([8, N], BF16) for _ in range(2)]
        vb = [carve([2, 256], BF16) for _ in range(2)]
        zs = [carve([2, 512], BF16) for _ in range(2)]
        dtt = [carve([2, 8]) for _ in range(2)]
        att = [carve([2, 8]) for _ in range(2)]
        vf = carve([256])
        off_bfront = cur[0]
        xdt = carve([512], BF16)
        xdtd = carve([512], BF16)
        Btok = carve([256], BF16)
        xsD = carve([512], BF16)
        rseg = carve([1024])
        decT2 = [carve([1024], BF16) for _ in range(2)]
        cbTm = carve([256], BF16)
        aexp = carve([512])
        off_y1 = cur[0]
        y1 = carve([512])
        yg = carve([512])
        junk = carve([512], BF16)
        yc = carve([512], BF16)
        STb = carve([2, 512], BF16)
        de2 = [carve([8]) for _ in range(2)]
        ec2 = [carve([8]) for _ in range(2)]
        e1 = carve([8])
        dtr = carve([8])
        dechp2 = [carve([4, 16]) for _ in range(2)]
        ss = carve([8])
        lns = carve([8])
        rs = carve([8])
        off_sin = cur[0]
        Sin = carve([4, 3, 128])
        tmpS = carve([2, 128])
        BmaskQ = carve([2, 256], BF16)
        gext_s = carve([2, 16, 10], BF16)
        xe_s = carve([2, 16, 12], BF16)
        ocst = carve([2, 16, 2])
        oxst = carve([8, 16, 3])
        hcs = carve([2, 16, 2])
        hxs = carve([8, 16, 3])
        mixer_words = cur[0]
        for _nm in ("y1", "yg", "STb", "xdt", "xdtd", "cbTm", "xsD", "yc", "Btok", "aexp", "rseg"):
            _a = locals()[_nm]
            CARVE_DBG[_nm] = (int(_a.offset), [list(x) for x in _a.ap], str(_a.dtype))
        cur[0] = 0
        h2 = carve([8, NCOLMAX], BF16)
        assert cur[0] <= off_cross, (cur[0], off_cross)
        rr = carve([2, NFF], BF16)
        assert cur[0] <= off_cross, (cur[0], off_cross)
        yout = carve([8, 256])
        ffn_words = cur[0]
        assert cur[0] <= off_xact, (cur[0], off_xact)
        cur[0] = off_xact
        hidb = [carve([8, NFF], BF16)]
        assert cur[0] <= off_xact + 2 * 8 * N // 2, cur[0]
        cur[0] = off_sin
        hidb.append(carve([8, NFF], BF16))
        assert cur[0] <= mixer_words, (cur[0], mixer_words)
        cur[0] = off_bfront
        sq2 = carve([8, NFF], BF16)
        lnv2 = carve([NFF])
        rstd2 = carve([NFF])
        assert cur[0] <= off_y1, (cur[0], off_y1)

        bk = [PSB("bk%d" % i, [128, 512]) for i in range(4)]
        bk45 = PSB("bk45", [128, 1024])
        bk6 = PSB("bk6", [128, 512])
        bk7 = PSB("bk7", [128, 512])
        BKT = {i: ["bk%d" % i] for i in range(4)}
        B45 = ["bk4", "bk5"]
        PA = [bk[0][:, 0:256], bk[1][:, 0:256]]
        ssq_ps = bk[2][:, 0:256]
        pv_ps = bk[2][:, 256:512]
        pz_ps = bk[2]
        pdt_ps = bk[2][:, 0:8]
        de_ps = bk[3][:, 8:16]
        ec_ps = bk[3][:, 16:24]
        cl_ps = bk[3][:, 32:96]
        sTA_ps = bk[3][:, 256:512]
        bkST = [(bk[2], BKT[2]), (bk[3], BKT[3])]

        cm_U = [cmat[:, 0, :], cmat[:, 2, :]]
        cm_L = [cmat[:, 1, :], cmat[:, 3, :]]
        ident = cmat[:, 4, :]

        P.op("sp", lambda e: e.dma_start(out=cvt[:], in_=cv_d), writes=["cvt"], dma="c0")
        P.op("sp", lambda e: e.dma_start(out=cmat[:], in_=cm_d), writes=["cmat"], dma="c1")
        P.op("sp", lambda e: e.dma_start(out=bind[:], in_=bind_d), writes=["bind"], dma="c2")
        P.op("sp", lambda e: e.dma_start(out=sel_f[:], in_=sel_d), writes=["sel_f"], dma="c3")
        for l in range(DEPTH):
            P.op("pool", lambda e, l=l: e.dma_start(
                out=wdt[:, l, :, :], in_=win_d[l, :, 2816:2824].rearrange("(k p) c -> p k c", p=128)),
                writes=["wdt"], dma="c4")
        P.op("dve", lambda e: e.tensor_copy(identb[:], ident), reads=["cmat"], writes=["identb"])
        P.op("dve", lambda e: e.memset(onesb[:], 1.0), writes=["onesb"])
        P.op("dve", lambda e: e.tensor_copy(sel_b[:], sel_f[:]), reads=["sel_f"], writes=["sel_b"])
        P.op("dve", lambda e: e.memset(S_all[:], 0.0), writes=[("S", l) for l in range(DEPTH)])
        P.op("dve", lambda e: e.memset(hist_g[:], 0.0), writes=[("hist_g", l) for l in range(DEPTH)])
        P.op("dve", lambda e: e.memset(hist_x[:], 0.0), writes=[("hist_x", l) for l in range(DEPTH)])
        P.op("dve", lambda e: e.memset(Cmask[:], 0.0), writes=["Cmask"])
        P.op("dve", lambda e: e.memset(ws_nat1[:], 0.0), writes=["ws_nat1z"])

        free_slots = deque(range(NS))
        pending = deque()
        loc = {}
        for gi in range(len(GROUPS)):
            for l in range(NL):
                for i in range(11):
                    pending.append((gi, l, "win", i))
                for i in range(4):
                    pending.append((gi, l, "wout", i))
                for j in range(32):
                    pending.append((gi, l, "ffn", j))

        def rtok(s_):
            return [("ring", s_), ("ring2", s_)]

        def issue(ld, s_):
            gi, l, kind, i = ld
            if kind in ("win", "wout"):
                src = (win_d if kind == "win" else wout_d)[l, :, 256 * i:256 * i + 256]
                dst = sub(ring[:], s_ * 2048, [[256, 8], [1, 256]])
                P.op("pool", lambda e: e.dma_start(out=dst, in_=src.rearrange("(k p) c -> p k c", p=128)),
                     writes=rtok(s_), dma=("w", s_, 0))
            else:
                src1 = wf1_d[l, :, 128 * i:128 * i + 128].rearrange("(k p) c -> p k c", p=128)
                dst1 = sub(ring[:], s_ * 2048, [[128, 8], [1, 128]])
                P.op("pool", lambda e: e.dma_start(out=dst1, in_=src1), writes=[("ring", s_)], dma=("w", s_, 0))
                src2 = wf2_d[l, 128 * i:128 * i + 128, :]
                dst2 = sub(ring[:], s_ * 2048 + 1024, [[1, 1024]])
                P.op("pool", lambda e: e.dma_start(out=dst2, in_=src2), writes=[("ring2", s_)], dma=("w", s_, 1))

        def pump():
            while free_slots and pending:
                ld = pending.popleft()
                s_ = free_slots.popleft()
                issue(ld, s_)
                loc[ld] = s_

        def wslot(ld):
            if ld not in loc:
                pump()
            assert ld in loc, ("ring too small for", ld)
            return loc[ld]

        def release(ld):
            free_slots.append(loc.pop(ld))
            pump()

        def wview(s_, k, c0, n):
            return sub(ring[:], s_ * 2048 + k * 256 + c0, [[1, n]])

        def rmsnorm_steps(xall, xk, gname, l, sqb, ssqp, pstok, lnvb, rstdb, out_fn, n, xtok, otok, tagp, fine=False):
            if not fine:
                P.op("act", lambda e: e.activation(sqb[:, :, 0:n], xall, AF.Square), reads=xtok, writes=[(tagp, "sq")])
                yield

                def mm(e):
                    for k in range(8):
                        i = e.matmul(ssqp[:, 0:n], lhsT=onesb[:], rhs=sqb[:, k, 0:n], start=(k == 0), stop=(k == 7))
                    return i
                P.op("pe", mm, reads=[(tagp, "sq"), "onesb"], writes=[pstok])
            else:
                for q in range(4):
                    P.op("act", lambda e, q=q: e.activation(sqb[:, 2 * q:2 * q + 2, 0:n], xall[:, 2 * q:2 * q + 2, :], AF.Square),
                         reads=xtok, writes=[(tagp, "sq", q)])
                yield
                for q in range(4):
                    def mmq(e, q=q):
                        for k in (2 * q, 2 * q + 1):
                            i = e.matmul(ssqp[:, 0:n], lhsT=onesb[:], rhs=sqb[:, k, 0:n], start=(k == 0), stop=(k == 7))
                        return i
                    P.op("pe", mmq, reads=[(tagp, "sq", q), "onesb"], writes=[pstok])
            P.op("act", lambda e: e.activation(lnvb[:, 0:n], ssqp[:, 0:n], AF.Ln, bias=EPS, scale=1.0 / 1024),
                 reads=[pstok], writes=[(tagp, "lnv")])
            P.op("act", lambda e: e.activation(rstdb[:, 0:n], lnvb[:, 0:n], AF.Exp, scale=-0.5),
                 reads=[(tagp, "lnv")], writes=[(tagp, "rstd")])
            yield
            for k in range(8):
                gcol = cvt[:, cvcol(gname, l, k):cvcol(gname, l, k) + 1]
                P.op(NORM_ENG if tagp == "nA" else "dve", lambda e, k=k, gcol=gcol: e.scalar_tensor_tensor(
                    out_fn(k), xk(k), gcol, rstdb[:, 0:n], ALU.mult, ALU.mult),
                    reads=xtok + [(tagp, "rstd"), "cvt"], writes=(otok + [("hTk", k)]) if fine else otok)

        def rmsnorm_cols(*a):
            for _ in rmsnorm_steps(*a):
                pass

        def b16(ap, n=16):
            return ap.rearrange("p (b t) -> p b t", b=n)

        def prep_load(l):
            P.op("sp", lambda e: e.dma_start(out=rowsT[:], in_=rows_d[l:l + 1, :].partition_broadcast(128)),
                 writes=["rowsT"], dma="r0")
            P.op("sp", lambda e: e.dma_start(out=brow_f[:], in_=brow_d[l]), writes=["brow_f"], dma="r1")
            P.op("sp", lambda e: e.dma_start(out=ws_nat[:], in_=ws_d[l].rearrange("h t s -> t h s")),
                 writes=["ws_nat"], dma="r2")
            for b in range(16):
                P.op("sp", lambda e, b=b: e.dma_start(
                    out=ws_nat1[8 * b:8 * b + 8, :, 8 * b:8 * b + 8],
                    in_=ws_d[l, :, 0:8, 0:8].rearrange("h t s -> t h s")),
                    reads=["ws_nat1z"], writes=[("ws_blk", b)], dma=("r3", b % 4))

        def prep_compute(l):
            P.op("act", lambda e: e.activation(Arow[:], rowsT[:, 8:16], AF.Exp), reads=["rowsT"], writes=["Arow"])
            P.op("dve", lambda e: e.tensor_scalar(Arow[:], Arow[:], -1.0, None, ALU.mult), reads=["Arow"], writes=["Arow"])
            P.op("dve", lambda e: e.tensor_copy(brow_b[:], brow_f[:]), reads=["brow_f"], writes=["brow_b"])
            for k in range(4):
                c0_ = cvcol("sw", l, k * 8)
                P.op("dve", lambda e, k=k, c0_=c0_: e.tensor_tensor(
                    diagC[:, k, :, :], sub(ident, 0, [[0, 8], [1, 128]]), sub(cvt[:, c0_:c0_ + 1], 0, [[1, 8], [0, 128]]), ALU.mult),
                    reads=["cmat", "cvt"], writes=["diagC"])
            for k in range(3):
                c0_ = cvcol("cw", l, k * 2)
                P.op("dve", lambda e, k=k, c0_=c0_: e.tensor_tensor(
                    diagB[:, k, :, :], sub(ident, 0, [[0, 2], [1, 128]]), sub(cvt[:, c0_:c0_ + 1], 0, [[1, 2], [0, 128]]), ALU.mult),
                    reads=["cmat", "cvt"], writes=["diagB"])
            for v in range(2):
                src_t = ws_nat if v == 0 else ws_nat1
                rd = ["ws_nat", "cmat"] if v == 0 else ["ws_nat1z", "cmat"] + [("ws_blk", b) for b in range(16)]

                bkt_ = bk7 if v == 0 else bk6
                bkn_ = "bk7" if v == 0 else "bk6"

                def tr(e, src_t=src_t, bkt_=bkt_):
                    for h in range(4):
                        i = e.matmul(bkt_[:, h * 128:(h + 1) * 128], lhsT=src_t[:, h, :], rhs=ident, start=True, stop=True)
                    return i
                P.op("pe", tr, reads=rd, writes=[bkn_])
                P.op("dve", lambda e, v=v, bkt_=bkt_: e.tensor_tensor(
                    wmT[:, v, :, :], bkt_[:].rearrange("p (h t) -> p h t", h=4),
                    sub(cm_U[v], 0, [[0, 4], [1, 128]]), ALU.mult),
                    reads=[bkn_, "cmat"], writes=[("wmT", v)])

        def layer_prep(l):
            prep_load(l)
            prep_compute(l)

        def norm1_steps(l, sg, guard=()):
            slots, c0, n = sg["slots"], sg["c0"], sg["n"]
            xtok = [("xT", t, k) for t in slots for k in range(8)] + list(guard)
            for _ in rmsnorm_steps(xT[:, :, c0:c0 + n], lambda k: xT[:, k, c0:c0 + n], "g1", l,
                                   sq, ssq_ps, "bk2", lnv, rstd, lambda k: hT[:, k, 0:n], n, xtok, ["hT"], "nA", fine=True):
                yield

        def stream_A(gi, l, sg, skip_norm=False):
            tiles, slots, c0, n, is_s, par = sg["tiles"], sg["slots"], sg["c0"], sg["n"], sg["sample"], sg["par"]
            xtok = [("xT", t, k) for t in slots for k in range(8)]
            for _ in ([] if skip_norm else [0]):
              for _ in rmsnorm_steps(xT[:, :, c0:c0 + n], lambda k: xT[:, k, c0:c0 + n], "g1", l,
                                     sq, ssq_ps, "bk2", lnv, rstd, lambda k: hT[:, k, 0:n], n, xtok, ["hT"], "nA", fine=True):
                  yield
            yield
            wl = lambda i: wslot((gi, l, "win", i))

            def proj(load, half, pa):
                s_ = wl(load)

                def mm(e):
                    for k in range(8):
                        i = e.matmul(pa[:, 0:n], lhsT=wview(s_, k, half * 128, 128), rhs=hT[:, k, 0:n],
                                     start=(k == 0), stop=(k == 7))
                    return i
                return mm, rtok(s_) + ["hT"]
            pi = [0]

            def nextpa():
                pi[0] ^= 1
                return PA[pi[0]], "bk%d" % pi[0]
            for ti in range(len(tiles)):
                if ti == 0:
                    for k in range(8):
                        P.op("pe", lambda e, k=k: e.matmul(pdt_ps, lhsT=hT[:, k, 0:128], rhs=wdt[:, l, k, :],
                                                          start=(k == 0), stop=(k == 7)),
                             reads=[("hTk", k), "wdt"], writes=["bk2"])
                else:
                    def mmd(e, ti=ti):
                        for k in range(8):
                            i = e.matmul(pdt_ps, lhsT=hT[:, k, ti * 128:(ti + 1) * 128], rhs=wdt[:, l, k, :],
                                         start=(k == 0), stop=(k == 7))
                        return i
                    P.op("pe", mmd, reads=["hT", "wdt"], writes=["bk2"])
                P.op("dve", lambda e: e.tensor_tensor(dtr, pdt_ps, rowsT[:, 0:8], ALU.add),
                     reads=["bk2", "rowsT"], writes=["dtr"])
                P.op("act", lambda e: e.activation(e1, dtr, AF.Exp), reads=["dtr"], writes=["e1"])
                P.op("act", lambda e, ti=ti: e.activation(dtt[par][:, ti, :], e1, AF.Ln, bias=1.0), reads=["e1"],
                     writes=[("dtt", par, ti)])
                P.op("dve", lambda e, ti=ti: e.tensor_tensor(att[par][:, ti, :], dtt[par][:, ti, :], Arow[:], ALU.mult),
                     reads=[("dtt", par, ti), "Arow"], writes=[("att", par, ti)])
            yield
            for j in range(2):
                pa, pt = nextpa()
                mm, rd = proj(0, j, pa)
                P.op("pe", mm, reads=rd, writes=[pt])
                P.op("act", lambda e, j=j, pa=pa: e.activation(uT[par][:, j, 0:n], pa[:, 0:n], AF.Gelu),
                     reads=[pt], writes=[("uT", par)])
            yield
            sv = wl(1)
            for ti, t in enumerate(tiles):
                def mmv(e, ti=ti):
                    for k in range(8):
                        i = e.matmul(pv_ps, lhsT=hT[:, k, ti * 128:(ti + 1) * 128], rhs=wview(sv, k, 0, 256),
                                     start=(k == 0), stop=(k == 7))
                    return i
                P.op("pe", mmv, reads=["hT"] + rtok(sv), writes=["bk2"])
                P.op("act", lambda e, ti=ti: e.activation(vb[par][:, ti, :], pv_ps, AF.Gelu),
                     reads=["bk2"], writes=[("vb", par, ti)])
                if is_s or t == LAST_TILE:
                    P.op("act", lambda e: e.activation(vf, pv_ps, AF.Gelu), reads=["bk2"], writes=["vf"])
                    dst = cvs_d[l] if is_s else cvp_d[l]
                    P.op("sp", lambda e, dst=dst: e.dma_start(out=dst, in_=vf), reads=["vf"], dma="ov", scr=True)
            yield
            cps = bk[2]
            for j in range(2):
                pa, pt = nextpa()
                mm, rd = proj(2, j, pa)
                P.op("pe", mm, reads=rd, writes=[pt])
                P.op("act", lambda e, pa=pa, j=j: e.activation(bgb[:, j, 0:n], pa[:, 0:n], AF.Copy),
                     reads=[pt], writes=[("bgb", j)])
            for j in range(2):
                pa, pt = nextpa()
                mm, rd = proj(3, j, pa)
                P.op("pe", mm, reads=rd, writes=[pt])
                P.op("act", lambda e, pa=pa: e.activation(cg[:, 0:n], pa[:, 0:n], AF.Copy), reads=[pt], writes=["cg"])
                pa, pt = nextpa()
                mm, rd = proj(4, j, pa)
                P.op("pe", mm, reads=rd, writes=[pt])
                if not is_s:
                    gx = gext[:, j, :]
                    gt = ("gext", j)
                    taps = [gx[:, k:k + n] for k in range(3)]
                    cout = cps[:, 0:n]
                    P.op("dve", lambda e, gx=gx, j=j: e.tensor_copy(gx[:, 0:2], hist_g[:, l, j, :]),
                         reads=[("hist_g", l)], writes=[gt])
                    P.op("dve", lambda e, gx=gx, pa=pa: e.tensor_tensor(gx[:, 2:2 + n], pa[:, 0:n], cg[:, 0:n], ALU.mult),
                         reads=[pt, "cg", gt], writes=[gt])
                    P.op("dve", lambda e, pa=pa, j=j: e.tensor_tensor(hist_g[:, l, j, :], pa[:, n - 2:n], cg[:, n - 2:n], ALU.mult),
                         reads=[pt, "cg", gt], writes=[("hist_g", l)])
                else:
                    gx = gext_s[:, j, :, :]
                    gt = ("gext_s", j)
                    taps = [gx[:, :, k:k + 8] for k in range(3)]
                    cout = b16(cps[:, 0:128])
                    P.op("dve", lambda e, gx=gx, j=j: e.tensor_copy(gx[:, :, 0:2], hcs[:, j, :, :]),
                         reads=["hcs"], writes=[gt])
                    P.op("dve", lambda e, gx=gx, pa=pa: e.tensor_tensor(gx[:, :, 2:10], b16(pa[:, 0:128]), b16(cg[:, 0:128]), ALU.mult),
                         reads=[pt, "cg", gt], writes=[gt])
                    P.op("dve", lambda e, pa=pa, j=j: e.tensor_tensor(
                        ocst[:, j, :, :], b16(pa[:, 0:128])[:, :, 6:8], b16(cg[:, 0:128])[:, :, 6:8], ALU.mult),
                        reads=[pt, "cg"], writes=["ocst"])

                def mmc(e, j=j, taps=taps, cout=cout):
                    for k in range(3):
                        i = e.matmul(cout, lhsT=diagB[:, k, j, :], rhs=taps[k], start=(k == 0), stop=(k == 2))
                    return i
                P.op("pe", mmc, reads=[gt, "diagB"], writes=["bk2"])
                P.op("dve", lambda e, j=j: e.tensor_tensor(mixT[par][:, 2 + j, 0:n], cps[:, 0:n], bgb[:, j, 0:n], ALU.mult),
                     reads=["bk2", ("bgb", j)], writes=[("mixT", par, 2 + j)])
                yield
            if is_s:
                P.op("sp", lambda e: e.dma_start(out=ocs_d[l].rearrange("(j p) b k -> p j b k", p=128), in_=ocst),
                     reads=["ocst"], dma="oc", scr=True)
            sz = [wl(5), wl(6)]
            for ti in range(len(tiles)):
                def mmz(e, ti=ti):
                    for hf in range(2):
                        for k in range(8):
                            i = e.matmul(pz_ps[:, hf * 256:(hf + 1) * 256], lhsT=hT[:, k, ti * 128:(ti + 1) * 128],
                                         rhs=wview(sz[hf], k, 0, 256), start=(k == 0), stop=(k == 7))
                    return i
                P.op("pe", mmz, reads=["hT"] + rtok(sz[0]) + rtok(sz[1]), writes=["bk2"])
                P.op("act", lambda e, ti=ti: e.activation(zs[par][:, ti, :], pz_ps[:], AF.Silu),
                     reads=["bk2"], writes=[("zs", par, ti)])
            yield
            def xbc_front(c):
                pa, pt = nextpa()
                mm, rd = proj(7 + c // 2, c % 2, pa)
                P.op("pe", mm, reads=rd, writes=[pt])
                r = c % 2
                if not is_s:
                    xx = xe[:, r, :]
                    xt_ = ("xe", r)
                    taps = [xx[:, k:k + n] for k in range(4)]
                    cout = cps[:, 0:n]
                    P.op("dve", lambda e, xx=xx, c=c: e.tensor_copy(xx[:, 0:3], hist_x[:, l, c, :]),
                         reads=[("hist_x", l)], writes=[xt_])
                    P.op("act", lambda e, xx=xx, pa=pa: e.activation(xx[:, 3:3 + n], pa[:, 0:n], AF.Copy),
                         reads=[pt, xt_], writes=[xt_])
                    P.op("act", lambda e, pa=pa, c=c: e.activation(hist_x[:, l, c, :], pa[:, n - 3:n], AF.Copy),
                         reads=[pt, xt_], writes=[("hist_x", l)])
                else:
                    xx = xe_s[:, r, :, :]
                    xt_ = ("xe_s", r)
                    taps = [xx[:, :, k:k + 8] for k in range(4)]
                    cout = b16(cps[:, 0:128])
                    P.op("dve", lambda e, xx=xx, c=c: e.tensor_copy(xx[:, :, 0:3], hxs[:, c, :, :]),
                         reads=["hxs"], writes=[xt_])
                    P.op("act", lambda e, xx=xx, pa=pa: e.activation(xx[:, :, 3:11], b16(pa[:, 0:128]), AF.Copy),
                         reads=[pt, xt_], writes=[xt_])
                    P.op("act", lambda e, pa=pa, c=c: e.activation(oxst[:, c, :, :], b16(pa[:, 0:128])[:, :, 5:8], AF.Copy),
                         reads=[pt], writes=["oxst"])
                return xt_, taps, cout

            def xbc_back(c, xt_, taps, cout):
                bcol = cvt[:, cvcol("sb", l, c):cvcol("sb", l, c) + 1]

                def mmc(e):
                    for k in range(4):
                        i = e.matmul(cout, lhsT=diagC[:, k, c, :], rhs=taps[k], start=(k == 0), stop=(k == 3))
                    return i
                P.op("pe", mmc, reads=[xt_, "diagC"], writes=["bk2"])
                P.op("act", lambda e: e.activation(xact[par][:, c, 0:n], cps[:, 0:n], AF.Silu, bias=bcol),
                     reads=["bk2", "cvt"], writes=[("xact", par, c)])
            pend = xbc_front(0)
            for c in range(8):
                nxt_ = xbc_front(c + 1) if c + 1 < 8 else None
                xbc_back(c, *pend)
                pend = nxt_
                yield
            if is_s:
                P.op("sp", lambda e: e.dma_start(out=oxs_d[l].rearrange("(c p) b k -> p c b k", p=128), in_=oxst),
                     reads=["oxst"], dma="ox", scr=True)

        def stream_B(gi, l, sg):
            tiles, slots, c0, n, is_s, par = sg["tiles"], sg["slots"], sg["c0"], sg["n"], sg["sample"], sg["par"]
            v = 1 if is_s else 0
            U, L = cm_U[v], cm_L[v]
            NB = 16 if is_s else 1
            bcol0 = 1 if is_s else 0
            XA = xact[par]
            h8 = lambda a: a.rearrange("p (h q) -> p h q", h=8)
            for ti, t in enumerate(tiles):
                a_ = att[par][:, ti, :]
                de, ec, dechp, decT = de2[ti], ec2[ti], dechp2[ti], decT2[ti]
                P.op("dve", lambda e, a_=a_: e.tensor_tensor(
                    rseg.rearrange("p (h t) -> p h t", h=8), sub(U, 0, [[0, 8], [1, 128]]),
                    sub(a_, 0, [[1, 8], [0, 128]]), ALU.mult),
                    reads=["cmat", ("att", par, ti)], writes=["rseg"])
                P.op("dve", lambda e, a_=a_: e.tensor_copy(h8(aexp), sub(a_, 0, [[1, 8], [0, 64]])),
                     reads=[("att", par, ti)], writes=["aexp"])
                yield

                def mmS(e, a_=a_):
                    e.matmul(de_ps, lhsT=L, rhs=a_, start=True, stop=True)
                    e.matmul(ec_ps, lhsT=U, rhs=a_, start=True, stop=True)
                    for j in range(4):
                        i = e.matmul(cl_ps[:, j * 16:j * 16 + NB], lhsT=aexp[:, j * 128:(j + 1) * 128],
                                     rhs=bind[:, bcol0:bcol0 + NB], start=True, stop=True)
                    return i
                P.op("pe", mmS, reads=["cmat", ("att", par, ti), "aexp", "bind"], writes=["bk3"])
                P.op("act", lambda e, de=de: e.activation(de, de_ps, AF.Exp), reads=["bk3"], writes=[("de", ti)])
                P.op("act", lambda e, ec=ec: e.activation(ec, ec_ps, AF.Exp), reads=["bk3"], writes=[("ec", ti)])
                for j in range(4):
                    P.op("act", lambda e, j=j, dechp=dechp: e.activation(dechp[:, j, 0:NB], cl_ps[:, j * 16:j * 16 + NB], AF.Exp),
                         reads=["bk3"], writes=[("dechp", ti)])

                def mmSeg(e):
                    e.matmul(bk45[:, 0:512], lhsT=L, rhs=rseg[:, 0:512], start=True, stop=True)
                    return e.matmul(bk45[:, 512:1024], lhsT=L, rhs=rseg[:, 512:1024], start=True, stop=True)
                P.op("pe", mmSeg, reads=["cmat", "rseg"], writes=B45)
                P.op("act", lambda e, decT=decT: e.activation(decT, bk45[:], AF.Exp), reads=B45, writes=[("decT", ti)])
                yield
            def tile_gen(ti, t):
                tc0 = ti * 128
                de, ec, dechp, decT = de2[ti], ec2[ti], dechp2[ti], decT2[ti]
                def mmA(e, ti=ti):
                    for j in range(2):
                        e.matmul(sTA_ps[:, j * 128:(j + 1) * 128], lhsT=sel_b[:],
                                 rhs=brow_b[:, v * 256 + j * 128: v * 256 + (j + 1) * 128], start=True, stop=False)
                        for hh in range(2):
                            h = 2 * j + hh
                            i = e.matmul(sTA_ps[64 * hh:64 * hh + 64, j * 128:(j + 1) * 128],
                                         lhsT=vb[par][:, ti, h * 64:(h + 1) * 64], rhs=wmT[:, v, h, :],
                                         start=False, stop=True, tile_position=(0, 64 * hh))
                    return i
                P.op("pe", mmA, reads=["sel_b", "brow_b", ("vb", par, ti), ("wmT", v)], writes=["bk3"])
                P.op("dve", lambda e, tc0=tc0: e.tensor_tensor(
                    mixT[par][:, 0:2, tc0:tc0 + 128], sTA_ps.rearrange("p (j t) -> p j t", j=2),
                    uT[par][:, :, tc0:tc0 + 128], ALU.mult),
                    reads=["bk3", ("uT", par)], writes=[("mixT", par, 0), ("mixT", par, 1)])
                yield
                dt_ = dtt[par][:, ti, :]
                a_ = att[par][:, ti, :]
                def mmT(e, tc0=tc0):
                    for c in range(4):
                        e.matmul(bk45[:, c * 128:(c + 1) * 128], lhsT=XA[:, c, tc0:tc0 + 128], rhs=identb[:],
                                 start=True, stop=True)
                    for c in range(2):
                        i = e.matmul(bk[3][:, c * 128:(c + 1) * 128], lhsT=XA[:, 4 + c, tc0:tc0 + 128], rhs=identb[:],
                                     start=True, stop=True)
                    return i
                P.op("pe", mmT, reads=[("xact", par, c) for c in range(6)] + ["identb"], writes=["bk4", "bk3"])
                tp3 = h8(bk45[:, 0:512])
                P.op("dve", lambda e, dt_=dt_: e.tensor_tensor(h8(xdt), tp3, sub(dt_, 0, [[1, 8], [0, 64]]), ALU.mult),
                     reads=["bk4", ("dtt", par, ti)], writes=["xdt"])
                P.op("dve", lambda e: e.tensor_tensor(h8(xsD), tp3, sub(rowsT[:, 16:24], 0, [[1, 8], [0, 64]]), ALU.mult),
                     reads=["bk4", "rowsT"], writes=["xsD"])
                P.op("act", lambda e: e.activation(Btok, bk[3][:, 0:256], AF.Copy), reads=["bk3"], writes=["Btok"])
                P.op("dve", lambda e, de=de: e.tensor_tensor(h8(xdtd), h8(xdt), sub(de, 0, [[1, 8], [0, 64]]), ALU.mult),
                     reads=["xdt", ("de", ti)], writes=["xdtd"])
                yield
                def mmCB(e, tc0=tc0):
                    for g in range(2):
                        i = e.matmul(bk7[:, g * 128:(g + 1) * 128], lhsT=XA[:, 4 + g, tc0:tc0 + 128],
                                     rhs=XA[:, 6 + g, tc0:tc0 + 128], start=True, stop=True)
                    return i
                P.op("pe", mmCB, reads=[("xact", par, c) for c in range(4, 8)], writes=["bk7"])
                P.op("dve", lambda e: e.tensor_tensor(
                    cbTm.rearrange("p (g t) -> p g t", g=2), bk7[:, 0:256].rearrange("p (g t) -> p g t", g=2),
                    sub(U, 0, [[0, 2], [1, 128]]), ALU.mult), reads=["bk7", "cmat"], writes=["cbTm"])
                d4 = decT.rearrange("p (g k t) -> p g k t", g=2, k=4)
                P.op("dve", lambda e, d4=d4: e.tensor_tensor(d4, d4, sub(cbTm, 0, [[128, 2], [0, 4], [1, 128]]), ALU.mult),
                     reads=[("decT", ti), "cbTm"], writes=[("decT", ti)])
                yield

                def mmYD(e, decT=decT):
                    e.matmul(bk6[:, 0:512], lhsT=identb[:], rhs=xsD, start=True, stop=False)
                    for h in range(8):
                        i = e.matmul(bk6[:, h * 64:(h + 1) * 64], lhsT=decT[:, h * 128:(h + 1) * 128],
                                     rhs=xdt[:, h * 64:(h + 1) * 64], start=False, stop=(h == 7))
                    return i
                P.op("pe", mmYD, reads=[("decT", ti), "xdt", "xsD", "identb"], writes=["bk6"])
                if not is_s:
                    def mmST(e):
                        for j in range(4):
                            i = e.matmul(bk7[:, j * 128:(j + 1) * 128], lhsT=S_all[:, l, j, :], rhs=ident,
                                         start=True, stop=True)
                        return i
                    P.op("pe", mmST, reads=[("S", l), "cmat"], writes=["bk7"])
                    P.op("act", lambda e: e.activation(STb[:, 0, :], bk7[:], AF.Copy), reads=["bk7"], writes=[("STb", 0)])
                    yield

                    def mmYO(e, tc0=tc0):
                        for g in range(2):
                            i = e.matmul(bk7[:, g * 256:(g + 1) * 256], lhsT=XA[:, 6 + g, tc0:tc0 + 128],
                                         rhs=STb[:, 0, g * 256:(g + 1) * 256], start=True, stop=True)
                        return i
                    P.op("pe", mmYO, reads=[("xact", par, 6), ("xact", par, 7), ("STb", 0)], writes=["bk7"])
                    yield

                    def mmSt(e):
                        for j in range(4):
                            g = j // 2
                            i = e.matmul(bk45[:, j * 128:(j + 1) * 128], lhsT=xdtd[:, j * 128:(j + 1) * 128],
                                         rhs=Btok[:, g * 128:(g + 1) * 128], start=True, stop=True)
                        return i
                    P.op("pe", mmSt, reads=["xdtd", "Btok"], writes=["bk4"])
                    for j in range(4):
                        P.op("dve", lambda e, j=j, dechp=dechp: e.scalar_tensor_tensor(
                            S_all[:, l, j, :], S_all[:, l, j, :], dechp[:, j, 0:1], bk45[:, j * 128:(j + 1) * 128],
                            ALU.mult, ALU.add), reads=[("S", l), ("dechp", ti), "bk4"], writes=[("S", l)])
                    if t == LAST_TILE:
                        P.op("sp", lambda e: e.dma_start(out=osp_d[l].rearrange("(j p) n -> p j n", p=128),
                                                         in_=S_all[:, l, :, :]), reads=[("S", l)], dma="os")
                else:
                    P.op("dve", lambda e: e.tensor_copy(
                        sub(Cmask[:], 0, [[2048, 2], [136, 16], [1, 8]]),
                        sub(XA[:, 6, 0:1], 0, [[NSG, 2], [8, 16], [1, 8]])),
                        reads=[("xact", par, 6), ("xact", par, 7)], writes=["Cmask"])
                    def load_state(b):
                        P.op("sp", lambda e, b=b: e.dma_start(
                            out=Sin[:, :, b % 3, :], in_=ssm_d[l, b].rearrange("(j p) n -> p j n", p=128)),
                            writes=[("Sin", b % 3)], dma=("si", b % 3), scr=True)
                    load_state(0)
                    load_state(1)
                    for b in range(16):
                        r = b % 2
                        bkr, bkt = bkST[r]
                        r3 = b % 3
                        if b + 2 < 16:
                            load_state(b + 2)

                        def mmST(e, r3=r3, bkr=bkr):
                            for j in range(4):
                                i = e.matmul(bkr[:, j * 128:(j + 1) * 128], lhsT=Sin[:, j, r3, :], rhs=ident,
                                             start=True, stop=True)
                            return i
                        P.op("pe", mmST, reads=[("Sin", r3), "cmat"], writes=bkt)
                        P.op("act", lambda e, r=r, bkr=bkr: e.activation(STb[:, r, :], bkr[:], AF.Copy),
                             reads=bkt, writes=[("STb", r)])

                        def mmYO(e, b=b, r=r):
                            for g in range(2):
                                i = e.matmul(bk7[:, g * 256:(g + 1) * 256], lhsT=Cmask[:, g, b, :],
                                             rhs=STb[:, r, g * 256:(g + 1) * 256], start=(b == 0 and g == 0), stop=(b == 15),
                                             skip_group_check=True)
                            return i
                        P.op("pe", mmYO, reads=["Cmask", ("STb", r)], writes=["bk7"])
                        P.op("dve", lambda e, b=b, r=r: e.tensor_scalar(BmaskQ[:, r, :], Btok, bind[:, 1 + b:2 + b], None, ALU.mult),
                             reads=["Btok", "bind"], writes=[("BmaskQ", r)])
                        st_ps = bk45[:, r * 512:(r + 1) * 512]

                        def mmSt(e, r=r, st_ps=st_ps):
                            for j in range(4):
                                g = j // 2
                                i = e.matmul(st_ps[:, j * 128:(j + 1) * 128], lhsT=xdtd[:, j * 128:(j + 1) * 128],
                                             rhs=BmaskQ[:, r, g * 128:(g + 1) * 128], start=True, stop=True)
                            return i
                        P.op("pe", mmSt, reads=["xdtd", ("BmaskQ", r)], writes=[B45[r]])
                        for j in range(4):
                            P.op("dve", lambda e, j=j, b=b, r3=r3, st_ps=st_ps, dechp=dechp: e.scalar_tensor_tensor(
                                Sin[:, j, r3, :], Sin[:, j, r3, :], dechp[:, j, b:b + 1], st_ps[:, j * 128:(j + 1) * 128],
                                ALU.mult, ALU.add), reads=[("Sin", r3), ("dechp", ti), B45[r]], writes=[("Sin", r3)])
                        P.op("sp", lambda e, b=b, r3=r3: e.dma_start(
                            out=oss_d[l, b].rearrange("(j p) n -> p j n", p=128), in_=Sin[:, :, r3, :]),
                            reads=[("Sin", r3)], dma=("so", r3), scr=True)
                        yield
                yield "SPLIT"
                P.op("dve", lambda e, ec=ec: e.tensor_tensor(h8(y1), h8(bk7[:]), sub(ec, 0, [[1, 8], [0, 64]]), ALU.mult),
                     reads=["bk7", ("ec", ti)], writes=["y1"])
                P.op("dve", lambda e: e.tensor_tensor(y1, y1, bk6[:], ALU.add), reads=["y1", "bk6"], writes=["y1"])
                P.op("dve", lambda e, ti=ti: e.tensor_tensor(yg, y1, zs[par][:, ti, :], ALU.mult),
                     reads=["y1", ("zs", par, ti)], writes=["yg"])
                P.op("dve", lambda e: e.memset(ss[:, 0:1], 0.0), writes=["ss"])
                P.op("act", lambda e: e.activation(junk, yg, AF.Square, accum_out=ss[:, 0:1]),
                     reads=["yg", "ss"], writes=["junk", "ss"])
                P.op("act", lambda e: e.activation(lns[:, 0:1], ss[:, 0:1], AF.Ln, bias=EPS, scale=1.0 / 512),
                     reads=["ss"], writes=["lns"])
                P.op("act", lambda e: e.activation(rs[:, 0:1], lns[:, 0:1], AF.Exp, scale=-0.5),
                     reads=["lns"], writes=["rs"])
                yield
                P.op("dve", lambda e: e.scalar_tensor_tensor(yc, yg, rs[:, 0:1], rowsT[:, 24:536], ALU.mult, ALU.mult),
                     reads=["yg", "rs", "rowsT"], writes=["yc"])
                yield

                def mmYT(e):
                    for j in range(4):
                        i = e.matmul(bk45[:, 512 + j * 128:512 + (j + 1) * 128], lhsT=yc[:, j * 128:(j + 1) * 128], rhs=identb[:],
                                     start=True, stop=True)
                    return i
                P.op("pe", mmYT, reads=["yc", "identb"], writes=["bk5"])
                P.op("act", lambda e, tc0=tc0: e.activation(
                    mixT[par][:, 4:8, tc0:tc0 + 128], bk45[:, 512:1024].rearrange("p (j t) -> p j t", j=4), AF.Copy),
                    reads=["bk5"], writes=[("mixT", par, 4 + j) for j in range(4)])
                yield
            prev_tail = None
            pc = sg.get("prev_carry")
            for ti, t in enumerate(tiles):
                g = tile_gen(ti, t)
                while True:
                    r_ = next(g)
                    if r_ == "SPLIT":
                        while pc is not None and not pc.done:
                            if pc.step():
                                yield
                        break
                    yield
                    if prev_tail is not None:
                        try:
                            P.scr_tok = "SCRT"
                            next(prev_tail)
                            P.scr_tok = "SCR"
                            yield
                        except StopIteration:
                            P.scr_tok = "SCR"
                            prev_tail = None
                while prev_tail is not None:
                    try:
                        P.scr_tok = "SCRT"
                        next(prev_tail)
                        P.scr_tok = "SCR"
                        yield
                    except StopIteration:
                        P.scr_tok = "SCR"
                        prev_tail = None
                prev_tail = g
            sg["carry"] = prev_tail

        def stream_C(gi, l, sg):
            tiles, slots, c0, n, is_s, par = sg["tiles"], sg["slots"], sg["c0"], sg["n"], sg["sample"], sg["par"]
            g_ = sg["carry"]
            while True:
                try:
                    P.scr_tok = "SCRT"
                    next(g_)
                    P.scr_tok = "SCR"
                    yield
                except StopIteration:
                    P.scr_tok = "SCR"
                    break
            xtok = lambda oc: [("xT", t, oc) for t in slots]
            for oc in range(8):
                pa, pt = PA[oc % 2], "bk%d" % (oc % 2)
                s_ = wslot((gi, l, "wout", oc // 2))

                def mm(e, oc=oc, s_=s_, pa=pa):
                    for k in range(8):
                        i = e.matmul(pa[:, 0:n], lhsT=wview(s_, k, (oc % 2) * 128, 128), rhs=mixT[par][:, k, 0:n],
                                     start=(k == 0), stop=(k == 7))
                    return i
                P.scr_tok = "SCRT"
                P.op("pe", mm, reads=rtok(s_) + [("mixT", par, k) for k in range(8)], writes=[pt])
                P.op("dve", lambda e, oc=oc, pa=pa: e.tensor_tensor(
                    xT[:, oc, c0:c0 + n], xT[:, oc, c0:c0 + n], pa[:, 0:n], ALU.add),
                    reads=[pt] + xtok(oc), writes=xtok(oc))
                P.scr_tok = "SCR"
                yield

        def ffn_norm_steps(l, fs):
            c0, n, slots = fs["c0"], fs["n"], fs["slots"]
            xtok = [("xT", t, k) for t in slots for k in range(8)]
            for _ in rmsnorm_steps(xT[:, :, c0:c0 + n], lambda k: xT[:, k, c0:c0 + n], "g2", l,
                                   sq2, bk45[:, 0:512], "bk4", lnv2, rstd2, lambda k: h2[:, k, c0:c0 + n], n,
                                   xtok, [("h2", c0)], "nF", True):
                yield

        def ffn_make(gi, l, fsgs):
            its = [(blk, fs) for blk in range(4) for fs in fsgs]
            slots_of = {}

            def ffn_up(i, bo=0):
                blk, fs = its[i]
                if blk not in slots_of:
                    slots_of[blk] = [wslot((gi, l, "ffn", blk * 8 + hc)) for hc in range(8)]
                sl = slots_of[blk]
                c0, n = fs["c0"], fs["n"]
                hb_ = i % 2
                for hc in range(8):
                    s_ = sl[hc]
                    hp = bk[bo + hc % 2]
                    hpt = BKT[bo + hc % 2]

                    def mm1(e, s_=s_, hp=hp):
                        for k in range(8):
                            i_ = e.matmul(hp[:, 0:n], lhsT=sub(ring[:], s_ * 2048 + k * 128, [[1, 128]]),
                                          rhs=h2[:, k, c0:c0 + n], start=(k == 0), stop=(k == 7))
                        return i_
                    P.op("pe", mm1, reads=[("ring", s_), ("h2", c0)], writes=hpt)
                    P.op("act", lambda e, hp=hp, hc=hc: e.activation(rr[:, hc % 2, 0:n], hp[:, 0:n], AF.Relu),
                         reads=hpt, writes=[("rr", hc % 2)])
                    if SQ_ON_ACT:
                        P.op("act", lambda e, hc=hc: e.activation(hidb[hb_][:, hc, 0:n], rr[:, hc % 2, 0:n], AF.Square),
                             reads=[("rr", hc % 2)], writes=[("hid", hb_, hc)])
                    else:
                        P.op("dve", lambda e, hc=hc: e.tensor_tensor(
                            hidb[hb_][:, hc, 0:n], rr[:, hc % 2, 0:n], rr[:, hc % 2, 0:n], ALU.mult),
                            reads=[("rr", hc % 2)], writes=[("hid", hb_, hc)])
                    yield

            def ffn_down(i):
                blk, fs = its[i]
                sl = slots_of[blk]
                c0, n, slots = fs["c0"], fs["n"], fs["slots"]
                hb_ = i % 2
                for oc in range(8):
                    op_ = bk[2 + oc % 2]

                    def mm2(e, oc=oc, op_=op_):
                        for hc in range(8):
                            i_ = e.matmul(op_[:, 0:n], lhsT=sub(ring[:], sl[hc] * 2048 + 1024 + oc * 128, [[1, 128]]),
                                          rhs=hidb[hb_][:, hc, 0:n], start=(hc == 0), stop=(hc == 7))
                        return i_
                    P.op("pe", mm2, reads=[("ring2", s_) for s_ in sl] + [("hid", hb_, hc) for hc in range(8)],
                         writes=BKT[2 + oc % 2])
                    xt = [("xT", t, oc) for t in slots]
                    P.op("dve", lambda e, oc=oc, op_=op_: e.tensor_tensor(
                        xT[:, oc, c0:c0 + n], xT[:, oc, c0:c0 + n], op_[:, 0:n], ALU.add),
                        reads=BKT[2 + oc % 2] + xt, writes=xt)
                if fs is fsgs[-1]:
                    for hc in range(8):
                        release((gi, l, "ffn", blk * 8 + hc))
            return dict(its=its, up=ffn_up, down=ffn_down)

        def ffn_phase(gi, l, fsgs, ctx, mid_hook=None, skip=(), upped0=False, tail_hook=None):
            for fs in fsgs:
                if fs in skip:
                    continue
                for _ in ffn_norm_steps(l, fs):
                    pass
            its = ctx["its"]
            if not upped0:
                for _ in ctx["up"](0):
                    pass
            for i in range(len(its)):
                if i + 1 < len(its):
                    for _ in ctx["up"](i + 1):
                        pass
                if tail_hook is not None and i == len(its) - 1:
                    tail_hook()
                ctx["down"](i)
                if mid_hook is not None and i == len(its) // 2:
                    mid_hook()

        def fence():
            o_ = P.op("dve", lambda e: e.memset(dummy[:], 0.0), reads=[], writes=["SCR", "SCRT"], scr=False)
            FENCE_T.append((o_.finish / 1e3, P.busy.get("pe", 0.0) / 1e3))

        def fenceA():
            P.op("dve", lambda e: e.memset(dummy[:], 0.0), reads=[], writes=["SCR"], scr=False)

        pump()
        for gi, (ptiles, has_s) in enumerate(GROUPS):
            npc = 128 * len(ptiles)
            ncol = npc + (128 if has_s else 0)
            nsl = ncol // 128
            tids = list(ptiles) + ([16] if has_s else [])
            P.op("sp", lambda e, ptiles=ptiles, npc=npc: e.dma_start(
                out=xT[:, :, 0:npc],
                in_=xp_d[:, ptiles[0] * 128:ptiles[0] * 128 + npc].rearrange("(k p) n -> p k n", p=128)),
                writes=[("xT", t, k) for t in range(len(ptiles)) for k in range(8)], dma="xi0")
            if has_s:
                P.op("sp", lambda e, npc=npc: e.dma_start(
                    out=xT[:, :, npc:npc + 128], in_=xs_d.rearrange("(k p) n -> p k n", p=128)),
                    writes=[("xT", nsl - 1, k) for k in range(8)], dma="xi1")
            sgs = []
            for i in range(0, len(ptiles), 2):
                tl = ptiles[i:i + 2]
                sgs.append(dict(tiles=tl, slots=list(range(i, i + len(tl))), c0=i * 128, n=128 * len(tl), sample=False))
            if has_s:
                sgs.append(dict(tiles=[16], slots=[nsl - 1], c0=npc, n=128, sample=True))
            fsgs = []
            if FFN_ALIGN and ncol - sgs[-1]["n"] <= NFF and sgs[-1]["n"] >= FFN_ALIGN:
                cuts = [0, ncol - sgs[-1]["n"], ncol]
            else:
                nf = -(-ncol // NFF)
                wf = -(-ncol // nf)
                cuts = list(range(0, ncol, wf)) + [ncol]
            for c, c1 in zip(cuts[:-1], cuts[1:]):
                n = c1 - c
                fsgs.append(dict(c0=c, n=n, slots=list(range(c // 128, (c + n - 1) // 128 + 1))))
            pre_normed = False
            for l in range(NL):
                if l == 0 and gi == 0:
                    layer_prep(0)
                fence()
                if has_s:
                    P.op("sp", lambda e, l=l: e.dma_start(out=hcs, in_=hc_d[l].rearrange("(j p) b k -> p j b k", p=128)),
                         writes=["hcs"], dma="h0", scr=True)
                    P.op("sp", lambda e, l=l: e.dma_start(out=hxs, in_=hx_d[l].rearrange("(c p) b k -> p c b k", p=128)),
                         writes=["hxs"], dma="h1", scr=True)
                for si, sg in enumerate(sgs):
                    sg["par"] = si % 2
                def run_streams(streams, gated=None, gate=None):
                    live = [s_ for s_ in streams if s_ is not None and not s_.done]
                    while live or (gated is not None and not gated.done):
                        if gated is not None and (gate is None or gate.done) and gated not in live and not gated.done:
                            live.append(gated)
                        if not live:
                            break
                        live.sort(key=lambda x: x.t - x.prio)
                        st_ = live[0]
                        P.step_finish = 0.0
                        if st_.step():
                            st_.t = max(st_.t, P.step_finish)
                        live = [s_ for s_ in live if not s_.done]
                def rel_win():
                    for i in range(11):
                        release((gi, l, "win", i))
                run_streams([Strm(stream_A(gi, l, sgs[0], skip_norm=pre_normed))])
                pre_normed = False
                if len(sgs) == 1 and EARLY_REL:
                    rel_win()
                carry = None
                for si, sg in enumerate(sgs):
                    nxt = Strm(stream_A(gi, l, sgs[si + 1])) if si + 1 < len(sgs) else None
                    sg["prev_carry"] = carry
                    if carry is not None:
                        P.step_finish = 0.0
                        carry.step()
                        carry.t = P.step_finish
                    run_streams([carry, Strm(stream_B(gi, l, sg), PRIO_B)], gated=nxt, gate=carry)
                    carry = Strm(stream_C(gi, l, sg), PRIO_C)
                    if si == len(sgs) - 2 and EARLY_REL:
                        rel_win()
                fenceA()
                last_slots = set(sgs[-1]["slots"])
                pre = [fs for fs in fsgs if not (set(fs["slots"]) & last_slots)]
                fctx = ffn_make(gi, l, fsgs)
                upped0 = PRE_UP and bool(pre) and (fsgs[0] in pre)

                def pre_ffn(pre=pre, fctx=fctx, upped0=upped0):
                    for fs in pre:
                        for _ in ffn_norm_steps(l, fs):
                            yield
                    if upped0:
                        for _ in fctx["up"](0, PRE_BANK):
                            yield
                run_streams([carry, Strm(pre_ffn())])
                if not EARLY_REL:
                    rel_win()
                for i in range(4):
                    release((gi, l, "wout", i))
                if DEBUG_STOP:
                    break
                fence()
                nl_ = (l + 1) if l + 1 < NL else (0 if gi + 1 < len(GROUPS) else None)
                th_ = None
                if PRE_A and l + 1 < NL and set(sgs[0]["slots"]) <= set(fsgs[0]["slots"]) and len(fsgs) > 1:
                    def th_(l=l):
                        P.op("dve", lambda e: e.memset(dummy[:], 0.0), reads=[],
                             writes=[("h2", fs["c0"]) for fs in fsgs] + ["hTguard"])
                        for _ in norm1_steps(l + 1, sgs[0], guard=["hTguard"]):
                            pass
                    pre_normed = True
                if nl_ is not None:
                    prep_load(nl_)
                    ffn_phase(gi, l, fsgs, fctx, mid_hook=lambda nl_=nl_: prep_compute(nl_), skip=pre, upped0=upped0, tail_hook=th_)
                else:
                    ffn_phase(gi, l, fsgs, fctx, skip=pre, upped0=upped0, tail_hook=th_)
            blocks = [(c, min(256, npc - c), False) for c in range(0, npc, 256)] + ([(npc, 128, True)] if has_s else [])
            for (c, n, smp) in ([] if DEBUG_STOP else blocks):
                slots = list(range(c // 128, (c + n) // 128))
                xtok = [("xT", t, k) for t in slots for k in range(8)]
                rmsnorm_cols(xT[:, :, c:c + n], lambda k, c=c, n=n: xT[:, k, c:c + n], "gf", 0,
                             sq2, bk45[:, 0:512], "bk4", lnv2, rstd2, lambda k, n=n: yout[:, k, 0:n], n, xtok, ["yout"], "nO")
                if smp:
                    dst = ys_d.rearrange("(k p) n -> p k n", p=128)
                else:
                    t0 = ptiles[0] + c // 128
                    dst = yp_d[:, t0 * 128:t0 * 128 + n].rearrange("(k p) n -> p k n", p=128)
                P.op("sp", lambda e, dst=dst, n=n: e.dma_start(out=dst, in_=yout[:, :, 0:n]), reads=["yout"], dma="yo", scr=True)
        for l in range(NL):
            P.op("sp", lambda e, l=l: e.dma_start(out=ocp_d[l].rearrange("(j p) k -> p j k", p=128), in_=hist_g[:, l, :, :]),
                 reads=[("hist_g", l)], dma="op0")
            P.op("sp", lambda e, l=l: e.dma_start(out=oxp_d[l].rearrange("(c p) k -> p c k", p=128), in_=hist_x[:, l, :, :]),
                 reads=[("hist_x", l)], dma="op1")
        print("ops:", P.nops, "scratch words mixer/ffn:", mixer_words, ffn_words, "model_us: %.0f" % (max(P.eng_free.values()) / 1e3), "busy_us:", {k: int(v / 1e3) for k, v in P.busy.items()}, flush=True)
        P.emit()
    return nc


_NC = None


def kernel(x_prompt, x_sample, state_conv, state_ssm_conv, state_ssm, norm1, w_in, w_s, b_s, conv_w,
           ssm_conv_w, ssm_conv_b, dt_bias, a_log, d_skip, ssm_norm, w_out, norm2, w_ff1, w_ff2, final_norm):
    global _NC
    f = lambda a: np.ascontiguousarray(np.asarray(a, dtype=np.float32))
    x_prompt, x_sample = f(x_prompt), f(x_sample)
    state_conv, state_ssm_conv, state_ssm = f(state_conv), f(state_ssm_conv), f(state_ssm)
    cv = np.zeros((128, NCV), np.float32)

    def put(nm, l, arr):
        base, n = CVL[nm]
        cv[:, base + l * n: base + l * n + n] = arr.reshape(n, 128).T
    for l in range(DEPTH):
        put("g1", l, f(norm1)[l])
        put("g2", l, f(norm2)[l])
        put("cw", l, f(conv_w)[l])
        put("sw", l, f(ssm_conv_w)[l])
        put("sb", l, f(ssm_conv_b)[l])
    base, n = CVL["gf"]
    cv[:, base:base + 8] = f(final_norm).reshape(8, 128).T
    rows = np.concatenate([f(dt_bias), f(a_log), f(d_skip), f(ssm_norm)], axis=1)
    bs = f(b_s)
    brow = np.zeros((DEPTH, 2, 512), np.float32)
    for hh in range(2):
        for j in range(2):
            brow[:, hh, j * 128:(j + 1) * 128] = bs[:, 2 * j + hh, :]
            brow[:, hh, 256 + j * 128:256 + (j + 1) * 128] = np.tile(bs[:, 2 * j + hh, 0:8], (1, 16))
    idx = np.arange(128)
    U_p = (idx[:, None] <= idx[None, :]).astype(np.float32)
    L_p = (idx[:, None] > idx[None, :]).astype(np.float32)
    same = (idx[:, None] // 8 == idx[None, :] // 8).astype(np.float32)
    cmat = np.stack([U_p, L_p, U_p * same, L_p * same, np.eye(128, dtype=np.float32)], axis=1)
    bind = np.zeros((128, 17), np.float32)
    bind[:, 0] = 1.0
    bind[idx, 1 + idx // 8] = 1.0
    sel2 = np.zeros((2, 128), np.float32)
    sel2[0, 0:64] = 1.0
    sel2[1, 64:128] = 1.0
    shared = dict(w_in=f(w_in), w_out=f(w_out), w_ff1=f(w_ff1), w_ff2=f(w_ff2), cv=cv, rows=np.ascontiguousarray(rows),
                  w_s=f(w_s), brow=brow, cmat=np.ascontiguousarray(cmat), bind=bind, sel2=sel2)
    in_maps = []
    for c in range(NCORES):
        sl = slice(16 * c, 16 * c + 16)
        m = dict(shared)
        m["xp"] = np.ascontiguousarray(x_prompt[c].T)
        m["xs"] = np.ascontiguousarray(x_sample[sl].reshape(128, 1024).T)
        m["hc"] = np.ascontiguousarray(state_conv[:, sl].transpose(0, 3, 1, 2))
        m["hx"] = np.ascontiguousarray(state_ssm_conv[:, sl].transpose(0, 3, 1, 2))
        m["ssm"] = np.ascontiguousarray(state_ssm[:, sl].reshape(DEPTH, 16, 512, 128))
        in_maps.append(m)
    if _NC is None:
        _NC = build()
    res = run_bass_kernel_spmd(_NC, in_maps, core_ids=list(range(NCORES)))
    R = res.results
    y_prompt = np.stack([R[c]["yp"].T for c in range(NCORES)])
    y_sample = np.concatenate([R[c]["ys"].T.reshape(16, 8, 1024) for c in range(NCORES)])
    chunk_v_prompt = np.stack([R[c]["cvp"] for c in range(NCORES)], axis=1)
    conv_prompt = np.stack([R[c]["ocp"].transpose(0, 2, 1) for c in range(NCORES)], axis=1)
    ssm_conv_prompt = np.stack([R[c]["oxp"].transpose(0, 2, 1) for c in range(NCORES)], axis=1)
    ssm_prompt = np.stack([R[c]["osp"].reshape(DEPTH, 8, 64, 128) for c in range(NCORES)], axis=1)
    chunk_v_sample = np.concatenate([R[c]["cvs"].reshape(DEPTH, 16, 8, 256) for c in range(NCORES)], axis=1)
    conv_sample = np.concatenate([R[c]["ocs"].transpose(0, 2, 3, 1) for c in range(NCORES)], axis=1)
    ssm_conv_sample = np.concatenate([R[c]["oxs"].transpose(0, 2, 3, 1) for c in range(NCORES)], axis=1)
    ssm_sample = np.concatenate([R[c]["oss"].reshape(DEPTH, 16, 8, 64, 128) for c in range(NCORES)], axis=1)
    outs = (y_prompt, y_sample, chunk_v_prompt, conv_prompt, ssm_conv_prompt, ssm_prompt,
            chunk_v_sample, conv_sample, ssm_conv_sample, ssm_sample)
    return tuple(np.ascontiguousarray(o, dtype=np.float32) for o in outs)
```

```python
from contextlib import ExitStack
from collections import deque
import numpy as np
import concourse.bass as bass
import concourse.mybir as mybir
from concourse.bass_utils import run_bass_kernel_spmd

F32 = mybir.dt.float32
BF16 = mybir.dt.bfloat16
AF = mybir.ActivationFunctionType
ALU = mybir.AluOpType

NCORES = 8
DEPTH = 4
EPS = 1e-5
D_IN = 2824
COMPUTE = ("pe", "act", "dve", "pool")
ENGINES = ("pe", "act", "dve", "pool", "sp")


class Op:
    __slots__ = ("eng", "fn", "deps", "signal", "dma_key", "event", "idx", "finish")

    def __init__(self, eng, fn, dma_key):
        self.eng = eng
        self.fn = fn
        self.deps = []
        self.signal = dma_key is not None
        self.dma_key = dma_key
        self.event = None
        self.idx = -1
        self.finish = 0.0


class _FakeIns:
    def then_inc(self, *a, **k):
        return self


class _FakeEng:
    def __init__(self, kind):
        self.kind = kind
        self.cost = 0.0

    @staticmethod
    def _free(ap):
        n = 1
        for d in ap.shape[1:]:
            n *= int(d)
        return n

    def matmul(self, out, lhsT=None, rhs=None, **kw):
        n = max(self._free(out), 64)
        self.cost += n / 2.0 * (4.0 if lhsT.dtype == F32 else 1.0) + 8.0
        return _FakeIns()

    def dma_start(self, out=None, in_=None, **kw):
        self.cost += 2500.0
        return _FakeIns()

    def __getattr__(self, name):
        def f(*a, **k):
            out = a[0] if a else k.get("out", k.get("ap"))
            fr = self._free(out)
            if self.kind == "act":
                self.cost += 230.0 + 0.83 * fr
            elif self.kind == "pool":
                self.cost += 250.0 + 2.1 * fr
            else:
                self.cost += 110.0 + 1.05 * fr
            return _FakeIns()
        return f


HOP_NS = 120.0
PRIO_B = 0.0
PRE_UP = False
AEXP_ACT = False
BMASK_ACT = True
SQ_ON_ACT = True
PRE_A = True
FFN_ALIGN = 0
PRE_BANK = 2
EARLY_REL = True
FENCE_T = []
PRIO_C = 0.0


class Strm:
    def __init__(self, g, prio=0.0):
        self.g = g
        self.done = g is None
        self.t = 0.0
        self.prio = prio

    def step(self):
        if self.done:
            return False
        try:
            next(self.g)
            return True
        except StopIteration:
            self.done = True
            return False

BLAME = None
NORM_ENG = 'dve'


class Prog:
    def __init__(self, nc):
        self.nc = nc
        self.eng_ops = {e: [] for e in ENGINES}
        self.last_writer = {}
        self.readers = {}
        self.dma_keys = []
        self.last_dma = {}
        self.nops = 0
        self.eng_free = {e: 0.0 for e in ENGINES}
        self.scr_tok = "SCR"
        self.step_finish = 0.0
        self.busy = {}
        self.stall = {}

    def op(self, eng, fn, reads=(), writes=(), dma=None, scr=None):
        if scr is None:
            scr = dma is None
        if scr:
            reads = list(reads) + [self.scr_tok]
        o = Op(eng, fn, dma)
        o.idx = self.nops
        self.nops += 1
        if dma is not None and dma not in self.last_dma:
            self.dma_keys.append(dma)
        is_dma = dma is not None
        deps = {}

        def add(d, kind):
            if d is None or d is o:
                return
            if (not is_dma) and d.dma_key is None and d.eng == eng and kind != "raw":
                return
            deps[d.idx] = d

        if is_dma:
            add(self.last_dma.get(dma), "raw")
            self.last_dma[dma] = o
        for t in reads:
            add(self.last_writer.get(t), "raw")
        for t in writes:
            add(self.last_writer.get(t), "waw")
            for r in self.readers.get(t, ()):
                add(r, "war")
        o.deps = list(deps.values())
        for d in o.deps:
            d.signal = True
        fe = _FakeEng(eng)
        try:
            fn(fe)
        except Exception:
            fe.cost = 500.0
        start = self.eng_free[eng]
        blame = None
        for d in o.deps:
            if d.finish + HOP_NS > start:
                start = d.finish + HOP_NS
                blame = d
        if blame is not None and BLAME is not None:
            key = (eng, fn.__code__.co_firstlineno, blame.eng, blame.fn.__code__.co_firstlineno)
            BLAME[key] = BLAME.get(key, 0.0) + (start - self.eng_free[eng])
        if is_dma:
            o.finish = start + fe.cost
            self.eng_free[eng] = start + 60.0
        else:
            o.finish = start + fe.cost
            self.eng_free[eng] = o.finish
        if o.finish > self.step_finish:
            self.step_finish = o.finish
        self.busy[eng] = self.busy.get(eng, 0.0) + fe.cost
        self.stall[eng] = self.stall.get(eng, 0.0) + (start - (self.eng_free[eng] - (fe.cost if not is_dma else 60.0)) if False else 0.0)
        for t in reads:
            self.readers.setdefault(t, []).append(o)
        for t in writes:
            self.last_writer[t] = o
            self.readers[t] = []
        self.eng_ops[eng].append(o)
        return o

    def emit(self):
        nc = self.nc
        with ExitStack() as st:
            esem = {e: st.enter_context(nc.semaphore("s_" + e)) for e in COMPUTE}
            dsem = {k: st.enter_context(nc.semaphore("d%d" % i)) for i, k in enumerate(self.dma_keys)}
            for e in ENGINES:
                cnt = 0
                for o in self.eng_ops[e]:
                    if o.dma_key is None and o.signal:
                        cnt += 1
                        o.event = (esem[e], cnt)
            dcnt = {k: 0 for k in self.dma_keys}
            allops = sorted((o for e in ENGINES for o in self.eng_ops[e]), key=lambda o: o.idx)
            for o in allops:
                if o.dma_key is not None:
                    dcnt[o.dma_key] += 16
                    o.event = (dsem[o.dma_key], dcnt[o.dma_key])
            block = st.enter_context(nc.Block())

            def run(e, handle):
                waited = {}
                for o in self.eng_ops[e]:
                    need = {}
                    for d in o.deps:
                        sem, val = d.event
                        if need.get(id(sem), (None, 0))[1] < val:
                            need[id(sem)] = (sem, val)
                    for k, (sem, val) in need.items():
                        if waited.get(k, 0) < val:
                            handle.wait_ge(sem, val)
                            waited[k] = val
                    ins = o.fn(handle)
                    if o.dma_key is not None:
                        ins.then_inc(o.event[0], 16)
                    elif o.signal:
                        ins.then_inc(o.event[0], 1)
                if e == "sp":
                    for k in self.dma_keys:
                        if dcnt[k] and waited.get(id(dsem[k]), 0) < dcnt[k]:
                            handle.wait_ge(dsem[k], dcnt[k])

            @block.tensor
            def _(h):
                run("pe", h)

            @block.scalar
            def _(h):
                run("act", h)

            @block.vector
            def _(h):
                run("dve", h)

            @block.gpsimd
            def _(h):
                run("pool", h)

            @block.sync
            def _(h):
                run("sp", h)


def sub(ap, off, dims, np_=128, p0=0):
    ps = ap.ap[0][0]
    return bass.AP(ap.tensor, ap.offset + p0 * ps + off, [[ps, np_]] + [list(d) for d in dims])


def _cv_layout():
    lay = {}
    c = 0
    for nm, n in (("g1", 8), ("g2", 8), ("cw", 6), ("sw", 32), ("sb", 8)):
        lay[nm] = (c, n)
        c += n * DEPTH
    lay["gf"] = (c, 8)
    c += 8
    return lay, c


CVL, NCV = _cv_layout()


def cvcol(nm, l, i):
    base, n = CVL[nm]
    return base + l * n + i


NS = 16
NSG = 256
NFF = 512
GROUPS = [(list(range(0, 4)), True), (list(range(4, 10)), False), (list(range(10, 16)), False)]
NCOLMAX = 6 * 128
NL = DEPTH
LAST_TILE = 15
DEBUG_STOP = False
CARVE_DBG = {}


def build():
    nc = bass.Bass("TRN2", target_bir_lowering=False)
    P = Prog(nc)
    din = lambda n, s: nc.dram_tensor(n, s, F32, kind="ExternalInput").ap()
    dout = lambda n, s: nc.dram_tensor(n, s, F32, kind="ExternalOutput").ap()
    xp_d = din("xp", [1024, 128 * (LAST_TILE + 1)])
    xs_d = din("xs", [1024, 128])
    hc_d = din("hc", [DEPTH, 256, 16, 2])
    hx_d = din("hx", [DEPTH, 1024, 16, 3])
    ssm_d = din("ssm", [DEPTH, 16, 512, 128])
    win_d = din("w_in", [DEPTH, 1024, D_IN])
    wout_d = din("w_out", [DEPTH, 1024, 1024])
    wf1_d = din("w_ff1", [DEPTH, 1024, 4096])
    wf2_d = din("w_ff2", [DEPTH, 4096, 1024])
    cv_d = din("cv", [128, NCV])
    rows_d = din("rows", [DEPTH, 536])
    ws_d = din("w_s", [DEPTH, 4, 128, 128])
    brow_d = din("brow", [DEPTH, 2, 512])
    cm_d = din("cmat", [128, 5, 128])
    bind_d = din("bind", [128, 17])
    sel_d = din("sel2", [2, 128])

    yp_d = dout("yp", [1024, 128 * (LAST_TILE + 1)])
    ys_d = dout("ys", [1024, 128])
    cvp_d = dout("cvp", [DEPTH, 128, 256])
    cvs_d = dout("cvs", [DEPTH, 128, 256])
    ocp_d = dout("ocp", [DEPTH, 256, 2])
    oxp_d = dout("oxp", [DEPTH, 1024, 3])
    ocs_d = dout("ocs", [DEPTH, 256, 16, 2])
    oxs_d = dout("oxs", [DEPTH, 1024, 16, 3])
    osp_d = dout("osp", [DEPTH, 512, 128])
    oss_d = dout("oss", [DEPTH, 16, 512, 128])

    with ExitStack() as st:
        SB = lambda n, s, d=F32: st.enter_context(nc.sbuf_tensor(n, s, d))
        PSB = lambda n, s, d=F32: st.enter_context(nc.psum_tensor(n, s, d))
        xT = SB("xT", [128, 8, NCOLMAX])
        ring = SB("ring", [128, NS, 2048], BF16)
        wdt = SB("wdt", [128, DEPTH, 8, 8], BF16)
        cvt = SB("cvt", [128, NCV])
        cmat = SB("cmat_t", [128, 5, 128])
        identb = SB("identb", [128, 128], BF16)
        onesb = SB("onesb", [128, 128], BF16)
        bind = SB("bind_t", [128, 17])
        sel_f = SB("sel_f", [2, 128])
        sel_b = SB("sel_b", [2, 128], BF16)
        S_all = SB("S_all", [128, DEPTH, 4, 128])
        hist_g = SB("hist_g", [128, DEPTH, 2, 2])
        hist_x = SB("hist_x", [128, DEPTH, 8, 3])
        Cmask = SB("Cmask", [128, 2, 16, 128], BF16)
        rowsT = SB("rowsT", [128, 536])
        Arow = SB("Arow", [128, 8])
        wmT = SB("wmT", [128, 2, 4, 128], BF16)
        ws_nat = SB("ws_nat", [128, 4, 128])
        ws_nat1 = SB("ws_nat1", [128, 4, 128])
        brow_f = SB("brow_f", [2, 512])
        diagC = SB("diagC", [128, 4, 8, 128], BF16)
        diagB = SB("diagB", [128, 3, 2, 128], BF16)
        brow_b = SB("brow_b", [2, 512], BF16)

        SCRW = 19 * 1024
        scr = SB("scr", [128, SCRW])
        scr_b = scr.bitcast(BF16)
        dummy = SB("dummy_t", [128, 8])
        cur = [0]

        def carve(shape, dt=F32):
            n = int(np.prod(shape))
            words = n if dt == F32 else (n + 1) // 2
            words = (words + 7) // 8 * 8
            o = cur[0]
            cur[0] += words
            assert cur[0] <= SCRW, ("scratch overflow", cur[0])
            dims = []
            stride = 1
            for s_ in reversed(shape):
                dims.append([stride, s_])
                stride *= s_
            dims.reverse()
            if dt == F32:
                return bass.AP(scr, o, [[SCRW, 128]] + dims)
            return bass.AP(scr_b, 2 * o, [[2 * SCRW, 128]] + dims)

        N = NSG
        sq = carve([8, N], BF16)
        hT = carve([8, N], BF16)
        lnv = carve([N])
        rstd = carve([N])
        cg = carve([N])
        gext = carve([2, N + 2], BF16)
        xe = carve([2, N + 4], BF16)
        bgb = carve([2, N], BF16)
        off_cross = cur[0]
        uT = [carve([2, N], BF16) for _ in range(2)]
        mixT = [carve([8, N], BF16) for _ in range(2)]
        off_xact = cur[0]
        xact = [carve([8, N], BF16) for _ in range(2)]
        vb = [carve([2, 256], BF16) for _ in range(2)]
        zs = [carve([2, 512], BF16) for _ in range(2)]
        dtt = [carve([2, 8]) for _ in range(2)]
        att = [carve([2, 8]) for _ in range(2)]
        vf = carve([256])
        off_bfront = cur[0]
        xdt = carve([512], BF16)
        xdtd = carve([512], BF16)
        Btok = carve([256], BF16)
        xsD = carve([512], BF16)
        rseg = carve([1024])
        decT2 = [carve([1024], BF16) for _ in range(2)]
        cbTm = carve([256], BF16)
        aexp = carve([512])
        off_y1 = cur[0]
        y1 = carve([512])
        yg = carve([512])
        junk = carve([512], BF16)
        yc = carve([512], BF16)
        STb = carve([2, 512], BF16)
        de2 = [carve([8]) for _ in range(2)]
        ec2 = [carve([8]) for _ in range(2)]
        e1 = carve([8])
        dtr = carve([8])
        dechp2 = [carve([4, 16]) for _ in range(2)]
        ss = carve([8])
        lns = carve([8])
        rs = carve([8])
        off_sin = cur[0]
        Sin = carve([4, 3, 128])
        tmpS = carve([2, 128])
        BmaskQ = carve([2, 256], BF16)
        gext_s = carve([2, 16, 10], BF16)
        xe_s = carve([2, 16, 12], BF16)
        ocst = carve([2, 16, 2])
        oxst = carve([8, 16, 3])
        hcs = carve([2, 16, 2])
        hxs = carve([8, 16, 3])
        mixer_words = cur[0]
        for _nm in ("y1", "yg", "STb", "xdt", "xdtd", "cbTm", "xsD", "yc", "Btok", "aexp", "rseg"):
            _a = locals()[_nm]
            CARVE_DBG[_nm] = (int(_a.offset), [list(x) for x in _a.ap], str(_a.dtype))
        cur[0] = 0
        h2 = carve([8, NCOLMAX], BF16)
        assert cur[0] <= off_cross, (cur[0], off_cross)
        rr = carve([2, NFF], BF16)
        assert cur[0] <= off_cross, (cur[0], off_cross)
        yout = carve([8, 256])
        ffn_words = cur[0]
        assert cur[0] <= off_xact, (cur[0], off_xact)
        cur[0] = off_xact
        hidb = [carve([8, NFF], BF16)]
        assert cur[0] <= off_xact + 2 * 8 * N // 2, cur[0]
        cur[0] = off_sin
        hidb.append(carve([8, NFF], BF16))
        assert cur[0] <= mixer_words, (cur[0], mixer_words)
        cur[0] = off_bfront
        sq2 = carve([8, NFF], BF16)
        lnv2 = carve([NFF])
        rstd2 = carve([NFF])
        assert cur[0] <= off_y1, (cur[0], off_y1)

        bk = [PSB("bk%d" % i, [128, 512]) for i in range(4)]
        bk45 = PSB("bk45", [128, 1024])
        bk6 = PSB("bk6", [128, 512])
        bk7 = PSB("bk7", [128, 512])
        BKT = {i: ["bk%d" % i] for i in range(4)}
        B45 = ["bk4", "bk5"]
        PA = [bk[0][:, 0:256], bk[1][:, 0:256]]
        ssq_ps = bk[2][:, 0:256]
        pv_ps = bk[2][:, 256:512]
        pz_ps = bk[2]
        pdt_ps = bk[2][:, 0:8]
        de_ps = bk[3][:, 8:16]
        ec_ps = bk[3][:, 16:24]
        cl_ps = bk[3][:, 32:96]
        sTA_ps = bk[3][:, 256:512]
        bkST = [(bk[2], BKT[2]), (bk[3], BKT[3])]

        cm_U = [cmat[:, 0, :], cmat[:, 2, :]]
        cm_L = [cmat[:, 1, :], cmat[:, 3, :]]
        ident = cmat[:, 4, :]

        P.op("sp", lambda e: e.dma_start(out=cvt[:], in_=cv_d), writes=["cvt"], dma="c0")
        P.op("sp", lambda e: e.dma_start(out=cmat[:], in_=cm_d), writes=["cmat"], dma="c1")
        P.op("sp", lambda e: e.dma_start(out=bind[:], in_=bind_d), writes=["bind"], dma="c2")
        P.op("sp", lambda e: e.dma_start(out=sel_f[:], in_=sel_d), writes=["sel_f"], dma="c3")
        for l in range(DEPTH):
            P.op("pool", lambda e, l=l: e.dma_start(
                out=wdt[:, l, :, :], in_=win_d[l, :, 2816:2824].rearrange("(k p) c -> p k c", p=128)),
                writes=["wdt"], dma="c4")
        P.op("dve", lambda e: e.tensor_copy(identb[:], ident), reads=["cmat"], writes=["identb"])
        P.op("dve", lambda e: e.memset(onesb[:], 1.0), writes=["onesb"])
        P.op("dve", lambda e: e.tensor_copy(sel_b[:], sel_f[:]), reads=["sel_f"], writes=["sel_b"])
        P.op("dve", lambda e: e.memset(S_all[:], 0.0), writes=[("S", l) for l in range(DEPTH)])
        P.op("dve", lambda e: e.memset(hist_g[:], 0.0), writes=[("hist_g", l) for l in range(DEPTH)])
        P.op("dve", lambda e: e.memset(hist_x[:], 0.0), writes=[("hist_x", l) for l in range(DEPTH)])
        P.op("dve", lambda e: e.memset(Cmask[:], 0.0), writes=["Cmask"])
        P.op("dve", lambda e: e.memset(ws_nat1[:], 0.0), writes=["ws_nat1z"])

        free_slots = deque(range(NS))
        pending = deque()
        loc = {}
        for gi in range(len(GROUPS)):
            for l in range(NL):
                for i in range(11):
                    pending.append((gi, l, "win", i))
                for i in range(4):
                    pending.append((gi, l, "wout", i))
                for j in range(32):
                    pending.append((gi, l, "ffn", j))

        def rtok(s_):
            return [("ring", s_), ("ring2", s_)]

        def issue(ld, s_):
            gi, l, kind, i = ld
            if kind in ("win", "wout"):
                src = (win_d if kind == "win" else wout_d)[l, :, 256 * i:256 * i + 256]
                dst = sub(ring[:], s_ * 2048, [[256, 8], [1, 256]])
                P.op("pool", lambda e: e.dma_start(out=dst, in_=src.rearrange("(k p) c -> p k c", p=128)),
                     writes=rtok(s_), dma=("w", s_, 0))
            else:
                src1 = wf1_d[l, :, 128 * i:128 * i + 128].rearrange("(k p) c -> p k c", p=128)
                dst1 = sub(ring[:], s_ * 2048, [[128, 8], [1, 128]])
                P.op("pool", lambda e: e.dma_start(out=dst1, in_=src1), writes=[("ring", s_)], dma=("w", s_, 0))
                src2 = wf2_d[l, 128 * i:128 * i + 128, :]
                dst2 = sub(ring[:], s_ * 2048 + 1024, [[1, 1024]])
                P.op("pool", lambda e: e.dma_start(out=dst2, in_=src2), writes=[("ring2", s_)], dma=("w", s_, 1))

        def pump():
            while free_slots and pending:
                ld = pending.popleft()
                s_ = free_slots.popleft()
                issue(ld, s_)
                loc[ld] = s_

        def wslot(ld):
            if ld not in loc:
                pump()
            assert ld in loc, ("ring too small for", ld)
            return loc[ld]

        def release(ld):
            free_slots.append(loc.pop(ld))
            pump()

        def wview(s_, k, c0, n):
            return sub(ring[:], s_ * 2048 + k * 256 + c0, [[1, n]])

        def rmsnorm_steps(xall, xk, gname, l, sqb, ssqp, pstok, lnvb, rstdb, out_fn, n, xtok, otok, tagp, fine=False):
            if not fine:
                P.op("act", lambda e: e.activation(sqb[:, :, 0:n], xall, AF.Square), reads=xtok, writes=[(tagp, "sq")])
                yield

                def mm(e):
                    for k in range(8):
                        i = e.matmul(ssqp[:, 0:n], lhsT=onesb[:], rhs=sqb[:, k, 0:n], start=(k == 0), stop=(k == 7))
                    return i
                P.op("pe", mm, reads=[(tagp, "sq"), "onesb"], writes=[pstok])
            else:
                for q in range(4):
                    P.op("act", lambda e, q=q: e.activation(sqb[:, 2 * q:2 * q + 2, 0:n], xall[:, 2 * q:2 * q + 2, :], AF.Square),
                         reads=xtok, writes=[(tagp, "sq", q)])
                yield
                for q in range(4):
                    def mmq(e, q=q):
                        for k in (2 * q, 2 * q + 1):
                            i = e.matmul(ssqp[:, 0:n], lhsT=onesb[:], rhs=sqb[:, k, 0:n], start=(k == 0), stop=(k == 7))
                        return i
                    P.op("pe", mmq, reads=[(tagp, "sq", q), "onesb"], writes=[pstok])
            P.op("act", lambda e: e.activation(lnvb[:, 0:n], ssqp[:, 0:n], AF.Ln, bias=EPS, scale=1.0 / 1024),
                 reads=[pstok], writes=[(tagp, "lnv")])
            P.op("act", lambda e: e.activation(rstdb[:, 0:n], lnvb[:, 0:n], AF.Exp, scale=-0.5),
                 reads=[(tagp, "lnv")], writes=[(tagp, "rstd")])
            yield
            for k in range(8):
                gcol = cvt[:, cvcol(gname, l, k):cvcol(gname, l, k) + 1]
                P.op(NORM_ENG if tagp == "nA" else "dve", lambda e, k=k, gcol=gcol: e.scalar_tensor_tensor(
                    out_fn(k), xk(k), gcol, rstdb[:, 0:n], ALU.mult, ALU.mult),
                    reads=xtok + [(tagp, "rstd"), "cvt"], writes=(otok + [("hTk", k)]) if fine else otok)

        def rmsnorm_cols(*a):
            for _ in rmsnorm_steps(*a):
                pass

        def b16(ap, n=16):
            return ap.rearrange("p (b t) -> p b t", b=n)

        def prep_load(l):
            P.op("sp", lambda e: e.dma_start(out=rowsT[:], in_=rows_d[l:l + 1, :].partition_broadcast(128)),
                 writes=["rowsT"], dma="r0")
            P.op("sp", lambda e: e.dma_start(out=brow_f[:], in_=brow_d[l]), writes=["brow_f"], dma="r1")
            P.op("sp", lambda e: e.dma_start(out=ws_nat[:], in_=ws_d[l].rearrange("h t s -> t h s")),
                 writes=["ws_nat"], dma="r2")
            for b in range(16):
                P.op("sp", lambda e, b=b: e.dma_start(
                    out=ws_nat1[8 * b:8 * b + 8, :, 8 * b:8 * b + 8],
                    in_=ws_d[l, :, 0:8, 0:8].rearrange("h t s -> t h s")),
                    reads=["ws_nat1z"], writes=[("ws_blk", b)], dma=("r3", b % 4))

        def prep_compute(l):
            P.op("act", lambda e: e.activation(Arow[:], rowsT[:, 8:16], AF.Exp), reads=["rowsT"], writes=["Arow"])
            P.op("dve", lambda e: e.tensor_scalar(Arow[:], Arow[:], -1.0, None, ALU.mult), reads=["Arow"], writes=["Arow"])
            P.op("dve", lambda e: e.tensor_copy(brow_b[:], brow_f[:]), reads=["brow_f"], writes=["brow_b"])
            for k in range(4):
                c0_ = cvcol("sw", l, k * 8)
                P.op("dve", lambda e, k=k, c0_=c0_: e.tensor_tensor(
                    diagC[:, k, :, :], sub(ident, 0, [[0, 8], [1, 128]]), sub(cvt[:, c0_:c0_ + 1], 0, [[1, 8], [0, 128]]), ALU.mult),
                    reads=["cmat", "cvt"], writes=["diagC"])
            for k in range(3):
                c0_ = cvcol("cw", l, k * 2)
                P.op("dve", lambda e, k=k, c0_=c0_: e.tensor_tensor(
                    diagB[:, k, :, :], sub(ident, 0, [[0, 2], [1, 128]]), sub(cvt[:, c0_:c0_ + 1], 0, [[1, 2], [0, 128]]), ALU.mult),
                    reads=["cmat", "cvt"], writes=["diagB"])
            for v in range(2):
                src_t = ws_nat if v == 0 else ws_nat1
                rd = ["ws_nat", "cmat"] if v == 0 else ["ws_nat1z", "cmat"] + [("ws_blk", b) for b in range(16)]

                bkt_ = bk7 if v == 0 else bk6
                bkn_ = "bk7" if v == 0 else "bk6"

                def tr(e, src_t=src_t, bkt_=bkt_):
                    for h in range(4):
                        i = e.matmul(bkt_[:, h * 128:(h + 1) * 128], lhsT=src_t[:, h, :], rhs=ident, start=True, stop=True)
                    return i
                P.op("pe", tr, reads=rd, writes=[bkn_])
                P.op("dve", lambda e, v=v, bkt_=bkt_: e.tensor_tensor(
                    wmT[:, v, :, :], bkt_[:].rearrange("p (h t) -> p h t", h=4),
                    sub(cm_U[v], 0, [[0, 4], [1, 128]]), ALU.mult),
                    reads=[bkn_, "cmat"], writes=[("wmT", v)])

        def layer_prep(l):
            prep_load(l)
            prep_compute(l)

        def norm1_steps(l, sg, guard=()):
            slots, c0, n = sg["slots"], sg["c0"], sg["n"]
            xtok = [("xT", t, k) for t in slots for k in range(8)] + list(guard)
            for _ in rmsnorm_steps(xT[:, :, c0:c0 + n], lambda k: xT[:, k, c0:c0 + n], "g1", l,
                                   sq, ssq_ps, "bk2", lnv, rstd, lambda k: hT[:, k, 0:n], n, xtok, ["hT"], "nA", fine=True):
                yield

        def stream_A(gi, l, sg, skip_norm=False):
            tiles, slots, c0, n, is_s, par = sg["tiles"], sg["slots"], sg["c0"], sg["n"], sg["sample"], sg["par"]
            xtok = [("xT", t, k) for t in slots for k in range(8)]
            for _ in ([] if skip_norm else [0]):
              for _ in rmsnorm_steps(xT[:, :, c0:c0 + n], lambda k: xT[:, k, c0:c0 + n], "g1", l,
                                     sq, ssq_ps, "bk2", lnv, rstd, lambda k: hT[:, k, 0:n], n, xtok, ["hT"], "nA", fine=True):
                  yield
            yield
            wl = lambda i: wslot((gi, l, "win", i))

            def proj(load, half, pa):
                s_ = wl(load)

                def mm(e):
                    for k in range(8):
                        i = e.matmul(pa[:, 0:n], lhsT=wview(s_, k, half * 128, 128), rhs=hT[:, k, 0:n],
                                     start=(k == 0), stop=(k == 7))
                    return i
                return mm, rtok(s_) + ["hT"]
            pi = [0]

            def nextpa():
                pi[0] ^= 1
                return PA[pi[0]], "bk%d" % pi[0]
            for ti in range(len(tiles)):
                if ti == 0:
                    for k in range(8):
                        P.op("pe", lambda e, k=k: e.matmul(pdt_ps, lhsT=hT[:, k, 0:128], rhs=wdt[:, l, k, :],
                                                          start=(k == 0), stop=(k == 7)),
                             reads=[("hTk", k), "wdt"], writes=["bk2"])
                else:
                    def mmd(e, ti=ti):
                        for k in range(8):
                            i = e.matmul(pdt_ps, lhsT=hT[:, k, ti * 128:(ti + 1) * 128], rhs=wdt[:, l, k, :],
                                         start=(k == 0), stop=(k == 7))
                        return i
                    P.op("pe", mmd, reads=["hT", "wdt"], writes=["bk2"])
                P.op("dve", lambda e: e.tensor_tensor(dtr, pdt_ps, rowsT[:, 0:8], ALU.add),
                     reads=["bk2", "rowsT"], writes=["dtr"])
                P.op("act", lambda e: e.activation(e1, dtr, AF.Exp), reads=["dtr"], writes=["e1"])
                P.op("act", lambda e, ti=ti: e.activation(dtt[par][:, ti, :], e1, AF.Ln, bias=1.0), reads=["e1"],
                     writes=[("dtt", par, ti)])
                P.op("dve", lambda e, ti=ti: e.tensor_tensor(att[par][:, ti, :], dtt[par][:, ti, :], Arow[:], ALU.mult),
                     reads=[("dtt", par, ti), "Arow"], writes=[("att", par, ti)])
            yield
            for j in range(2):
                pa, pt = nextpa()
                mm, rd = proj(0, j, pa)
                P.op("pe", mm, reads=rd, writes=[pt])
                P.op("act", lambda e, j=j, pa=pa: e.activation(uT[par][:, j, 0:n], pa[:, 0:n], AF.Gelu),
                     reads=[pt], writes=[("uT", par)])
            yield
            sv = wl(1)
            for ti, t in enumerate(tiles):
                def mmv(e, ti=ti):
                    for k in range(8):
                        i = e.matmul(pv_ps, lhsT=hT[:, k, ti * 128:(ti + 1) * 128], rhs=wview(sv, k, 0, 256),
                                     start=(k == 0), stop=(k == 7))
                    return i
                P.op("pe", mmv, reads=["hT"] + rtok(sv), writes=["bk2"])
                P.op("act", lambda e, ti=ti: e.activation(vb[par][:, ti, :], pv_ps, AF.Gelu),
                     reads=["bk2"], writes=[("vb", par, ti)])
                if is_s or t == LAST_TILE:
                    P.op("act", lambda e: e.activation(vf, pv_ps, AF.Gelu), reads=["bk2"], writes=["vf"])
                    dst = cvs_d[l] if is_s else cvp_d[l]
                    P.op("sp", lambda e, dst=dst: e.dma_start(out=dst, in_=vf), reads=["vf"], dma="ov", scr=True)
            yield
            cps = bk[2]
            for j in range(2):
                pa, pt = nextpa()
                mm, rd = proj(2, j, pa)
                P.op("pe", mm, reads=rd, writes=[pt])
                P.op("act", lambda e, pa=pa, j=j: e.activation(bgb[:, j, 0:n], pa[:, 0:n], AF.Copy),
                     reads=[pt], writes=[("bgb", j)])
            for j in range(2):
                pa, pt = nextpa()
                mm, rd = proj(3, j, pa)
                P.op("pe", mm, reads=rd, writes=[pt])
                P.op("act", lambda e, pa=pa: e.activation(cg[:, 0:n], pa[:, 0:n], AF.Copy), reads=[pt], writes=["cg"])
                pa, pt = nextpa()
                mm, rd = proj(4, j, pa)
                P.op("pe", mm, reads=rd, writes=[pt])
                if not is_s:
                    gx = gext[:, j, :]
                    gt = ("gext", j)
                    taps = [gx[:, k:k + n] for k in range(3)]
                    cout = cps[:, 0:n]
                    P.op("dve", lambda e, gx=gx, j=j: e.tensor_copy(gx[:, 0:2], hist_g[:, l, j, :]),
                         reads=[("hist_g", l)], writes=[gt])
                    P.op("dve", lambda e, gx=gx, pa=pa: e.tensor_tensor(gx[:, 2:2 + n], pa[:, 0:n], cg[:, 0:n], ALU.mult),
                         reads=[pt, "cg", gt], writes=[gt])
                    P.op("dve", lambda e, pa=pa, j=j: e.tensor_tensor(hist_g[:, l, j, :], pa[:, n - 2:n], cg[:, n - 2:n], ALU.mult),
                         reads=[pt, "cg", gt], writes=[("hist_g", l)])
                else:
                    gx = gext_s[:, j, :, :]
                    gt = ("gext_s", j)
                    taps = [gx[:, :, k:k + 8] for k in range(3)]
                    cout = b16(cps[:, 0:128])
                    P.op("dve", lambda e, gx=gx, j=j: e.tensor_copy(gx[:, :, 0:2], hcs[:, j, :, :]),
                         reads=["hcs"], writes=[gt])
                    P.op("dve", lambda e, gx=gx, pa=pa: e.tensor_tensor(gx[:, :, 2:10], b16(pa[:, 0:128]), b16(cg[:, 0:128]), ALU.mult),
                         reads=[pt, "cg", gt], writes=[gt])
                    P.op("dve", lambda e, pa=pa, j=j: e.tensor_tensor(
                        ocst[:, j, :, :], b16(pa[:, 0:128])[:, :, 6:8], b16(cg[:, 0:128])[:, :, 6:8], ALU.mult),
                        reads=[pt, "cg"], writes=["ocst"])

                def mmc(e, j=j, taps=taps, cout=cout):
                    for k in range(3):
                        i = e.matmul(cout, lhsT=diagB[:, k, j, :], rhs=taps[k], start=(k == 0), stop=(k == 2))
                    return i
                P.op("pe", mmc, reads=[gt, "diagB"], writes=["bk2"])
                P.op("dve", lambda e, j=j: e.tensor_tensor(mixT[par][:, 2 + j, 0:n], cps[:, 0:n], bgb[:, j, 0:n], ALU.mult),
                     reads=["bk2", ("bgb", j)], writes=[("mixT", par, 2 + j)])
                yield
            if is_s:
                P.op("sp", lambda e: e.dma_start(out=ocs_d[l].rearrange("(j p) b k -> p j b k", p=128), in_=ocst),
                     reads=["ocst"], dma="oc", scr=True)
            sz = [wl(5), wl(6)]
            for ti in range(len(tiles)):
                def mmz(e, ti=ti):
                    for hf in range(2):
                        for k in range(8):
                            i = e.matmul(pz_ps[:, hf * 256:(hf + 1) * 256], lhsT=hT[:, k, ti * 128:(ti + 1) * 128],
                                         rhs=wview(sz[hf], k, 0, 256), start=(k == 0), stop=(k == 7))
                    return i
                P.op("pe", mmz, reads=["hT"] + rtok(sz[0]) + rtok(sz[1]), writes=["bk2"])
                P.op("act", lambda e, ti=ti: e.activation(zs[par][:, ti, :], pz_ps[:], AF.Silu),
                     reads=["bk2"], writes=[("zs", par, ti)])
            yield
            def xbc_front(c):
                pa, pt = nextpa()
                mm, rd = proj(7 + c // 2, c % 2, pa)
                P.op("pe", mm, reads=rd, writes=[pt])
                r = c % 2
                if not is_s:
                    xx = xe[:, r, :]
                    xt_ = ("xe", r)
                    taps = [xx[:, k:k + n] for k in range(4)]
                    cout = cps[:, 0:n]
                    P.op("dve", lambda e, xx=xx, c=c: e.tensor_copy(xx[:, 0:3], hist_x[:, l, c, :]),
                         reads=[("hist_x", l)], writes=[xt_])
                    P.op("act", lambda e, xx=xx, pa=pa: e.activation(xx[:, 3:3 + n], pa[:, 0:n], AF.Copy),
                         reads=[pt, xt_], writes=[xt_])
                    P.op("act", lambda e, pa=pa, c=c: e.activation(hist_x[:, l, c, :], pa[:, n - 3:n], AF.Copy),
                         reads=[pt, xt_], writes=[("hist_x", l)])
                else:
                    xx = xe_s[:, r, :, :]
                    xt_ = ("xe_s", r)
                    taps = [xx[:, :, k:k + 8] for k in range(4)]
                    cout = b16(cps[:, 0:128])
                    P.op("dve", lambda e, xx=xx, c=c: e.tensor_copy(xx[:, :, 0:3], hxs[:, c, :, :]),
                         reads=["hxs"], writes=[xt_])
                    P.op("act", lambda e, xx=xx, pa=pa: e.activation(xx[:, :, 3:11], b16(pa[:, 0:128]), AF.Copy),
                         reads=[pt, xt_], writes=[xt_])
                    P.op("act", lambda e, pa=pa, c=c: e.activation(oxst[:, c, :, :], b16(pa[:, 0:128])[:, :, 5:8], AF.Copy),
                         reads=[pt], writes=["oxst"])
                return xt_, taps, cout

            def xbc_back(c, xt_, taps, cout):
                bcol = cvt[:, cvcol("sb", l, c):cvcol("sb", l, c) + 1]

                def mmc(e):
                    for k in range(4):
                        i = e.matmul(cout, lhsT=diagC[:, k, c, :], rhs=taps[k], start=(k == 0), stop=(k == 3))
                    return i
                P.op("pe", mmc, reads=[xt_, "diagC"], writes=["bk2"])
                P.op("act", lambda e: e.activation(xact[par][:, c, 0:n], cps[:, 0:n], AF.Silu, bias=bcol),
                     reads=["bk2", "cvt"], writes=[("xact", par, c)])
            pend = xbc_front(0)
            for c in range(8):
                nxt_ = xbc_front(c + 1) if c + 1 < 8 else None
                xbc_back(c, *pend)
                pend = nxt_
                yield
            if is_s:
                P.op("sp", lambda e: e.dma_start(out=oxs_d[l].rearrange("(c p) b k -> p c b k", p=128), in_=oxst),
                     reads=["oxst"], dma="ox", scr=True)

        def stream_B(gi, l, sg):
            tiles, slots, c0, n, is_s, par = sg["tiles"], sg["slots"], sg["c0"], sg["n"], sg["sample"], sg["par"]
            v = 1 if is_s else 0
            U, L = cm_U[v], cm_L[v]
            NB = 16 if is_s else 1
            bcol0 = 1 if is_s else 0
            XA = xact[par]
            h8 = lambda a: a.rearrange("p (h q) -> p h q", h=8)
            for ti, t in enumerate(tiles):
                a_ = att[par][:, ti, :]
                de, ec, dechp, decT = de2[ti], ec2[ti], dechp2[ti], decT2[ti]
                P.op("dve", lambda e, a_=a_: e.tensor_tensor(
                    rseg.rearrange("p (h t) -> p h t", h=8), sub(U, 0, [[0, 8], [1, 128]]),
                    sub(a_, 0, [[1, 8], [0, 128]]), ALU.mult),
                    reads=["cmat", ("att", par, ti)], writes=["rseg"])
                if AEXP_ACT:
                    P.op("act", lambda e, a_=a_: e.activation(h8(aexp), sub(a_, 0, [[1, 8], [0, 64]]), AF.Copy),
                         reads=[("att", par, ti)], writes=["aexp"])
                else:
                    P.op("dve", lambda e, a_=a_: e.tensor_copy(h8(aexp), sub(a_, 0, [[1, 8], [0, 64]])),
                         reads=[("att", par, ti)], writes=["aexp"])
                yield

                def mmS(e, a_=a_):
                    e.matmul(de_ps, lhsT=L, rhs=a_, start=True, stop=True)
                    e.matmul(ec_ps, lhsT=U, rhs=a_, start=True, stop=True)
                    for j in range(4):
                        i = e.matmul(cl_ps[:, j * 16:j * 16 + NB], lhsT=aexp[:, j * 128:(j + 1) * 128],
                                     rhs=bind[:, bcol0:bcol0 + NB], start=True, stop=True)
                    return i
                P.op("pe", mmS, reads=["cmat", ("att", par, ti), "aexp", "bind"], writes=["bk3"])
                P.op("act", lambda e, de=de: e.activation(de, de_ps, AF.Exp), reads=["bk3"], writes=[("de", ti)])
                P.op("act", lambda e, ec=ec: e.activation(ec, ec_ps, AF.Exp), reads=["bk3"], writes=[("ec", ti)])
                for j in range(4):
                    P.op("act", lambda e, j=j, dechp=dechp: e.activation(dechp[:, j, 0:NB], cl_ps[:, j * 16:j * 16 + NB], AF.Exp),
                         reads=["bk3"], writes=[("dechp", ti)])

                def mmSeg(e):
                    e.matmul(bk45[:, 0:512], lhsT=L, rhs=rseg[:, 0:512], start=True, stop=True)
                    return e.matmul(bk45[:, 512:1024], lhsT=L, rhs=rseg[:, 512:1024], start=True, stop=True)
                P.op("pe", mmSeg, reads=["cmat", "rseg"], writes=B45)
                P.op("act", lambda e, decT=decT: e.activation(decT, bk45[:], AF.Exp), reads=B45, writes=[("decT", ti)])
                yield
            def tile_gen(ti, t):
                tc0 = ti * 128
                de, ec, dechp, decT = de2[ti], ec2[ti], dechp2[ti], decT2[ti]
                def mmA(e, ti=ti):
                    for j in range(2):
                        e.matmul(sTA_ps[:, j * 128:(j + 1) * 128], lhsT=sel_b[:],
                                 rhs=brow_b[:, v * 256 + j * 128: v * 256 + (j + 1) * 128], start=True, stop=False)
                        for hh in range(2):
                            h = 2 * j + hh
                            i = e.matmul(sTA_ps[64 * hh:64 * hh + 64, j * 128:(j + 1) * 128],
                                         lhsT=vb[par][:, ti, h * 64:(h + 1) * 64], rhs=wmT[:, v, h, :],
                                         start=False, stop=True, tile_position=(0, 64 * hh))
                    return i
                P.op("pe", mmA, reads=["sel_b", "brow_b", ("vb", par, ti), ("wmT", v)], writes=["bk3"])
                P.op("dve", lambda e, tc0=tc0: e.tensor_tensor(
                    mixT[par][:, 0:2, tc0:tc0 + 128], sTA_ps.rearrange("p (j t) -> p j t", j=2),
                    uT[par][:, :, tc0:tc0 + 128], ALU.mult),
                    reads=["bk3", ("uT", par)], writes=[("mixT", par, 0), ("mixT", par, 1)])
                yield
                dt_ = dtt[par][:, ti, :]
                a_ = att[par][:, ti, :]
                def mmT(e, tc0=tc0):
                    for c in range(4):
                        e.matmul(bk45[:, c * 128:(c + 1) * 128], lhsT=XA[:, c, tc0:tc0 + 128], rhs=identb[:],
                                 start=True, stop=True)
                    for c in range(2):
                        i = e.matmul(bk[3][:, c * 128:(c + 1) * 128], lhsT=XA[:, 4 + c, tc0:tc0 + 128], rhs=identb[:],
                                     start=True, stop=True)
                    return i
                P.op("pe", mmT, reads=[("xact", par, c) for c in range(6)] + ["identb"], writes=["bk4", "bk3"])
                tp3 = h8(bk45[:, 0:512])
                P.op("dve", lambda e, dt_=dt_: e.tensor_tensor(h8(xdt), tp3, sub(dt_, 0, [[1, 8], [0, 64]]), ALU.mult),
                     reads=["bk4", ("dtt", par, ti)], writes=["xdt"])
                P.op("dve", lambda e: e.tensor_tensor(h8(xsD), tp3, sub(rowsT[:, 16:24], 0, [[1, 8], [0, 64]]), ALU.mult),
                     reads=["bk4", "rowsT"], writes=["xsD"])
                P.op("act", lambda e: e.activation(Btok, bk[3][:, 0:256], AF.Copy), reads=["bk3"], writes=["Btok"])
                P.op("dve", lambda e, de=de: e.tensor_tensor(h8(xdtd), h8(xdt), sub(de, 0, [[1, 8], [0, 64]]), ALU.mult),
                     reads=["xdt", ("de", ti)], writes=["xdtd"])
                yield
                def mmCB(e, tc0=tc0):
                    for g in range(2):
                        i = e.matmul(bk7[:, g * 128:(g + 1) * 128], lhsT=XA[:, 4 + g, tc0:tc0 + 128],
                                     rhs=XA[:, 6 + g, tc0:tc0 + 128], start=True, stop=True)
                    return i
                P.op("pe", mmCB, reads=[("xact", par, c) for c in range(4, 8)], writes=["bk7"])
                P.op("dve", lambda e: e.tensor_tensor(
                    cbTm.rearrange("p (g t) -> p g t", g=2), bk7[:, 0:256].rearrange("p (g t) -> p g t", g=2),
                    sub(U, 0, [[0, 2], [1, 128]]), ALU.mult), reads=["bk7", "cmat"], writes=["cbTm"])
                d4 = decT.rearrange("p (g k t) -> p g k t", g=2, k=4)
                P.op("dve", lambda e, d4=d4: e.tensor_tensor(d4, d4, sub(cbTm, 0, [[128, 2], [0, 4], [1, 128]]), ALU.mult),
                     reads=[("decT", ti), "cbTm"], writes=[("decT", ti)])
                yield

                def mmYD(e, decT=decT):
                    e.matmul(bk6[:, 0:512], lhsT=identb[:], rhs=xsD, start=True, stop=False)
                    for h in range(8):
                        i = e.matmul(bk6[:, h * 64:(h + 1) * 64], lhsT=decT[:, h * 128:(h + 1) * 128],
                                     rhs=xdt[:, h * 64:(h + 1) * 64], start=False, stop=(h == 7))
                    return i
                P.op("pe", mmYD, reads=[("decT", ti), "xdt", "xsD", "identb"], writes=["bk6"])
                if not is_s:
                    def mmST(e):
                        for j in range(4):
                            i = e.matmul(bk7[:, j * 128:(j + 1) * 128], lhsT=S_all[:, l, j, :], rhs=ident,
                                         start=True, stop=True)
                        return i
                    P.op("pe", mmST, reads=[("S", l), "cmat"], writes=["bk7"])
                    P.op("act", lambda e: e.activation(STb[:, 0, :], bk7[:], AF.Copy), reads=["bk7"], writes=[("STb", 0)])
                    yield

                    def mmYO(e, tc0=tc0):
                        for g in range(2):
                            i = e.matmul(bk7[:, g * 256:(g + 1) * 256], lhsT=XA[:, 6 + g, tc0:tc0 + 128],
                                         rhs=STb[:, 0, g * 256:(g + 1) * 256], start=True, stop=True)
                        return i
                    P.op("pe", mmYO, reads=[("xact", par, 6), ("xact", par, 7), ("STb", 0)], writes=["bk7"])
                    yield

                    def mmSt(e):
                        for j in range(4):
                            g = j // 2
                            i = e.matmul(bk45[:, j * 128:(j + 1) * 128], lhsT=xdtd[:, j * 128:(j + 1) * 128],
                                         rhs=Btok[:, g * 128:(g + 1) * 128], start=True, stop=True)
                        return i
                    P.op("pe", mmSt, reads=["xdtd", "Btok"], writes=["bk4"])
                    for j in range(4):
                        P.op("dve", lambda e, j=j, dechp=dechp: e.scalar_tensor_tensor(
                            S_all[:, l, j, :], S_all[:, l, j, :], dechp[:, j, 0:1], bk45[:, j * 128:(j + 1) * 128],
                            ALU.mult, ALU.add), reads=[("S", l), ("dechp", ti), "bk4"], writes=[("S", l)])
                    if t == LAST_TILE:
                        P.op("sp", lambda e: e.dma_start(out=osp_d[l].rearrange("(j p) n -> p j n", p=128),
                                                         in_=S_all[:, l, :, :]), reads=[("S", l)], dma="os")
                else:
                    P.op("dve", lambda e: e.tensor_copy(
                        sub(Cmask[:], 0, [[2048, 2], [136, 16], [1, 8]]),
                        sub(XA[:, 6, 0:1], 0, [[NSG, 2], [8, 16], [1, 8]])),
                        reads=[("xact", par, 6), ("xact", par, 7)], writes=["Cmask"])
                    def load_state(b):
                        P.op("sp", lambda e, b=b: e.dma_start(
                            out=Sin[:, :, b % 3, :], in_=ssm_d[l, b].rearrange("(j p) n -> p j n", p=128)),
                            writes=[("Sin", b % 3)], dma=("si", b % 3), scr=True)
                    load_state(0)
                    load_state(1)
                    for b in range(16):
                        r = b % 2
                        bkr, bkt = bkST[r]
                        r3 = b % 3
                        if b + 2 < 16:
                            load_state(b + 2)

                        def mmST(e, r3=r3, bkr=bkr):
                            for j in range(4):
                                i = e.matmul(bkr[:, j * 128:(j + 1) * 128], lhsT=Sin[:, j, r3, :], rhs=ident,
                                             start=True, stop=True)
                            return i
                        P.op("pe", mmST, reads=[("Sin", r3), "cmat"], writes=bkt)
                        P.op("act", lambda e, r=r, bkr=bkr: e.activation(STb[:, r, :], bkr[:], AF.Copy),
                             reads=bkt, writes=[("STb", r)])

                        def mmYO(e, b=b, r=r):
                            for g in range(2):
                                i = e.matmul(bk7[:, g * 256:(g + 1) * 256], lhsT=Cmask[:, g, b, :],
                                             rhs=STb[:, r, g * 256:(g + 1) * 256], start=(b == 0 and g == 0), stop=(b == 15),
                                             skip_group_check=True)
                            return i
                        P.op("pe", mmYO, reads=["Cmask", ("STb", r)], writes=["bk7"])
                        if BMASK_ACT:
                            P.op("act", lambda e, b=b, r=r: e.activation(BmaskQ[:, r, :], Btok, AF.Copy, scale=bind[:, 1 + b:2 + b]),
                                 reads=["Btok", "bind"], writes=[("BmaskQ", r)])
                        else:
                            P.op("dve", lambda e, b=b, r=r: e.tensor_scalar(BmaskQ[:, r, :], Btok, bind[:, 1 + b:2 + b], None, ALU.mult),
                                 reads=["Btok", "bind"], writes=[("BmaskQ", r)])
                        st_ps = bk45[:, r * 512:(r + 1) * 512]

                        def mmSt(e, r=r, st_ps=st_ps):
                            for j in range(4):
                                g = j // 2
                                i = e.matmul(st_ps[:, j * 128:(j + 1) * 128], lhsT=xdtd[:, j * 128:(j + 1) * 128],
                                             rhs=BmaskQ[:, r, g * 128:(g + 1) * 128], start=True, stop=True)
                            return i
                        P.op("pe", mmSt, reads=["xdtd", ("BmaskQ", r)], writes=[B45[r]])
                        for j in range(4):
                            P.op("dve", lambda e, j=j, b=b, r3=r3, st_ps=st_ps, dechp=dechp: e.scalar_tensor_tensor(
                                Sin[:, j, r3, :], Sin[:, j, r3, :], dechp[:, j, b:b + 1], st_ps[:, j * 128:(j + 1) * 128],
                                ALU.mult, ALU.add), reads=[("Sin", r3), ("dechp", ti), B45[r]], writes=[("Sin", r3)])
                        P.op("sp", lambda e, b=b, r3=r3: e.dma_start(
                            out=oss_d[l, b].rearrange("(j p) n -> p j n", p=128), in_=Sin[:, :, r3, :]),
                            reads=[("Sin", r3)], dma=("so", r3), scr=True)
                        yield
                yield "SPLIT"
                P.op("dve", lambda e, ec=ec: e.tensor_tensor(h8(y1), h8(bk7[:]), sub(ec, 0, [[1, 8], [0, 64]]), ALU.mult),
                     reads=["bk7", ("ec", ti)], writes=["y1"])
                P.op("dve", lambda e: e.tensor_tensor(y1, y1, bk6[:], ALU.add), reads=["y1", "bk6"], writes=["y1"])
                P.op("dve", lambda e, ti=ti: e.tensor_tensor(yg, y1, zs[par][:, ti, :], ALU.mult),
                     reads=["y1", ("zs", par, ti)], writes=["yg"])
                P.op("dve", lambda e: e.memset(ss[:, 0:1], 0.0), writes=["ss"])
                P.op("act", lambda e: e.activation(junk, yg, AF.Square, accum_out=ss[:, 0:1]),
                     reads=["yg", "ss"], writes=["junk", "ss"])
                P.op("act", lambda e: e.activation(lns[:, 0:1], ss[:, 0:1], AF.Ln, bias=EPS, scale=1.0 / 512),
                     reads=["ss"], writes=["lns"])
                P.op("act", lambda e: e.activation(rs[:, 0:1], lns[:, 0:1], AF.Exp, scale=-0.5),
                     reads=["lns"], writes=["rs"])
                yield
                P.op("dve", lambda e: e.scalar_tensor_tensor(yc, yg, rs[:, 0:1], rowsT[:, 24:536], ALU.mult, ALU.mult),
                     reads=["yg", "rs", "rowsT"], writes=["yc"])
                yield

                def mmYT(e):
                    for j in range(4):
                        i = e.matmul(bk45[:, 512 + j * 128:512 + (j + 1) * 128], lhsT=yc[:, j * 128:(j + 1) * 128], rhs=identb[:],
                                     start=True, stop=True)
                    return i
                P.op("pe", mmYT, reads=["yc", "identb"], writes=["bk5"])
                P.op("act", lambda e, tc0=tc0: e.activation(
                    mixT[par][:, 4:8, tc0:tc0 + 128], bk45[:, 512:1024].rearrange("p (j t) -> p j t", j=4), AF.Copy),
                    reads=["bk5"], writes=[("mixT", par, 4 + j) for j in range(4)])
                yield
            prev_tail = None
            pc = sg.get("prev_carry")
            for ti, t in enumerate(tiles):
                g = tile_gen(ti, t)
                while True:
                    r_ = next(g)
                    if r_ == "SPLIT":
                        while pc is not None and not pc.done:
                            if pc.step():
                                yield
                        break
                    yield
                    if prev_tail is not None:
                        try:
                            P.scr_tok = "SCRT"
                            next(prev_tail)
                            P.scr_tok = "SCR"
                            yield
                        except StopIteration:
                            P.scr_tok = "SCR"
                            prev_tail = None
                while prev_tail is not None:
                    try:
                        P.scr_tok = "SCRT"
                        next(prev_tail)
                        P.scr_tok = "SCR"
                        yield
                    except StopIteration:
                        P.scr_tok = "SCR"
                        prev_tail = None
                prev_tail = g
            sg["carry"] = prev_tail

        def stream_C(gi, l, sg):
            tiles, slots, c0, n, is_s, par = sg["tiles"], sg["slots"], sg["c0"], sg["n"], sg["sample"], sg["par"]
            g_ = sg["carry"]
            while True:
                try:
                    P.scr_tok = "SCRT"
                    next(g_)
                    P.scr_tok = "SCR"
                    yield
                except StopIteration:
                    P.scr_tok = "SCR"
                    break
            xtok = lambda oc: [("xT", t, oc) for t in slots]
            for oc in range(8):
                pa, pt = PA[oc % 2], "bk%d" % (oc % 2)
                s_ = wslot((gi, l, "wout", oc // 2))

                def mm(e, oc=oc, s_=s_, pa=pa):
                    for k in range(8):
                        i = e.matmul(pa[:, 0:n], lhsT=wview(s_, k, (oc % 2) * 128, 128), rhs=mixT[par][:, k, 0:n],
                                     start=(k == 0), stop=(k == 7))
                    return i
                P.scr_tok = "SCRT"
                P.op("pe", mm, reads=rtok(s_) + [("mixT", par, k) for k in range(8)], writes=[pt])
                P.op("dve", lambda e, oc=oc, pa=pa: e.tensor_tensor(
                    xT[:, oc, c0:c0 + n], xT[:, oc, c0:c0 + n], pa[:, 0:n], ALU.add),
                    reads=[pt] + xtok(oc), writes=xtok(oc))
                P.scr_tok = "SCR"
                yield

        def ffn_norm_steps(l, fs):
            c0, n, slots = fs["c0"], fs["n"], fs["slots"]
            xtok = [("xT", t, k) for t in slots for k in range(8)]
            for _ in rmsnorm_steps(xT[:, :, c0:c0 + n], lambda k: xT[:, k, c0:c0 + n], "g2", l,
                                   sq2, bk45[:, 0:512], "bk4", lnv2, rstd2, lambda k: h2[:, k, c0:c0 + n], n,
                                   xtok, [("h2", c0)], "nF", True):
                yield

        def ffn_make(gi, l, fsgs):
            its = [(blk, fs) for blk in range(4) for fs in fsgs]
            slots_of = {}

            def ffn_up(i, bo=0):
                blk, fs = its[i]
                if blk not in slots_of:
                    slots_of[blk] = [wslot((gi, l, "ffn", blk * 8 + hc)) for hc in range(8)]
                sl = slots_of[blk]
                c0, n = fs["c0"], fs["n"]
                hb_ = i % 2
                for hc in range(8):
                    s_ = sl[hc]
                    hp = bk[bo + hc % 2]
                    hpt = BKT[bo + hc % 2]

                    def mm1(e, s_=s_, hp=hp):
                        for k in range(8):
                            i_ = e.matmul(hp[:, 0:n], lhsT=sub(ring[:], s_ * 2048 + k * 128, [[1, 128]]),
                                          rhs=h2[:, k, c0:c0 + n], start=(k == 0), stop=(k == 7))
                        return i_
                    P.op("pe", mm1, reads=[("ring", s_), ("h2", c0)], writes=hpt)
                    P.op("act", lambda e, hp=hp, hc=hc: e.activation(rr[:, hc % 2, 0:n], hp[:, 0:n], AF.Relu),
                         reads=hpt, writes=[("rr", hc % 2)])
                    if SQ_ON_ACT:
                        P.op("act", lambda e, hc=hc: e.activation(hidb[hb_][:, hc, 0:n], rr[:, hc % 2, 0:n], AF.Square),
                             reads=[("rr", hc % 2)], writes=[("hid", hb_, hc)])
                    else:
                        P.op("dve", lambda e, hc=hc: e.tensor_tensor(
                            hidb[hb_][:, hc, 0:n], rr[:, hc % 2, 0:n], rr[:, hc % 2, 0:n], ALU.mult),
                            reads=[("rr", hc % 2)], writes=[("hid", hb_, hc)])
                    yield

            def ffn_down(i):
                blk, fs = its[i]
                sl = slots_of[blk]
                c0, n, slots = fs["c0"], fs["n"], fs["slots"]
                hb_ = i % 2
                for oc in range(8):
                    op_ = bk[2 + oc % 2]

                    def mm2(e, oc=oc, op_=op_):
                        for hc in range(8):
                            i_ = e.matmul(op_[:, 0:n], lhsT=sub(ring[:], sl[hc] * 2048 + 1024 + oc * 128, [[1, 128]]),
                                          rhs=hidb[hb_][:, hc, 0:n], start=(hc == 0), stop=(hc == 7))
                        return i_
                    P.op("pe", mm2, reads=[("ring2", s_) for s_ in sl] + [("hid", hb_, hc) for hc in range(8)],
                         writes=BKT[2 + oc % 2])
                    xt = [("xT", t, oc) for t in slots]
                    P.op("dve", lambda e, oc=oc, op_=op_: e.tensor_tensor(
                        xT[:, oc, c0:c0 + n], xT[:, oc, c0:c0 + n], op_[:, 0:n], ALU.add),
                        reads=BKT[2 + oc % 2] + xt, writes=xt)
                if fs is fsgs[-1]:
                    for hc in range(8):
                        release((gi, l, "ffn", blk * 8 + hc))
            return dict(its=its, up=ffn_up, down=ffn_down)

        def ffn_phase(gi, l, fsgs, ctx, mid_hook=None, skip=(), upped0=False, tail_hook=None):
            for fs in fsgs:
                if fs in skip:
                    continue
                for _ in ffn_norm_steps(l, fs):
                    pass
            its = ctx["its"]
            if not upped0:
                for _ in ctx["up"](0):
                    pass
            for i in range(len(its)):
                if i + 1 < len(its):
                    for _ in ctx["up"](i + 1):
                        pass
                if tail_hook is not None and i == len(its) - 1:
                    tail_hook()
                ctx["down"](i)
                if mid_hook is not None and i == len(its) // 2:
                    mid_hook()

        def fence():
            o_ = P.op("dve", lambda e: e.memset(dummy[:], 0.0), reads=[], writes=["SCR", "SCRT"], scr=False)
            FENCE_T.append((o_.finish / 1e3, P.busy.get("pe", 0.0) / 1e3))

        def fenceA():
            P.op("dve", lambda e: e.memset(dummy[:], 0.0), reads=[], writes=["SCR"], scr=False)

        pump()
        for gi, (ptiles, has_s) in enumerate(GROUPS):
            npc = 128 * len(ptiles)
            ncol = npc + (128 if has_s else 0)
            nsl = ncol // 128
            tids = list(ptiles) + ([16] if has_s else [])
            P.op("sp", lambda e, ptiles=ptiles, npc=npc: e.dma_start(
                out=xT[:, :, 0:npc],
                in_=xp_d[:, ptiles[0] * 128:ptiles[0] * 128 + npc].rearrange("(k p) n -> p k n", p=128)),
                writes=[("xT", t, k) for t in range(len(ptiles)) for k in range(8)], dma="xi0")
            if has_s:
                P.op("sp", lambda e, npc=npc: e.dma_start(
                    out=xT[:, :, npc:npc + 128], in_=xs_d.rearrange("(k p) n -> p k n", p=128)),
                    writes=[("xT", nsl - 1, k) for k in range(8)], dma="xi1")
            sgs = []
            for i in range(0, len(ptiles), 2):
                tl = ptiles[i:i + 2]
                sgs.append(dict(tiles=tl, slots=list(range(i, i + len(tl))), c0=i * 128, n=128 * len(tl), sample=False))
            if has_s:
                sgs.append(dict(tiles=[16], slots=[nsl - 1], c0=npc, n=128, sample=True))
            fsgs = []
            if FFN_ALIGN and ncol - sgs[-1]["n"] <= NFF and sgs[-1]["n"] >= FFN_ALIGN:
                cuts = [0, ncol - sgs[-1]["n"], ncol]
            else:
                nf = -(-ncol // NFF)
                wf = -(-ncol // nf)
                cuts = list(range(0, ncol, wf)) + [ncol]
            for c, c1 in zip(cuts[:-1], cuts[1:]):
                n = c1 - c
                fsgs.append(dict(c0=c, n=n, slots=list(range(c // 128, (c + n - 1) // 128 + 1))))
            pre_normed = False
            for l in range(NL):
                if l == 0 and gi == 0:
                    layer_prep(0)
                fence()
                if has_s:
                    P.op("sp", lambda e, l=l: e.dma_start(out=hcs, in_=hc_d[l].rearrange("(j p) b k -> p j b k", p=128)),
                         writes=["hcs"], dma="h0", scr=True)
                    P.op("sp", lambda e, l=l: e.dma_start(out=hxs, in_=hx_d[l].rearrange("(c p) b k -> p c b k", p=128)),
                         writes=["hxs"], dma="h1", scr=True)
                for si, sg in enumerate(sgs):
                    sg["par"] = si % 2
                def run_streams(streams, gated=None, gate=None):
                    live = [s_ for s_ in streams if s_ is not None and not s_.done]
                    while live or (gated is not None and not gated.done):
                        if gated is not None and (gate is None or gate.done) and gated not in live and not gated.done:
                            live.append(gated)
                        if not live:
                            break
                        live.sort(key=lambda x: x.t - x.prio)
                        st_ = live[0]
                        P.step_finish = 0.0
                        if st_.step():
                            st_.t = max(st_.t, P.step_finish)
                        live = [s_ for s_ in live if not s_.done]
                def rel_win():
                    for i in range(11):
                        release((gi, l, "win", i))
                run_streams([Strm(stream_A(gi, l, sgs[0], skip_norm=pre_normed))])
                pre_normed = False
                if len(sgs) == 1 and EARLY_REL:
                    rel_win()
                carry = None
                for si, sg in enumerate(sgs):
                    nxt = Strm(stream_A(gi, l, sgs[si + 1])) if si + 1 < len(sgs) else None
                    sg["prev_carry"] = carry
                    if carry is not None:
                        P.step_finish = 0.0
                        carry.step()
                        carry.t = P.step_finish
                    run_streams([carry, Strm(stream_B(gi, l, sg), PRIO_B)], gated=nxt, gate=carry)
                    carry = Strm(stream_C(gi, l, sg), PRIO_C)
                    if si == len(sgs) - 2 and EARLY_REL:
                        rel_win()
                fenceA()
                last_slots = set(sgs[-1]["slots"])
                pre = [fs for fs in fsgs if not (set(fs["slots"]) & last_slots)]
                fctx = ffn_make(gi, l, fsgs)
                upped0 = PRE_UP and bool(pre) and (fsgs[0] in pre)

                def pre_ffn(pre=pre, fctx=fctx, upped0=upped0):
                    for fs in pre:
                        for _ in ffn_norm_steps(l, fs):
                            yield
                    if upped0:
                        for _ in fctx["up"](0, PRE_BANK):
                            yield
                run_streams([carry, Strm(pre_ffn())])
                if not EARLY_REL:
                    rel_win()
                for i in range(4):
                    release((gi, l, "wout", i))
                if DEBUG_STOP:
                    break
                fence()
                nl_ = (l + 1) if l + 1 < NL else (0 if gi + 1 < len(GROUPS) else None)
                th_ = None
                if PRE_A and l + 1 < NL and set(sgs[0]["slots"]) <= set(fsgs[0]["slots"]) and len(fsgs) > 1:
                    def th_(l=l):
                        P.op("dve", lambda e: e.memset(dummy[:], 0.0), reads=[],
                             writes=[("h2", fs["c0"]) for fs in fsgs] + ["hTguard"])
                        for _ in norm1_steps(l + 1, sgs[0], guard=["hTguard"]):
                            pass
                    pre_normed = True
                if nl_ is not None:
                    prep_load(nl_)
                    ffn_phase(gi, l, fsgs, fctx, mid_hook=lambda nl_=nl_: prep_compute(nl_), skip=pre, upped0=upped0, tail_hook=th_)
                else:
                    ffn_phase(gi, l, fsgs, fctx, skip=pre, upped0=upped0, tail_hook=th_)
            blocks = [(c, min(256, npc - c), False) for c in range(0, npc, 256)] + ([(npc, 128, True)] if has_s else [])
            for (c, n, smp) in ([] if DEBUG_STOP else blocks):
                slots = list(range(c // 128, (c + n) // 128))
                xtok = [("xT", t, k) for t in slots for k in range(8)]
                rmsnorm_cols(xT[:, :, c:c + n], lambda k, c=c, n=n: xT[:, k, c:c + n], "gf", 0,
                             sq2, bk45[:, 0:512], "bk4", lnv2, rstd2, lambda k, n=n: yout[:, k, 0:n], n, xtok, ["yout"], "nO")
                if smp:
                    dst = ys_d.rearrange("(k p) n -> p k n", p=128)
                else:
                    t0 = ptiles[0] + c // 128
                    dst = yp_d[:, t0 * 128:t0 * 128 + n].rearrange("(k p) n -> p k n", p=128)
                P.op("sp", lambda e, dst=dst, n=n: e.dma_start(out=dst, in_=yout[:, :, 0:n]), reads=["yout"], dma="yo", scr=True)
        for l in range(NL):
            P.op("sp", lambda e, l=l: e.dma_start(out=ocp_d[l].rearrange("(j p) k -> p j k", p=128), in_=hist_g[:, l, :, :]),
                 reads=[("hist_g", l)], dma="op0")
            P.op("sp", lambda e, l=l: e.dma_start(out=oxp_d[l].rearrange("(c p) k -> p c k", p=128), in_=hist_x[:, l, :, :]),
                 reads=[("hist_x", l)], dma="op1")
        print("ops:", P.nops, "scratch words mixer/ffn:", mixer_words, ffn_words, "model_us: %.0f" % (max(P.eng_free.values()) / 1e3), "busy_us:", {k: int(v / 1e3) for k, v in P.busy.items()}, flush=True)
        P.emit()
    return nc


_NC = None


def kernel(x_prompt, x_sample, state_conv, state_ssm_conv, state_ssm, norm1, w_in, w_s, b_s, conv_w,
           ssm_conv_w, ssm_conv_b, dt_bias, a_log, d_skip, ssm_norm, w_out, norm2, w_ff1, w_ff2, final_norm):
    global _NC
    f = lambda a: np.ascontiguousarray(np.asarray(a, dtype=np.float32))
    x_prompt, x_sample = f(x_prompt), f(x_sample)
    state_conv, state_ssm_conv, state_ssm = f(state_conv), f(state_ssm_conv), f(state_ssm)
    cv = np.zeros((128, NCV), np.float32)

    def put(nm, l, arr):
        base, n = CVL[nm]
        cv[:, base + l * n: base + l * n + n] = arr.reshape(n, 128).T
    for l in range(DEPTH):
        put("g1", l, f(norm1)[l])
        put("g2", l, f(norm2)[l])
        put("cw", l, f(conv_w)[l])
        put("sw", l, f(ssm_conv_w)[l])
        put("sb", l, f(ssm_conv_b)[l])
    base, n = CVL["gf"]
    cv[:, base:base + 8] = f(final_norm).reshape(8, 128).T
    rows = np.concatenate([f(dt_bias), f(a_log), f(d_skip), f(ssm_norm)], axis=1)
    bs = f(b_s)
    brow = np.zeros((DEPTH, 2, 512), np.float32)
    for hh in range(2):
        for j in range(2):
            brow[:, hh, j * 128:(j + 1) * 128] = bs[:, 2 * j + hh, :]
            brow[:, hh, 256 + j * 128:256 + (j + 1) * 128] = np.tile(bs[:, 2 * j + hh, 0:8], (1, 16))
    idx = np.arange(128)
    U_p = (idx[:, None] <= idx[None, :]).astype(np.float32)
    L_p = (idx[:, None] > idx[None, :]).astype(np.float32)
    same = (idx[:, None] // 8 == idx[None, :] // 8).astype(np.float32)
    cmat = np.stack([U_p, L_p, U_p * same, L_p * same, np.eye(128, dtype=np.float32)], axis=1)
    bind = np.zeros((128, 17), np.float32)
    bind[:, 0] = 1.0
    bind[idx, 1 + idx // 8] = 1.0
    sel2 = np.zeros((2, 128), np.float32)
    sel2[0, 0:64] = 1.0
    sel2[1, 64:128] = 1.0
    shared = dict(w_in=f(w_in), w_out=f(w_out), w_ff1=f(w_ff1), w_ff2=f(w_ff2), cv=cv, rows=np.ascontiguousarray(rows),
                  w_s=f(w_s), brow=brow, cmat=np.ascontiguousarray(cmat), bind=bind, sel2=sel2)
    in_maps = []
    for c in range(NCORES):
        sl = slice(16 * c, 16 * c + 16)
        m = dict(shared)
        m["xp"] = np.ascontiguousarray(x_prompt[c].T)
        m["xs"] = np.ascontiguousarray(x_sample[sl].reshape(128, 1024).T)
        m["hc"] = np.ascontiguousarray(state_conv[:, sl].transpose(0, 3, 1, 2))
        m["hx"] = np.ascontiguousarray(state_ssm_conv[:, sl].transpose(0, 3, 1, 2))
        m["ssm"] = np.ascontiguousarray(state_ssm[:, sl].reshape(DEPTH, 16, 512, 128))
        in_maps.append(m)
    if _NC is None:
        _NC = build()
    res = run_bass_kernel_spmd(_NC, in_maps, core_ids=list(range(NCORES)))
    R = res.results
    y_prompt = np.stack([R[c]["yp"].T for c in range(NCORES)])
    y_sample = np.concatenate([R[c]["ys"].T.reshape(16, 8, 1024) for c in range(NCORES)])
    chunk_v_prompt = np.stack([R[c]["cvp"] for c in range(NCORES)], axis=1)
    conv_prompt = np.stack([R[c]["ocp"].transpose(0, 2, 1) for c in range(NCORES)], axis=1)
    ssm_conv_prompt = np.stack([R[c]["oxp"].transpose(0, 2, 1) for c in range(NCORES)], axis=1)
    ssm_prompt = np.stack([R[c]["osp"].reshape(DEPTH, 8, 64, 128) for c in range(NCORES)], axis=1)
    chunk_v_sample = np.concatenate([R[c]["cvs"].reshape(DEPTH, 16, 8, 256) for c in range(NCORES)], axis=1)
    conv_sample = np.concatenate([R[c]["ocs"].transpose(0, 2, 3, 1) for c in range(NCORES)], axis=1)
    ssm_conv_sample = np.concatenate([R[c]["oxs"].transpose(0, 2, 3, 1) for c in range(NCORES)], axis=1)
    ssm_sample = np.concatenate([R[c]["oss"].reshape(DEPTH, 16, 8, 64, 128) for c in range(NCORES)], axis=1)
    outs = (y_prompt, y_sample, chunk_v_prompt, conv_prompt, ssm_conv_prompt, ssm_prompt,
            chunk_v_sample, conv_sample, ssm_conv_sample, ssm_sample)
    return tuple(np.ascontiguousarray(o, dtype=np.float32) for o in outs)
```
